# Optimizing a Trainium2 kernel written in Bass

```python
import math
import jax, jax.numpy as jnp
from jax import lax
import numpy as np

D_MODEL = 2048
BATCH = 4
SEQ = 8192
DEPTH = 1
DEC_BATCH = 8
DEC_SEQ = 2048
PAST_LEN = 128

D_ATT = D_MODEL
ATT_HEAD_DIM = 128
ATT_HEADS = D_ATT // (2 * ATT_HEAD_DIM)
Q_BLOCK = 128
D_SSM = D_MODEL
SSM_HEAD_DIM = 64
SSM_HEADS = D_SSM // SSM_HEAD_DIM
SSM_GROUPS = 4
SSM_STATE = 128
D_CONV = 5
CHUNK = 128
D_XBC = D_SSM + 2 * SSM_GROUPS * SSM_STATE
MEM_TOKENS = 256
MEM_HEADS = 4
MEM_HEAD_DIM = 128
D_MEM = MEM_HEADS * MEM_HEAD_DIM
D_MIX = D_ATT + D_SSM + D_MEM
IN_SIZES = (D_ATT, D_ATT, D_ATT, D_ATT, D_SSM, D_XBC, 2 * SSM_HEADS, D_MEM, D_MEM)
D_IN = sum(IN_SIZES)
EPS = 1e-5

kernel_name = "hybrid_diffattn_ssd_memory_encoder"


def _split(t, sizes):
    idx = np.cumsum(np.array(sizes))[:-1].tolist()
    return jnp.split(t, idx, axis=-1)


def layer_norm(x, g, b):
    xf = x.astype(jnp.float32)
    mu = jnp.mean(xf, axis=-1, keepdims=True)
    var = jnp.mean(jnp.square(xf - mu), axis=-1, keepdims=True)
    y = (xf - mu) * lax.rsqrt(var + EPS) * g.astype(jnp.float32) + b.astype(jnp.float32)
    return y.astype(x.dtype)


def rms_norm(x, g):
    xf = x.astype(jnp.float32)
    y = xf * lax.rsqrt(jnp.mean(jnp.square(xf), axis=-1, keepdims=True) + EPS)
    return (y * g.astype(jnp.float32)).astype(x.dtype)


def centered_depthwise_conv(u, w, bias):
    T = u.shape[1]
    pad = (D_CONV - 1) // 2
    up = jnp.pad(u, ((0, 0), (pad, pad), (0, 0)))
    out = up[:, 0:T] * w[0]
    for k in range(1, D_CONV):
        out = out + up[:, k:k + T] * w[k]
    return out + bias


def ssd_scan(x, dt, A, Bm, Cm):
    b, T, H, P = x.shape
    G, N = Bm.shape[2], Bm.shape[3]
    E = H // G
    nc = T // CHUNK
    xd = (x * dt[..., None]).reshape(b, nc, CHUNK, G, E, P)
    a = (dt * A).reshape(b, nc, CHUNK, G, E)
    Bc = Bm.reshape(b, nc, CHUNK, G, N)
    Cc = Cm.reshape(b, nc, CHUNK, G, N)
    a_cum = jnp.cumsum(a, axis=2)
    seg = a_cum[:, :, :, None] - a_cum[:, :, None, :]
    mask = jnp.tril(jnp.ones((CHUNK, CHUNK), dtype=bool))[None, None, :, :, None, None]
    Lmat = jnp.exp(jnp.where(mask, seg, -jnp.inf))
    CB = jnp.einsum("bclgn,bcsgn->bclsg", Cc, Bc)
    y_diag = jnp.einsum("bclsge,bcsgep->bclgep", CB[..., None] * Lmat, xd)
    decay_states = jnp.exp(a_cum[:, :, -1:] - a_cum)
    states = jnp.einsum("bclgn,bclge,bclgep->bcgepn", Bc, decay_states, xd)
    chunk_decay = jnp.exp(a_cum[:, :, -1])

    def step(h, inp):
        s, d = inp
        return h * d[..., None, None] + s, h

    h0 = jnp.zeros_like(states[:, 0])
    _, h_in = lax.scan(step, h0, (jnp.swapaxes(states, 0, 1), jnp.swapaxes(chunk_decay, 0, 1)))
    h_in = jnp.swapaxes(h_in, 0, 1)
    y_off = jnp.einsum("bclgn,bcgepn,bclge->bclgep", Cc, h_in, jnp.exp(a_cum))
    return (y_diag + y_off).reshape(b, T, H, P)


def ssm_branch(xbc, dt_raw, z, conv_w, conv_b, dt_bias, a_log, d_skip, norm_g):
    b, T, _ = xbc.shape
    xbc = jax.nn.silu(centered_depthwise_conv(xbc, conv_w, conv_b))
    xs, Bm, Cm = _split(xbc, (D_SSM, SSM_GROUPS * SSM_STATE, SSM_GROUPS * SSM_STATE))
    xs = xs.reshape(b, T, SSM_HEADS, SSM_HEAD_DIM)
    Bm = Bm.reshape(b, T, SSM_GROUPS, SSM_STATE)
    Cm = Cm.reshape(b, T, SSM_GROUPS, SSM_STATE)
    dt = jax.nn.softplus(dt_raw.reshape(b, T, 2, SSM_HEADS) + dt_bias)
    A = -jnp.exp(a_log.astype(jnp.float32))
    flip = lambda t: jnp.flip(t, axis=1)
    y_f = ssd_scan(xs, dt[:, :, 0], A[0], Bm, Cm)
    y_b = flip(ssd_scan(flip(xs), flip(dt[:, :, 1]), A[1], flip(Bm), flip(Cm)))
    y = (y_f + y_b + d_skip[:, None] * xs).astype(xs.dtype)
    y = y.reshape(b, T, D_SSM) * jax.nn.silu(z)
    y = rms_norm(y.reshape(b, T, SSM_GROUPS, D_SSM // SSM_GROUPS), jnp.ones((D_SSM // SSM_GROUPS,), y.dtype))
    return y.reshape(b, T, D_SSM) * norm_g


def diff_attention(q, k, v, lam, lam_init, subln_g):
    b, T, _ = q.shape
    H, d = ATT_HEADS, ATT_HEAD_DIM
    q = q.reshape(b, T, H, 2, d)
    k = k.reshape(b, T, H, 2, d)
    v = v.reshape(b, T, H, 2 * d)
    slopes = 2.0 ** (-8.0 * jnp.arange(1, H + 1, dtype=jnp.float32) / H)
    kpos = jnp.arange(T, dtype=jnp.int32)
    scale = 1.0 / math.sqrt(d)
    nb = T // Q_BLOCK
    qb = jnp.transpose(q.reshape(b, nb, Q_BLOCK, H, 2, d), (1, 0, 2, 3, 4, 5))
    starts = jnp.arange(nb, dtype=jnp.int32) * Q_BLOCK

    def block(args):
        qblk, q0 = args
        s = jnp.einsum("bqhmd,bkhmd->bhmqk", qblk, k).astype(jnp.float32) * scale
        qpos = q0 + jnp.arange(Q_BLOCK, dtype=jnp.int32)
        dist = jnp.abs(qpos[:, None] - kpos[None, :]).astype(jnp.float32)
        s = s - slopes[None, :, None, None, None] * dist[None, None, None]
        p = jax.nn.softmax(s, axis=-1)
        w = p[:, :, 0] - lam * p[:, :, 1]
        return jnp.einsum("bhqk,bkhe->bqhe", w.astype(v.dtype), v)

    out = lax.map(block, (qb, starts))
    out = jnp.transpose(out, (1, 0, 2, 3, 4)).reshape(b, T, H, 2 * d)
    out = rms_norm(out, subln_g) * (1.0 - lam_init)
    return out.reshape(b, T, D_ATT)


def memory_attention(q, k, v):
    b, T, _ = q.shape
    q = q.reshape(b, T, MEM_HEADS, MEM_HEAD_DIM)
    k = k.reshape(b, -1, MEM_HEADS, MEM_HEAD_DIM)
    v = v.reshape(b, -1, MEM_HEADS, MEM_HEAD_DIM)
    s = jnp.einsum("bqhd,bkhd->bhqk", q, k).astype(jnp.float32) / math.sqrt(MEM_HEAD_DIM)
    p = jax.nn.softmax(s, axis=-1)
    out = jnp.einsum("bhqk,bkhd->bqhd", p.astype(v.dtype), v)
    return out.reshape(b, T, D_MEM)


def encoder_layer(x, mem, layer_idx, w_in, conv_w, conv_b, dt_bias, a_log, d_skip, ssm_norm_g,
                  diff_lambda, subln_g, w_mem_kv, w_out, ln_g, ln_b):
    alpha = (2.0 * DEPTH) ** 0.25
    proj = jnp.einsum("btd,de->bte", x, w_in)
    q_att, k_att, v_att, g_att, z_ssm, xbc, dt_raw, q_mem, g_mem = _split(proj, IN_SIZES)
    lam_init = 0.8 - 0.6 * math.exp(-0.3 * layer_idx)
    dl = diff_lambda.astype(jnp.float32)
    lam = jnp.exp(jnp.sum(dl[0] * dl[1])) - jnp.exp(jnp.sum(dl[2] * dl[3])) + lam_init
    h_att = diff_attention(q_att, k_att, v_att, lam, lam_init, subln_g) * jax.nn.silu(g_att)
    h_ssm = ssm_branch(xbc, dt_raw, z_ssm, conv_w, conv_b, dt_bias, a_log, d_skip, ssm_norm_g)
    k_mem, v_mem = _split(jnp.einsum("bmd,de->bme", mem, w_mem_kv), (D_MEM, D_MEM))
    h_mem = memory_attention(q_mem, k_mem, v_mem) * jax.nn.silu(g_mem)
    h = jnp.concatenate([h_att, h_ssm, h_mem], axis=-1)
    out = jnp.einsum("bte,ed->btd", h, w_out)
    return layer_norm(alpha * x + out, ln_g, ln_b)


def setup_inputs(seed: int = 0) -> dict:
    key = jax.random.key(seed)
    ks = jax.random.split(key, 20)
    f32 = jnp.float32
    beta = (8.0 * DEPTH) ** -0.25
    col_scale = jnp.ones((D_IN,), f32).at[2 * D_ATT:3 * D_ATT].set(beta)
    w_in = jax.random.normal(ks[4], (DEPTH, D_MODEL, D_IN), f32) * D_MODEL ** -0.5 * col_scale
    dt0 = jnp.exp(jax.random.uniform(ks[7], (DEPTH, 2, SSM_HEADS), f32, math.log(1e-3), math.log(1e-1)))
    mem_scale = jnp.ones((2 * D_MEM,), f32).at[D_MEM:].set(beta)
    return {
        "x_prompt": jax.random.normal(ks[0], (BATCH, SEQ, D_MODEL), f32),
        "x_sample": jax.random.normal(ks[1], (DEC_BATCH, DEC_SEQ, D_MODEL), f32),
        "mem_prompt": jax.random.normal(ks[2], (BATCH, MEM_TOKENS, D_MODEL), f32),
        "mem_sample": jax.random.normal(ks[3], (DEC_BATCH, MEM_TOKENS, D_MODEL), f32),
        "ln_in_g": 1.0 + 0.02 * jax.random.normal(ks[5], (D_MODEL,), f32),
        "ln_in_b": 0.02 * jax.random.normal(ks[6], (D_MODEL,), f32),
        "w_in": w_in,
        "conv_w": jax.random.normal(ks[8], (DEPTH, D_CONV, D_XBC), f32) * D_CONV ** -0.5,
        "conv_b": 0.01 * jax.random.normal(ks[9], (DEPTH, D_XBC), f32),
        "dt_bias": dt0 + jnp.log(-jnp.expm1(-dt0)),
        "a_log": jnp.log(jax.random.uniform(ks[10], (DEPTH, 2, SSM_HEADS), f32, 1.0, 16.0)),
        "d_skip": 1.0 + 0.02 * jax.random.normal(ks[11], (DEPTH, SSM_HEADS), f32),
        "ssm_norm_g": 1.0 + 0.02 * jax.random.normal(ks[12], (DEPTH, D_SSM), f32),
        "diff_lambda": 0.1 * jax.random.normal(ks[13], (DEPTH, 4, ATT_HEAD_DIM), f32),
        "subln_g": 1.0 + 0.02 * jax.random.normal(ks[14], (DEPTH, 2 * ATT_HEAD_DIM), f32),
        "w_mem_kv": jax.random.normal(ks[15], (DEPTH, D_MODEL, 2 * D_MEM), f32) * D_MODEL ** -0.5 * mem_scale,
        "w_out": jax.random.normal(ks[16], (DEPTH, D_MIX, D_MODEL), f32) * D_MIX ** -0.5 * beta,
        "ln_g": 1.0 + 0.02 * jax.random.normal(ks[17], (DEPTH, D_MODEL), f32),
        "ln_b": 0.02 * jax.random.normal(ks[18], (DEPTH, D_MODEL), f32),
    }


def reference(x_prompt, x_sample, mem_prompt, mem_sample, ln_in_g, ln_in_b, w_in, conv_w, conv_b,
              dt_bias, a_log, d_skip, ssm_norm_g, diff_lambda, subln_g, w_mem_kv, w_out, ln_g, ln_b):
    def trunk(x, mem):
        x = layer_norm(x, ln_in_g, ln_in_b)
        for l in range(DEPTH):
            x = encoder_layer(x, mem, l, w_in[l], conv_w[l], conv_b[l], dt_bias[l], a_log[l], d_skip[l],
                              ssm_norm_g[l], diff_lambda[l], subln_g[l], w_mem_kv[l], w_out[l],
                              ln_g[l], ln_b[l])
        return x

    y_prompt = trunk(x_prompt, mem_prompt)
    y_sample = trunk(x_sample, mem_sample)
    return (y_prompt, y_sample)
```

```python
from contextlib import ExitStack
import math
import numpy as np
import concourse.bass as bass
import concourse.mybir as mybir
from concourse.bass_utils import run_bass_kernel_spmd

F32 = mybir.dt.float32
BF16 = mybir.dt.bfloat16
I32 = mybir.dt.int32
AF = mybir.ActivationFunctionType
ALU = mybir.AluOpType
AX = mybir.AxisListType

D = 2048
NKC = 16
D_XBC = 3072
D_MIX = 4608
NMEM = 256
EPS = 1e-5
ALPHA = 2.0 ** 0.25
LAM_INIT = 0.2
ATT_SCALE = 1.0 / math.sqrt(128.0)
SLOPES = [2.0 ** (-(h + 1)) for h in range(8)]
ALIBI_CUT = 64.0

EPOCH = 16000


class _Op:
    __slots__ = ("eng", "fn", "dma", "slot", "idx", "waits", "signal", "seq", "slot_cnt", "name")


class Sched:
    ENGS = ("pe", "act", "dve", "pool", "sp")

    def __init__(self, nc, same_engine_sync=True):
        self.nc = nc
        self.ops = {e: [] for e in self.ENGS}
        self.last_w = {}
        self.readers = {}
        self.seen = {e: {} for e in self.ENGS}
        self.slot_cnt = {}
        self.slot_last = {}
        self.same = same_engine_sync
        self.barrier_deps = []
        self.nops = 0

    def _need(self, o, d):
        if d.dma:
            key = ("s", d.slot)
            val = d.slot_cnt
        else:
            if d.eng == o.eng and (d.eng == "pe" or not self.same):
                return False
            key = ("c", d.eng)
            val = d.idx
        if self.seen[o.eng].get(key, 0) >= val:
            return False
        self.seen[o.eng][key] = val
        return True

    def op(self, eng, fn, reads=(), writes=(), dma=False, slot=None, name=None):
        o = _Op()
        o.eng = eng; o.fn = fn; o.dma = dma; o.slot = slot; o.signal = False; o.seq = None
        o.name = name; o.slot_cnt = 0
        lst = self.ops[eng]
        o.idx = len(lst) + 1
        deps = []
        for r in reads:
            deps.append(self.last_w.get(r))
        for w in writes:
            deps.append(self.last_w.get(w))
            rd = self.readers.get(w)
            if rd:
                deps.extend(rd.values())
        deps.extend(self.barrier_deps)
        if getattr(self, "serial", False) and getattr(self, "prev_op", None) is not None:
            deps.append(self.prev_op)
        self.prev_op = o
        if dma:
            assert slot is not None
            deps.append(self.slot_last.get(slot))
            c = self.slot_cnt.get(slot, 0) + 1
            self.slot_cnt[slot] = c
            o.slot_cnt = c
            self.slot_last[slot] = o
        deps = [d for d in deps if d is not None and d is not o]
        deps.sort(key=lambda d: -(d.slot_cnt if d.dma else d.idx))
        waits = []
        for d in deps:
            if self._need(o, d):
                d.signal = True
                waits.append(d)
        o.waits = waits
        lst.append(o)
        for r in reads:
            self.readers.setdefault(r, {})[("d", slot) if dma else eng] = o
        for w in writes:
            self.last_w[w] = o
            self.readers[w] = {}
        self.nops += 1
        return o

    def barrier(self):
        deps = []
        for e in self.ENGS:
            for o in reversed(self.ops[e]):
                if not o.dma:
                    deps.append(o)
                    break
        deps.extend(self.slot_last.values())
        self.barrier_deps = deps

    def emit(self):
        nc = self.nc
        self.barrier()
        self.op("sp", None, name="final")
        with ExitStack() as es:
            semtab = {}

            def getsem(key):
                if key not in semtab:
                    semtab[key] = es.enter_context(nc.semaphore("s%d" % len(semtab)))
                return semtab[key]

            for e in self.ENGS:
                n = 0
                for o in self.ops[e]:
                    if not o.dma and o.signal:
                        o.seq = n
                        n += 1
            engobj = {"pe": nc.tensor, "act": nc.scalar, "dve": nc.vector, "pool": nc.gpsimd, "sp": nc.sync}
            DE = EPOCH // 16
            for e in self.ENGS:
                for o in self.ops[e]:
                    if o.dma:
                        getsem(("s", o.slot, (o.slot_cnt - 1) // DE))
                    elif o.signal:
                        getsem(("c", e, o.seq // EPOCH))

            def run(e):
                eo = engobj[e]
                for o in self.ops[e]:
                    for d in o.waits:
                        if d.dma:
                            k = d.slot_cnt - 1
                            eo.wait_ge(getsem(("s", d.slot, k // DE)), 16 * (k % DE + 1))
                        else:
                            eo.wait_ge(getsem(("c", d.eng, d.seq // EPOCH)), d.seq % EPOCH + 1)
                    if o.fn is None:
                        continue
                    ins = o.fn()
                    if o.dma:
                        k = o.slot_cnt - 1
                        ins.then_inc(getsem(("s", o.slot, k // DE)), 16)
                    elif o.signal:
                        ins.then_inc(getsem(("c", e, o.seq // EPOCH)), 1)

            with nc.Block() as block:
                @block.tensor
                def _(x):
                    run("pe")

                @block.scalar
                def _(x):
                    run("act")

                @block.vector
                def _(x):
                    run("dve")

                @block.gpsimd
                def _(x):
                    run("pool")

                @block.sync
                def _(x):
                    run("sp")
            self.nsem = len(semtab)


class Prog:
    def __init__(self, jobs, debug=()):
        self.jobs = jobs
        self.debug = set(debug)
        self.nc = nc = bass.Bass("TRN2", target_bir_lowering=False)
        self.S = Sched(nc)
        self.es = ExitStack()
        self.dram = {}
        self.psum_rr = 0
        self.uid = 0

    def din(self, name, shape, dt=F32):
        t = self.nc.dram_tensor(name, list(shape), dt, kind="ExternalInput").ap()
        self.dram[name] = t
        return t

    def dout(self, name, shape, dt=F32):
        t = self.nc.dram_tensor(name, list(shape), dt, kind="ExternalOutput").ap()
        self.dram[name] = t
        return t

    def dscr(self, name, shape, dt=BF16):
        kind = "ExternalOutput" if name in self.debug else "Internal"
        t = self.nc.dram_tensor(name, list(shape), dt, kind=kind).ap()
        self.dram[name] = t
        return t

    def arena_init(self, words):
        self.A = self.es.enter_context(self.nc.sbuf_tensor("arena", [128, words], F32))
        self.AW = words
        self.aoff = 0
        self.perm_off = 0

    def af32(self, n, perm=False):
        off = self.aoff
        self.aoff += n
        assert self.aoff <= self.AW, ("arena overflow", self.aoff, self.AW)
        if perm:
            self.perm_off = self.aoff
        return self.A[:, off:off + n]

    def abf(self, n, perm=False):
        w = (n + 1) // 2
        v = self.af32(w, perm)
        return v.bitcast(BF16)[:, 0:n]

    def phase_reset(self):
        self.S.barrier()
        self.aoff = self.perm_off

    def dbg(self, name, ap, key, n):
        if name not in self.debug:
            return
        o = self.nc.dram_tensor("dbg_" + name, [128, n], ap.dtype, kind="ExternalOutput").ap()
        self.S.op("sp", lambda: self.nc.sync.dma_start(out=o[:, :], in_=ap), reads=[key], dma=True, slot="dbg_" + name)

    def bank(self):
        b = self.psum_rr
        self.psum_rr = (self.psum_rr + 1) % 8
        return b

    def u(self):
        self.uid += 1
        return self.uid


FM_GROUPS = [("QT", 0, 16), ("KT", 16, 16), ("XBC", 32, 24), ("QM", 56, 4)]
TM_GROUPS = [("V", 0, 4, False), ("SG", 4, 4, True), ("SZ", 8, 4, True), ("SGM", 12, 1, True)]


def _p_setup(P):
    nc = P.nc
    P.w_fm = P.din("w_fm", [60, 128, NKC * 128])
    P.w_tm = P.din("w_tm", [13, 128, NKC * 512])
    P.w_dt = P.din("w_dt", [128, NKC * 64])
    P.w_mk = P.din("w_mk", [4, 128, NKC * 128])
    P.w_mv = P.din("w_mv", [128, NKC * 512])
    P.w_o = P.din("w_o", [8, 128, 36 * 256])
    P.vecs = P.din("vecs", [1, 5 * D])
    P.smalls = P.din("smalls", [1, 1024])
    P.convp = P.din("convp", [128, 24 * 6])
    P.jx = {}
    P.jmem = {}
    P.jy = {}
    for (jn, TK, TQ) in P.jobs:
        P.jx[jn] = P.din("x_" + jn, [TK, D])
        P.jmem[jn] = P.din("mem_" + jn, [NMEM, D])
        P.jy[jn] = P.dout("y_" + jn, [TQ, D])
    P.arena_init(51200)
    P.PS = P.es.enter_context(nc.psum_tensor("ps", [128, 8, 512], F32))


def _p_consts(P):
    nc, S = P.nc, P.S
    C = P.C = {}
    jmp = P.af32(128, perm=True)
    S.op("pool", lambda: nc.gpsimd.iota(jmp, pattern=[[1, 128]], base=0, channel_multiplier=-1,
                                        allow_small_or_imprecise_dtypes=True), writes=["c_jmp"])

    def mk(key, op, dt=F32):
        t = P.af32(128, perm=True) if dt == F32 else P.abf(128, perm=True)
        S.op("dve", lambda: nc.vector.tensor_single_scalar(out=t, in_=jmp, scalar=0.0, op=op),
             reads=["c_jmp"], writes=["c_" + key])
        C[key] = t
        return t
    mk("ident", ALU.is_equal)
    mk("ident_bf", ALU.is_equal, BF16)
    mk("le", ALU.is_ge)
    mk("ge", ALU.is_le)
    mk("lt", ALU.is_gt)
    mk("gt", ALU.is_lt)
    for key, srck in (("le4f", "le"), ("ge4f", "ge")):
        t4 = P.af32(512, perm=True)
        t4b = P.abf(512, perm=True)
        for i in range(4):
            S.op("dve", lambda t4=t4, i=i, srck=srck: nc.vector.tensor_copy(out=t4[:, i * 128:(i + 1) * 128], in_=C[srck]),
                 reads=["c_" + srck], writes=["c_" + key])
            S.op("dve", lambda t4b=t4b, i=i, srck=srck: nc.vector.tensor_copy(out=t4b[:, i * 128:(i + 1) * 128], in_=C[srck]),
                 reads=["c_" + srck], writes=["c_" + key + "b"])
        C[key] = t4
        C[key + "b"] = t4b
    ones = P.af32(128, perm=True)
    S.op("dve", lambda: nc.vector.memset(ones, 1.0), writes=["c_ones"])
    C["ones"] = ones
    dt_ = P.af32(4 * 512, perm=True)
    dt3 = dt_.rearrange("p (a b) -> p a b", a=4)
    S.op("pool", lambda: nc.gpsimd.iota(dt_, pattern=[[-128, 4], [1, 512]], base=0, channel_multiplier=-1,
                                        allow_small_or_imprecise_dtypes=True), writes=["c_dtile"])
    S.op("dve", lambda: nc.vector.scalar_tensor_tensor(out=dt_, in0=dt_, scalar=-1.0, in1=dt_, op0=ALU.mult, op1=ALU.min),
         reads=["c_dtile"], writes=["c_dtile2", "c_dtile"])
    C["dtile"] = dt3
    IL = P.af32(64, perm=True)
    IR = P.af32(64, perm=True)
    IQ2 = P.af32(4, perm=True)
    S.op("pool", lambda: nc.gpsimd.iota(IL, pattern=[[-128, 64]], base=0, channel_multiplier=1,
                                        allow_small_or_imprecise_dtypes=True), writes=["c_IL"])
    S.op("pool", lambda: nc.gpsimd.iota(IR, pattern=[[128, 64]], base=0, channel_multiplier=1,
                                        allow_small_or_imprecise_dtypes=True), writes=["c_IR"])
    S.op("pool", lambda: nc.gpsimd.iota(IQ2, pattern=[[-128, 4]], base=512, channel_multiplier=-1,
                                        allow_small_or_imprecise_dtypes=True), writes=["c_IQ2"])
    bL = P.af32(8 * 64, perm=True).rearrange("p (h j) -> p h j", h=8)
    bR = P.af32(8 * 64, perm=True).rearrange("p (h j) -> p h j", h=8)
    fL = P.af32(8 * 4, perm=True).rearrange("p (h j) -> p h j", h=8)
    fR = P.af32(8 * 4, perm=True).rearrange("p (h j) -> p h j", h=8)
    for h in range(8):
        sl = SLOPES[h]
        S.op("dve", lambda h=h, sl=sl: nc.vector.tensor_single_scalar(out=bL[:, h, :], in_=IL, scalar=sl, op=ALU.mult),
             reads=["c_IL"], writes=["c_bL"])
        S.op("dve", lambda h=h, sl=sl: nc.vector.tensor_single_scalar(out=bR[:, h, :], in_=IR, scalar=-sl, op=ALU.mult),
             reads=["c_IR"], writes=["c_bR"])
        S.op("act", lambda h=h, sl=sl: nc.scalar.activation(out=fL[:, h, :], in_=IR[:, 0:4], func=AF.Exp, scale=-sl),
             reads=["c_IR"], writes=["c_fL"])
        S.op("act", lambda h=h, sl=sl: nc.scalar.activation(out=fR[:, h, :], in_=IQ2, func=AF.Exp, scale=-sl),
             reads=["c_IQ2"], writes=["c_fR"])
    C["bL"], C["bR"], C["fL"], C["fR"] = bL, bR, fL, fR
    sm = P.af32(1024, perm=True)
    S.op("sp", lambda: nc.sync.dma_start(out=sm, in_=P.smalls[0, :].partition_broadcast(128)),
         writes=["c_sm"], dma=True, slot="c_sm")
    C["dl"] = sm[:, 0:512]
    C["subln"] = sm[:, 512:768]
    C["dtb"] = sm[:, 768:832]
    C["alog"] = sm[:, 832:896]
    C["dskip"] = sm[:, 896:928]
    junk = P.af32(128)
    s12 = P.af32(4, perm=True)
    S.op("dve", lambda: nc.vector.tensor_tensor(out=junk, in0=sm[:, 0:128], in1=sm[:, 128:256], op=ALU.mult),
         reads=["c_sm"], writes=["junk0"])
    S.op("dve", lambda: nc.vector.reduce_sum(out=s12[:, 0:1], in_=junk, axis=AX.X), reads=["junk0"], writes=["c_s1"])
    S.op("dve", lambda: nc.vector.tensor_tensor(out=junk, in0=sm[:, 256:384], in1=sm[:, 384:512], op=ALU.mult),
         reads=["c_sm", "c_s1"], writes=["junk0"])
    S.op("dve", lambda: nc.vector.reduce_sum(out=s12[:, 1:2], in_=junk, axis=AX.X), reads=["junk0"], writes=["c_s2"])
    S.op("act", lambda: nc.scalar.activation(out=s12[:, 2:4], in_=s12[:, 0:2], func=AF.Exp),
         reads=["c_s1", "c_s2"], writes=["c_e12"])
    nlam = P.af32(1, perm=True)
    S.op("dve", lambda: nc.vector.tensor_tensor(out=nlam, in0=s12[:, 3:4], in1=s12[:, 2:3], op=ALU.subtract),
         reads=["c_e12"], writes=["c_nlam"])
    S.op("dve", lambda: nc.vector.tensor_single_scalar(out=nlam, in_=nlam, scalar=-LAM_INIT, op=ALU.add),
         reads=["c_nlam"], writes=["c_nlam"])
    C["nlam"] = nlam
    S.op("dve", lambda: nc.vector.tensor_single_scalar(out=C["subln"], in_=C["subln"], scalar=1.0 - LAM_INIT, op=ALU.mult),
         reads=["c_sm", "c_s1", "c_s2"], writes=["c_subln"])
    Abc = P.af32(64, perm=True)
    S.op("act", lambda: nc.scalar.activation(out=Abc, in_=C["alog"], func=AF.Exp), reads=["c_sm"], writes=["c_A"])
    S.op("dve", lambda: nc.vector.tensor_single_scalar(out=Abc, in_=Abc, scalar=-1.0, op=ALU.mult), reads=["c_A"], writes=["c_A"])
    C["A"] = Abc
    mh = P.af32(1, perm=True)
    S.op("dve", lambda: nc.vector.memset(mh, -0.5), writes=["c_mh"])
    C["mhalf"] = mh
    cp = P.af32(24 * 6, perm=True)
    S.op("sp", lambda: nc.sync.dma_start(out=cp, in_=P.convp[:, :]), writes=["c_convp"], dma=True, slot="c_convp")
    C["convp"] = cp.rearrange("p (f k) -> p f k", f=24)
    P.WOB = P.dscr("WOB", [8, 128, 36 * 256])
    P.aoff = P.perm_off
    wtmp = [P.abf(36 * 256) for _ in range(2)]
    for cc in range(8):
        ws = cc % 2
        kw = ("c_wtmp", ws)
        S.op("pool", lambda ws=ws, cc=cc: nc.gpsimd.dma_start(out=wtmp[ws], in_=P.w_o[cc], max_dma_last_dim=8192),
             writes=[kw], dma=True, slot=kw)
        S.op("sp", lambda ws=ws, cc=cc: nc.sync.dma_start(out=P.WOB[cc], in_=wtmp[ws]), reads=[kw], writes=[("WOB", cc)],
             dma=True, slot=("c_wst", ws))
    P.aoff = P.perm_off


def _ln_rstd(P, mv, rstd, nmr, rk, wk):
    nc, S = P.nc, P.S
    S.op("dve", lambda: nc.vector.tensor_single_scalar(out=rstd, in_=mv[:, 1:2], scalar=EPS, op=ALU.add),
         reads=rk, writes=[wk + "_rstd"])
    S.op("pool", lambda: nc.gpsimd.tensor_tensor(out=rstd, in0=rstd, in1=P.C["mhalf"], op=ALU.pow),
         reads=[wk + "_rstd", "c_mh"], writes=[wk + "_rstd"])
    S.op("dve", lambda: nc.vector.tensor_scalar(out=nmr, in0=mv[:, 0:1], scalar1=rstd, scalar2=-1.0, op0=ALU.mult, op1=ALU.mult),
         reads=rk + [wk + "_rstd"], writes=[wk + "_nmr"])


def _phase1(P, job):
    nc, S, C = P.nc, P.S, P.C
    jn, TK, TQ = job
    NB = min(1024, TQ)
    nblk = TK // NB
    ntt = NB // 128
    nsb = NB // 512
    x = P.jx[jn]
    sc = {}
    sc["QT"] = P.dscr(jn + "_QT", [D, TQ]); sc["KT"] = P.dscr(jn + "_KT", [D, TK])
    sc["V"] = P.dscr(jn + "_V", [TK, D]); sc["SG"] = P.dscr(jn + "_SG", [TQ, D]); sc["SZ"] = P.dscr(jn + "_SZ", [TQ, D])
    sc["XBC"] = P.dscr(jn + "_XBC", [D_XBC, TK]); sc["DTR"] = P.dscr(jn + "_DTR", [TK, 64], F32)
    sc["QM"] = P.dscr(jn + "_QM", [512, TQ]); sc["SGM"] = P.dscr(jn + "_SGM", [TQ, 512])
    sc["XN"] = P.dscr(jn + "_XN", [TQ, D], F32)
    sc["HT"] = P.dscr(jn + "_HT", [D_MIX, TQ])
    P.sc[jn] = sc

    P.phase_reset()
    vb = P.af32(2 * D)
    S.op("sp", lambda: nc.sync.dma_start(out=vb, in_=P.vecs[0, 0:2 * D].partition_broadcast(128)),
         writes=["p1_vb"], dma=True, slot="p1_vb")
    gin, bin_ = vb[:, 0:D], vb[:, D:2 * D]
    xnT2 = [P.abf(NKC * NB).rearrange("p (k t) -> p k t", k=NKC) for _ in range(2)]
    xin = [P.af32(D) for _ in range(2)]
    xc = [P.af32(D) for _ in range(2)]
    wfm = [P.abf(NKC * 128) for _ in range(3)]
    wtm = [P.abf(NKC * 512) for _ in range(2)]
    wdt = P.abf(NKC * 64)
    ost = [P.abf(512) for _ in range(4)]
    odt = [P.af32(64) for _ in range(2)]
    st = [P.af32(24) for _ in range(2)]
    mv = [P.af32(2) for _ in range(2)]
    rs = [P.af32(2) for _ in range(2)]
    S.op("pool", lambda: nc.gpsimd.dma_start(out=wdt, in_=P.w_dt[:, :]), writes=["p1_wdt"], dma=True, slot="p1_wdt")
    cnt = {"fm": 0, "tm": 0, "ost": 0, "ev": 0, "odt": 0, "ln": 0}

    def ln_tile(blk, tt):
        xb = blk % 2
        xnT = xnT2[xb]
        own = blk * NB < TQ
        t0 = blk * NB + tt * 128
        b = cnt["ln"] % 2
        cnt["ln"] += 1
        kx, kc_ = ("p1_xin", b), ("p1_xc", b)
        S.op("sp", lambda: nc.sync.dma_start(out=xin[b], in_=x[t0:t0 + 128, :]), writes=[kx], dma=True, slot=kx)
        for q in range(4):
            S.op("dve", lambda q=q: nc.vector.bn_stats(out=st[b][:, q * 6:(q + 1) * 6], in_=xin[b][:, q * 512:(q + 1) * 512]),
                 reads=[kx], writes=[("p1_st", b)])
        S.op("dve", lambda: nc.vector.bn_aggr(out=mv[b], in_=st[b]), reads=[("p1_st", b)], writes=[("p1_mv", b)])
        _ln_rstd(P, mv[b], rs[b][:, 0:1], rs[b][:, 1:2], [("p1_mv", b)], "p1_rs%d" % b)
        rk = ["p1_rs%d_rstd" % b, "p1_rs%d_nmr" % b]
        S.op("act", lambda: nc.scalar.activation(out=xc[b], in_=xin[b], func=AF.Identity, scale=rs[b][:, 0:1], bias=rs[b][:, 1:2]),
             reads=[kx] + rk, writes=[kc_])
        S.op("pool", lambda: nc.gpsimd.tensor_tensor(out=xc[b], in0=xc[b], in1=gin, op=ALU.mult), reads=[kc_, "p1_vb"], writes=[kc_])
        S.op("dve", lambda: nc.vector.tensor_tensor(out=xc[b], in0=xc[b], in1=bin_, op=ALU.add), reads=[kc_, "p1_vb"], writes=[kc_])
        if own:
            S.op("sp", lambda: nc.sync.dma_start(out=sc["XN"][t0:t0 + 128, :], in_=xc[b]),
                 reads=[kc_], writes=[(jn, "XN", t0)], dma=True, slot=("p1_xnst", b))
        for g in range(4):
            bk = P.bank()
            for i in range(4):
                kc = g * 4 + i
                S.op("pe", lambda kc=kc, bk=bk, i=i: nc.tensor.transpose(out=P.PS[:, bk, i * 128:(i + 1) * 128],
                                                                      in_=xc[b][:, kc * 128:(kc + 1) * 128], identity=C["ident"]),
                     reads=[kc_, "c_ident"], writes=[("ps", bk)])
            src = P.PS[:, bk, :].rearrange("p (a b) -> p a b", a=4)
            dst = xnT[:, g * 4:(g + 1) * 4, tt * 128:(tt + 1) * 128]
            if g % 2 == 0:
                S.op("dve", lambda src=src, dst=dst: nc.vector.tensor_copy(out=dst, in_=src),
                     reads=[("ps", bk)], writes=[("p1_xnT", xb, tt)])
            else:
                S.op("act", lambda src=src, dst=dst: nc.scalar.copy(out=dst, in_=src),
                     reads=[("ps", bk)], writes=[("p1_xnT", xb, tt)])

    def proj_items(blk):
        xb = blk % 2
        xnT = xnT2[xb]
        own = blk * NB < TQ
        xkeys = [("p1_xnT", xb, tt) for tt in range(ntt)]
        items = []

        def fm_item(dst, wi, j):
            ws = cnt["fm"] % 3
            cnt["fm"] += 1
            kw = ("p1_wfm", ws)
            S.op("pool", lambda: nc.gpsimd.dma_start(out=wfm[ws], in_=P.w_fm[wi]), writes=[kw], dma=True, slot=kw)
            for sb in range(nsb):
                bk = P.bank()
                for kc in range(NKC):
                    S.op("pe", lambda kc=kc, sb=sb, bk=bk: nc.tensor.matmul(
                        P.PS[:, bk, :], wfm[ws][:, kc * 128:(kc + 1) * 128], xnT[:, kc, sb * 512:(sb + 1) * 512],
                        start=(kc == 0), stop=(kc == NKC - 1)),
                        reads=[kw] + xkeys[sb * 4:(sb + 1) * 4], writes=[("ps", bk)])
                os_ = cnt["ost"] % 4
                cnt["ost"] += 1
                ko = ("p1_ost", os_)
                cnt["ev"] += 1
                if cnt["ev"] % 2 == 0:
                    S.op("dve", lambda os_=os_, bk=bk: nc.vector.tensor_copy(out=ost[os_], in_=P.PS[:, bk, :]),
                         reads=[("ps", bk)], writes=[ko])
                else:
                    S.op("act", lambda os_=os_, bk=bk: nc.scalar.copy(out=ost[os_], in_=P.PS[:, bk, :]),
                         reads=[("ps", bk)], writes=[ko])
                c0 = blk * NB + sb * 512
                S.op("sp", lambda os_=os_, c0=c0: nc.sync.dma_start(
                    out=sc[dst][j * 128:(j + 1) * 128, c0:c0 + 512], in_=ost[os_]),
                    reads=[ko], writes=[(jn, dst, j, c0)], dma=True, slot=ko)

        def tm_item(dst, wi, j, silu):
            ws = cnt["tm"] % 2
            cnt["tm"] += 1
            kw = ("p1_wtm", ws)
            S.op("pool", lambda: nc.gpsimd.dma_start(out=wtm[ws], in_=P.w_tm[wi], max_dma_last_dim=8192),
                 writes=[kw], dma=True, slot=kw)
            for tt in range(ntt):
                bk = P.bank()
                for kc in range(NKC):
                    S.op("pe", lambda kc=kc, tt=tt, bk=bk: nc.tensor.matmul(
                        P.PS[:, bk, :], xnT[:, kc, tt * 128:(tt + 1) * 128], wtm[ws][:, kc * 512:(kc + 1) * 512],
                        start=(kc == 0), stop=(kc == NKC - 1)),
                        reads=[kw, xkeys[tt]], writes=[("ps", bk)])
                os_ = cnt["ost"] % 4
                cnt["ost"] += 1
                ko = ("p1_ost", os_)
                if silu:
                    S.op("act", lambda os_=os_, bk=bk: nc.scalar.activation(out=ost[os_], in_=P.PS[:, bk, :], func=AF.Silu),
                         reads=[("ps", bk)], writes=[ko])
                else:
                    S.op("dve", lambda os_=os_, bk=bk: nc.vector.tensor_copy(out=ost[os_], in_=P.PS[:, bk, :]),
                         reads=[("ps", bk)], writes=[ko])
                t0 = blk * NB + tt * 128
                S.op("sp", lambda os_=os_, t0=t0: nc.sync.dma_start(
                    out=sc[dst][t0:t0 + 128, j * 512:(j + 1) * 512], in_=ost[os_]),
                    reads=[ko], writes=[(jn, dst, t0, j)], dma=True, slot=ko)

        def dt_item():
            for tt in range(ntt):
                bk = P.bank()
                for kc in range(NKC):
                    S.op("pe", lambda kc=kc, tt=tt, bk=bk: nc.tensor.matmul(
                        P.PS[:, bk, 0:64], xnT[:, kc, tt * 128:(tt + 1) * 128], wdt[:, kc * 64:(kc + 1) * 64],
                        start=(kc == 0), stop=(kc == NKC - 1)),
                        reads=["p1_wdt", xkeys[tt]], writes=[("ps", bk)])
                os_ = cnt["odt"] % 2
                cnt["odt"] += 1
                ko = ("p1_odt", os_)
                S.op("dve", lambda os_=os_, bk=bk: nc.vector.tensor_copy(out=odt[os_], in_=P.PS[:, bk, 0:64]),
                     reads=[("ps", bk)], writes=[ko])
                t0 = blk * NB + tt * 128
                S.op("sp", lambda os_=os_, t0=t0: nc.sync.dma_start(out=sc["DTR"][t0:t0 + 128, :], in_=odt[os_]),
                     reads=[ko], writes=[(jn, "DTR", t0)], dma=True, slot=ko)

        for (dst, w0, nw) in FM_GROUPS:
            if not own and dst in ("QT", "QM"):
                continue
            for j in range(nw):
                items.append(lambda dst=dst, wi=w0 + j, j=j: fm_item(dst, wi, j))
        for (dst, c0w, ncw, silu) in TM_GROUPS:
            if not own and dst != "V":
                continue
            for j in range(ncw):
                items.append(lambda dst=dst, wi=c0w + j, j=j, silu=silu: tm_item(dst, wi, j, silu))
        items.append(dt_item)
        return items

    for tt in range(ntt):
        ln_tile(0, tt)
    for blk in range(nblk):
        items = proj_items(blk)
        pending = [(blk + 1, tt) for tt in range(ntt)] if blk + 1 < nblk else []
        stride = max(1, (len(items) - 2) // max(1, len(pending))) if pending else 0
        for i, it in enumerate(items):
            it()
            if pending and i % stride == stride - 1:
                ln_tile(*pending.pop(0))
        while pending:
            ln_tile(*pending.pop(0))


IN_SIZES = (2048, 2048, 2048, 2048, 2048, 3072, 64, 512, 512)
_OFF = np.concatenate([[0], np.cumsum(IN_SIZES)]).tolist()
O_Q, O_K, O_V, O_G, O_Z, O_XBC, O_DT, O_QM, O_GM = _OFF[:9]


def _tile_w(w, cols, width):
    K = w.shape[0]
    sub = w[:, cols]
    n = sub.shape[1]
    t = sub.reshape(K // 128, 128, n // width, width)
    t = np.transpose(t, (2, 1, 0, 3))
    return np.ascontiguousarray(t.reshape(n // width, 128, (K // 128) * width), dtype=np.float32)


def prep_shared(inp, flip):
    w_in = np.asarray(inp["w_in"][0], np.float32)
    ar = np.arange
    fm_cols = np.concatenate([ar(O_Q, O_Q + 2048), ar(O_K, O_K + 2048), ar(O_XBC, O_XBC + 3072), ar(O_QM, O_QM + 512)])
    tm_cols = np.concatenate([ar(O_V, O_V + 2048), ar(O_G, O_G + 2048), ar(O_Z, O_Z + 2048), ar(O_GM, O_GM + 512)])
    dt_cols = ar(O_DT, O_DT + 64)
    conv_w = np.asarray(inp["conv_w"][0], np.float32)
    dt_bias = np.asarray(inp["dt_bias"][0], np.float32)
    a_log = np.asarray(inp["a_log"][0], np.float32)
    if flip:
        dt_cols = np.concatenate([dt_cols[32:], dt_cols[:32]])
        conv_w = conv_w[::-1]
        dt_bias = dt_bias[::-1]
        a_log = a_log[::-1]
    out = {}
    out["w_fm"] = _tile_w(w_in, fm_cols, 128)
    out["w_tm"] = _tile_w(w_in, tm_cols, 512)
    out["w_dt"] = _tile_w(w_in, dt_cols, 64)[0]
    wkv = np.asarray(inp["w_mem_kv"][0], np.float32)
    out["w_mk"] = _tile_w(wkv, ar(0, 512), 128)
    out["w_mv"] = _tile_w(wkv, ar(512, 1024), 512)[0]
    out["w_o"] = _tile_w(np.asarray(inp["w_out"][0], np.float32), ar(0, 2048), 256)
    out["vecs"] = np.concatenate([np.asarray(inp[k], np.float32).reshape(-1) for k in
                                  ("ln_in_g", "ln_in_b", "ssm_norm_g", "ln_g", "ln_b")]).reshape(1, 5 * D)
    sm = np.zeros((1, 1024), np.float32)
    sm[0, 0:512] = np.asarray(inp["diff_lambda"], np.float32).reshape(-1)
    sm[0, 512:768] = np.asarray(inp["subln_g"], np.float32).reshape(-1)
    sm[0, 768:832] = dt_bias.reshape(-1)
    sm[0, 832:896] = a_log.reshape(-1)
    sm[0, 896:928] = np.asarray(inp["d_skip"], np.float32).reshape(-1)
    out["smalls"] = sm
    cb = np.asarray(inp["conv_b"][0], np.float32)
    cp = np.concatenate([conv_w.T, cb[:, None]], axis=1)
    cp = cp.reshape(24, 128, 6).transpose(1, 0, 2).reshape(128, 144)
    out["convp"] = np.ascontiguousarray(cp)
    return out


def _ht_store(P, jn, hts, kh, row0, q0, nj):
    nc, S = P.nc, P.S
    HT = P.sc[jn]["HT"]
    dst = HT[row0:row0 + nj * 128, q0:q0 + 512].rearrange("(j p) q -> p j q", j=nj)
    S.op("sp", lambda: nc.sync.dma_start(out=dst, in_=hts), reads=[kh], writes=[(jn, "HT", row0, q0)], dma=True, slot=kh)


def _phase3(P, job):
    nc, S, C = P.nc, P.S, P.C
    jn, TK, TQ = job
    sc = P.sc[jn]
    nkt = TK // 128
    nqc = TQ // 512
    P.phase_reset()
    KTb = [P.abf(2 * TK).rearrange("p (m t) -> p m t", m=2) for _ in range(2)]
    Vb = [P.abf(nkt * 258).rearrange("p (t e) -> p t e", e=258) for _ in range(2)]
    Qb = [P.abf(2 * 512).rearrange("p (m q) -> p m q", m=2) for _ in range(2)]
    Eb = [P.abf(512) for _ in range(5)]
    tD = [P.af32(512) for _ in range(2)]
    acc2 = [P.af32(4 * 2 * 257).rearrange("p (a m e) -> p a m e", a=4, m=2) for _ in range(2)]
    o_t = [P.af32(256) for _ in range(2)]
    sq_t = P.af32(256)
    t1_t = [P.af32(256) for _ in range(2)]
    sgb = [P.abf(256) for _ in range(2)]
    hb = [P.abf(256) for _ in range(2)]
    hts = [P.abf(2 * 512).rearrange("p (j q) -> p j q", j=2) for _ in range(2)]
    sm = [P.af32(8) for _ in range(2)]
    for b in range(2):
        S.op("dve", lambda b=b: nc.vector.memset(Vb[b][:, :, 256:258], 1.0), writes=[("a_v", b)])
    ctr = {"s": 0, "e": 0, "td": 0, "fin": 0, "q": 0, "hts": 0}

    def sbank():
        b = 4 + ctr["s"] % 4
        ctr["s"] += 1
        return b

    for h in range(8):
        hbuf = h % 2
        kk, kv = ("a_kt", hbuf), ("a_v", hbuf)
        src = sc["KT"][h * 256:(h + 1) * 256, :].rearrange("(m d) t -> d m t", m=2)
        S.op("sp", lambda hbuf=hbuf, src=src: nc.sync.dma_start(out=KTb[hbuf], in_=src),
             reads=[(jn, "KTall")], writes=[kk], dma=True, slot=kk)
        srcv = sc["V"][:, h * 256:(h + 1) * 256].rearrange("(t p) e -> p t e", p=128)
        for v0 in range(0, nkt, 8):
            v1 = min(nkt, v0 + 8)
            S.op("sp", lambda hbuf=hbuf, srcv=srcv, v0=v0, v1=v1: nc.sync.dma_start(out=Vb[hbuf][:, v0:v1, 0:256], in_=srcv[:, v0:v1, :]),
                 reads=[(jn, "Vall")], writes=[kv], dma=True, slot=(kv, v0))
        slope = SLOPES[h]
        for qc in range(nqc):
            q0 = qc * 512
            qb_ = ctr["q"] % 2
            ctr["q"] += 1
            kq = ("a_q", qb_)
            srcq = sc["QT"][h * 256:(h + 1) * 256, q0:q0 + 512].rearrange("(m d) q -> d m q", m=2)
            S.op("sp", lambda qb_=qb_, srcq=srcq: nc.sync.dma_start(out=Qb[qb_], in_=srcq),
                 reads=[(jn, "QTall")], writes=[kq], dma=True, slot=kq)
            kt0 = q0 // 128
            ab = ctr["q"] % 2
            acc = acc2[ab]
            regions = []
            ltiles = [kt for kt in range(0, kt0) if slope * (q0 - (kt * 128 + 127)) < ALIBI_CUT]
            rtiles = [kt for kt in range(kt0 + 4, nkt) if slope * (kt * 128 - (q0 + 511)) < ALIBI_CUT]
            if ltiles:
                regions.append(("L", ltiles))
            regions.append(("D", list(range(kt0, kt0 + 4))))
            if rtiles:
                regions.append(("R", rtiles))
            units = []
            for m in range(2):
                for ri, (rg, kts) in enumerate(regions):
                    for ki, kt in enumerate(kts):
                        units.append(dict(m=m, rg=rg, kt=kt, ki=ki, n=len(kts), first_region=(ri == 0)))

            def emit_qk(un, hbuf=hbuf, qb_=qb_, kk=kk, kq=kq, kt0=kt0, h=h, slope=slope):
                m, rg, kt = un["m"], un["rg"], un["kt"]
                bs = sbank()
                S.op("pe", lambda: nc.tensor.matmul(
                    P.PS[:, bs, :], KTb[hbuf][:, m, kt * 128:(kt + 1) * 128], Qb[qb_][:, m, :], start=True, stop=True),
                    reads=[kk, kq], writes=[("ps", bs)])
                e = ctr["e"] % 5
                ctr["e"] += 1
                ke = ("a_E", e)
                un["e"], un["ke"] = e, ke
                if rg == "L":
                    j = kt0 - kt
                    S.op("act", lambda: nc.scalar.activation(
                        out=Eb[e], in_=P.PS[:, bs, :], func=AF.Exp, scale=ATT_SCALE, bias=C["bL"][:, h, j:j + 1]),
                        reads=[("ps", bs), "c_bL"], writes=[ke])
                elif rg == "R":
                    j = kt - kt0 - 4
                    S.op("act", lambda: nc.scalar.activation(
                        out=Eb[e], in_=P.PS[:, bs, :], func=AF.Exp, scale=ATT_SCALE, bias=C["bR"][:, h, j:j + 1]),
                        reads=[("ps", bs), "c_bR"], writes=[ke])
                else:
                    jd = kt - kt0
                    td = ctr["td"] % 2
                    ctr["td"] += 1
                    S.op("dve", lambda: nc.vector.scalar_tensor_tensor(
                        out=tD[td], in0=C["dtile"][:, jd, :], scalar=slope / ATT_SCALE, in1=P.PS[:, bs, :],
                        op0=ALU.mult, op1=ALU.add),
                        reads=[("ps", bs), "c_dtile2"], writes=[("a_tD", td)])
                    S.op("act", lambda: nc.scalar.activation(
                        out=Eb[e], in_=tD[td], func=AF.Exp, scale=ATT_SCALE),
                        reads=[("a_tD", td)], writes=[ke])

            def emit_pv(un, hbuf=hbuf, kv=kv, h=h, acc=acc, ab=ab):
                m, rg, kt, ki, n = un["m"], un["rg"], un["kt"], un["ki"], un["n"]
                e, ke = un["e"], un["ke"]
                for qb in range(4):
                    S.op("pe", lambda qb=qb: nc.tensor.matmul(
                        P.PS[:, qb, 0:257], Eb[e][:, qb * 128:(qb + 1) * 128], Vb[hbuf][:, kt, 0:257],
                        start=(ki == 0), stop=(ki == n - 1)),
                        reads=[ke, kv], writes=[("ps", qb)])
                if ki != n - 1:
                    return
                for qb in range(4):
                    ka = ("a_acc", ab, qb, m)
                    dst = acc[:, qb, m, :]
                    srcp = P.PS[:, qb, 0:257]
                    if rg == "L":
                        f = C["fL"][:, h, qb:qb + 1]
                    elif rg == "R":
                        f = C["fR"][:, h, qb:qb + 1]
                    else:
                        f = None
                    if un["first_region"]:
                        if f is None:
                            S.op("dve", lambda dst=dst, srcp=srcp: nc.vector.tensor_copy(out=dst, in_=srcp),
                                 reads=[("ps", qb)], writes=[ka])
                        else:
                            S.op("dve", lambda dst=dst, srcp=srcp, f=f: nc.vector.tensor_scalar(
                                out=dst, in0=srcp, scalar1=f, scalar2=None, op0=ALU.mult),
                                reads=[("ps", qb), "c_fL", "c_fR"], writes=[ka])
                    else:
                        ff = 1.0 if f is None else f
                        S.op("dve", lambda dst=dst, srcp=srcp, ff=ff: nc.vector.scalar_tensor_tensor(
                            out=dst, in0=srcp, scalar=ff, in1=dst, op0=ALU.mult, op1=ALU.add),
                            reads=[("ps", qb), "c_fL", "c_fR", ka], writes=[ka])

            LA = 3
            for i in range(len(units) + LA):
                if i < len(units):
                    emit_qk(units[i])
                if i - LA >= 0:
                    emit_pv(units[i - LA])
            hs = ctr["hts"] % 2
            ctr["hts"] += 1
            kh = ("a_hts", hs)
            for qb in range(4):
                fb = ctr["fin"] % 2
                ctr["fin"] += 1
                smv = sm[fb]
                ks = ("a_sm", fb)
                ko = ("a_o", fb)
                ka0, ka1 = ("a_acc", ab, qb, 0), ("a_acc", ab, qb, 1)
                tq = q0 + qb * 128
                ksg = ("a_sg", fb)
                S.op("sp", lambda fb=fb, tq=tq, h=h: nc.sync.dma_start(out=sgb[fb], in_=sc["SG"][tq:tq + 128, h * 256:(h + 1) * 256]),
                     reads=[(jn, "SGall")], writes=[ksg], dma=True, slot=ksg)
                S.op("dve", lambda smv=smv, qb=qb, acc=acc: nc.vector.reciprocal(out=smv[:, 0:2], in_=acc[:, qb, :, 256]),
                     reads=[ka0, ka1], writes=[ks])
                S.op("dve", lambda smv=smv: nc.vector.tensor_tensor(out=smv[:, 2:3], in0=smv[:, 1:2], in1=C["nlam"], op=ALU.mult),
                     reads=[ks, "c_nlam"], writes=[ks])
                S.op("dve", lambda fb=fb, qb=qb, smv=smv, acc=acc: nc.vector.tensor_scalar(
                    out=o_t[fb], in0=acc[:, qb, 0, 0:256], scalar1=smv[:, 0:1], scalar2=None, op0=ALU.mult),
                    reads=[ka0, ks], writes=[ko])
                S.op("dve", lambda fb=fb, qb=qb, smv=smv, acc=acc: nc.vector.scalar_tensor_tensor(
                    out=o_t[fb], in0=acc[:, qb, 1, 0:256], scalar=smv[:, 2:3], in1=o_t[fb], op0=ALU.mult, op1=ALU.add),
                    reads=[ka1, ks, ko], writes=[ko])
                S.op("dve", lambda fb=fb: nc.vector.tensor_tensor(out=sq_t, in0=o_t[fb], in1=o_t[fb], op=ALU.mult),
                     reads=[ko], writes=["a_sq"])
                S.op("dve", lambda smv=smv: nc.vector.reduce_sum(out=smv[:, 3:4], in_=sq_t, axis=AX.X),
                     reads=["a_sq"], writes=[ks])
                S.op("dve", lambda smv=smv: nc.vector.tensor_scalar(out=smv[:, 4:5], in0=smv[:, 3:4], scalar1=1.0 / 256.0, scalar2=EPS,
                                                                    op0=ALU.mult, op1=ALU.add), reads=[ks], writes=[ks])
                S.op("pool", lambda smv=smv: nc.gpsimd.tensor_tensor(out=smv[:, 4:5], in0=smv[:, 4:5], in1=C["mhalf"], op=ALU.pow),
                     reads=[ks, "c_mh"], writes=[ks])
                S.op("dve", lambda fb=fb: nc.vector.tensor_tensor(out=t1_t[fb], in0=C["subln"], in1=sgb[fb], op=ALU.mult),
                     reads=["c_subln", ksg], writes=[("a_t1", fb)])
                S.op("dve", lambda fb=fb, smv=smv: nc.vector.scalar_tensor_tensor(
                    out=hb[fb], in0=o_t[fb], scalar=smv[:, 4:5], in1=t1_t[fb], op0=ALU.mult, op1=ALU.mult),
                    reads=[ko, ks, ("a_t1", fb)], writes=[("a_hb", fb)])
                bs = sbank()
                psb = P.PS[:, bs, :].bitcast(BF16)
                for j in range(2):
                    S.op("pe", lambda fb=fb, j=j, psb=psb: nc.tensor.transpose(out=psb[:, j * 128:(j + 1) * 128],
                                                                           in_=hb[fb][:, j * 128:(j + 1) * 128], identity=C["ident_bf"]),
                         reads=[("a_hb", fb), "c_ident_bf"], writes=[("ps", bs)])
                S.op("dve", lambda hs=hs, qb=qb, psb=psb: nc.vector.tensor_copy(
                    out=hts[hs][:, :, qb * 128:(qb + 1) * 128], in_=psb[:, 0:256].rearrange("p (j q) -> p j q", j=2)),
                    reads=[("ps", bs)], writes=[kh])
            _ht_store(P, jn, hts[hs], kh, h * 256, q0, 2)


def _phase2(P, job):
    nc, S, C = P.nc, P.S, P.C
    jn, TK, TQ = job
    sc = P.sc[jn]
    nqc = TQ // 512
    P.phase_reset()
    mem = P.jmem[jn]
    mt = [P.af32(D) for _ in range(2)]
    memT = P.abf(NKC * 256).rearrange("p (k t) -> p k t", k=NKC)
    wk = [P.abf(NKC * 128) for _ in range(2)]
    wv = P.abf(NKC * 512)
    KmT = P.abf(4 * 256).rearrange("p (h m) -> p h m", h=4)
    Vm = P.abf(2 * 4 * 130).rearrange("p (t h e) -> p t h e", t=2, h=4)
    Qb = [P.abf(4 * 512).rearrange("p (h q) -> p h q", h=4) for _ in range(2)]
    Eb = [P.abf(512) for _ in range(4)]
    sgb = [P.abf(512) for _ in range(2)]
    rz = [P.af32(4) for _ in range(2)]
    t1 = [P.af32(128) for _ in range(2)]
    hb = [P.abf(128) for _ in range(2)]
    hts = [P.abf(4 * 512).rearrange("p (j q) -> p j q", j=4) for _ in range(2)]
    S.op("dve", lambda: nc.vector.memset(Vm[:, :, :, 128:130], 1.0), writes=["m_Vm1"])
    S.op("pool", lambda: nc.gpsimd.dma_start(out=wv, in_=P.w_mv[:, :], max_dma_last_dim=8192), writes=["m_wv"], dma=True, slot="m_wv")
    for t in range(2):
        kx = ("m_mt", t)
        S.op("sp", lambda t=t: nc.sync.dma_start(out=mt[t], in_=mem[t * 128:(t + 1) * 128, :]), writes=[kx], dma=True, slot=kx)
        for g in range(4):
            bk = P.bank()
            for i in range(4):
                kc = g * 4 + i
                S.op("pe", lambda t=t, kc=kc, bk=bk, i=i: nc.tensor.transpose(out=P.PS[:, bk, i * 128:(i + 1) * 128],
                                                                          in_=mt[t][:, kc * 128:(kc + 1) * 128], identity=C["ident"]),
                     reads=[kx, "c_ident"], writes=[("ps", bk)])
            S.op("dve", lambda t=t, g=g, bk=bk: nc.vector.tensor_copy(
                out=memT[:, g * 4:(g + 1) * 4, t * 128:(t + 1) * 128], in_=P.PS[:, bk, :].rearrange("p (a b) -> p a b", a=4)),
                reads=[("ps", bk)], writes=[("m_memT", t)])
    mk = [("m_memT", 0), ("m_memT", 1)]
    for h in range(4):
        ws = h % 2
        kw = ("m_wk", ws)
        S.op("pool", lambda ws=ws, h=h: nc.gpsimd.dma_start(out=wk[ws], in_=P.w_mk[h]), writes=[kw], dma=True, slot=kw)
        bk = P.bank()
        for kc in range(NKC):
            S.op("pe", lambda ws=ws, kc=kc, bk=bk: nc.tensor.matmul(
                P.PS[:, bk, 0:256], wk[ws][:, kc * 128:(kc + 1) * 128], memT[:, kc, :], start=(kc == 0), stop=(kc == NKC - 1)),
                reads=[kw] + mk, writes=[("ps", bk)])
        S.op("dve", lambda h=h, bk=bk: nc.vector.tensor_copy(out=KmT[:, h, :], in_=P.PS[:, bk, 0:256]),
             reads=[("ps", bk)], writes=["m_KmT"])
    for t in range(2):
        bk = P.bank()
        for kc in range(NKC):
            S.op("pe", lambda t=t, kc=kc, bk=bk: nc.tensor.matmul(
                P.PS[:, bk, :], memT[:, kc, t * 128:(t + 1) * 128], wv[:, kc * 512:(kc + 1) * 512], start=(kc == 0), stop=(kc == NKC - 1)),
                reads=["m_wv", mk[t]], writes=[("ps", bk)])
        S.op("dve", lambda t=t, bk=bk: nc.vector.tensor_copy(
            out=Vm[:, t, :, 0:128], in_=P.PS[:, bk, :].rearrange("p (h e) -> p h e", h=4)),
            reads=[("ps", bk), "m_Vm1"], writes=["m_Vm"])
    ctr = {"s": 0, "e": 0, "f": 0}

    def sbank():
        b = 4 + ctr["s"] % 4
        ctr["s"] += 1
        return b
    scale = 1.0 / math.sqrt(128.0)
    for qc in range(nqc):
        q0 = qc * 512
        qb_ = qc % 2
        kq = ("m_q", qb_)
        srcq = sc["QM"][:, q0:q0 + 512].rearrange("(h d) q -> d h q", h=4)
        S.op("sp", lambda qb_=qb_, srcq=srcq: nc.sync.dma_start(out=Qb[qb_], in_=srcq), writes=[kq], dma=True, slot=kq)
        hs = qc % 2
        kh = ("m_hts", hs)
        for h in range(4):
            for t in range(2):
                bs = sbank()
                S.op("pe", lambda h=h, t=t, qb_=qb_, bs=bs: nc.tensor.matmul(
                    P.PS[:, bs, :], KmT[:, h, t * 128:(t + 1) * 128], Qb[qb_][:, h, :], start=True, stop=True),
                    reads=["m_KmT", kq], writes=[("ps", bs)])
                e = ctr["e"] % 4
                ctr["e"] += 1
                ke = ("m_E", e)
                S.op("act", lambda e=e, bs=bs: nc.scalar.activation(out=Eb[e], in_=P.PS[:, bs, :], func=AF.Exp, scale=scale),
                     reads=[("ps", bs)], writes=[ke])
                for qb in range(4):
                    S.op("pe", lambda e=e, qb=qb, t=t, h=h: nc.tensor.matmul(
                        P.PS[:, qb, 0:129], Eb[e][:, qb * 128:(qb + 1) * 128], Vm[:, t, h, 0:129], start=(t == 0), stop=(t == 1)),
                        reads=[ke, "m_Vm"], writes=[("ps", qb)])
            for qb in range(4):
                fb = ctr["f"] % 2
                ctr["f"] += 1
                tq = q0 + qb * 128
                ksg = ("m_sg", fb)
                if h == 0:
                    pass
                S.op("sp", lambda fb=fb, tq=tq, h=h: nc.sync.dma_start(out=sgb[fb][:, 0:128], in_=sc["SGM"][tq:tq + 128, h * 128:(h + 1) * 128]),
                     writes=[ksg], dma=True, slot=ksg)
                S.op("dve", lambda fb=fb, qb=qb: nc.vector.reciprocal(out=rz[fb][:, 0:1], in_=P.PS[:, qb, 128:129]),
                     reads=[("ps", qb)], writes=[("m_rz", fb)])
                S.op("dve", lambda fb=fb, qb=qb: nc.vector.tensor_scalar(
                    out=t1[fb], in0=P.PS[:, qb, 0:128], scalar1=rz[fb][:, 0:1], scalar2=None, op0=ALU.mult),
                    reads=[("ps", qb), ("m_rz", fb)], writes=[("m_t1", fb)])
                S.op("dve", lambda fb=fb: nc.vector.tensor_tensor(out=hb[fb], in0=t1[fb], in1=sgb[fb][:, 0:128], op=ALU.mult),
                     reads=[("m_t1", fb), ksg], writes=[("m_hb", fb)])
                bs = sbank()
                psb = P.PS[:, bs, :].bitcast(BF16)
                S.op("pe", lambda fb=fb, psb=psb: nc.tensor.transpose(out=psb[:, 0:128], in_=hb[fb], identity=C["ident_bf"]),
                     reads=[("m_hb", fb), "c_ident_bf"], writes=[("ps", bs)])
                S.op("dve", lambda hs=hs, h=h, qb=qb, psb=psb: nc.vector.tensor_copy(
                    out=hts[hs][:, h, qb * 128:(qb + 1) * 128], in_=psb[:, 0:128]),
                    reads=[("ps", bs)], writes=[kh])
        _ht_store(P, jn, hts[hs], kh, 4096, q0, 4)


def _bc(ap, n):
    return ap.to_broadcast([ap.shape[0], ap.shape[1], n])


def _phase4(P, job):
    nc, S, C = P.nc, P.S, P.C
    jn, TK, TQ = job
    sc = P.sc[jn]
    nck, nq = TK // 128, TQ // 128
    XC = sc["XC"] = P.dscr(jn + "_XC", [D_XBC, TK])
    HB = sc["HB"] = P.dscr(jn + "_HB", [nq, 128, 2048])
    P.phase_reset()
    ub = [P.abf(TK + 4) for _ in range(2)]
    dg = [P.abf(5 * 128).rearrange("p (k c) -> p k c", k=5) for _ in range(2)]
    cst = [P.abf(512) for _ in range(4)]
    for b in range(2):
        S.op("dve", lambda b=b: nc.vector.memset(ub[b][:, 0:2], 0.0), writes=[("c4_ub", b)])
        S.op("dve", lambda b=b: nc.vector.memset(ub[b][:, TK + 2:TK + 4], 0.0), writes=[("c4_ub", b)])
    nst = 0
    for ft in range(24):
        b = ft % 2
        ku, kd = ("c4_ub", b), ("c4_dg", b)
        S.op("sp", lambda b=b, ft=ft: nc.sync.dma_start(out=ub[b][:, 2:TK + 2], in_=sc["XBC"][ft * 128:(ft + 1) * 128, :]),
             writes=[ku], dma=True, slot=ku)
        for k in range(5):
            S.op("dve", lambda b=b, ft=ft, k=k: nc.vector.tensor_scalar(
                out=dg[b][:, k, :], in0=C["ident_bf"], scalar1=C["convp"][:, ft, k:k + 1], scalar2=None, op0=ALU.mult),
                reads=["c_ident_bf", "c_convp"], writes=[kd])
        for sb in range(TK // 512):
            bk = P.bank()
            for k in range(5):
                S.op("pe", lambda b=b, k=k, sb=sb, bk=bk: nc.tensor.matmul(
                    P.PS[:, bk, :], dg[b][:, k, :], ub[b][:, sb * 512 + k:sb * 512 + k + 512], start=(k == 0), stop=(k == 4)),
                    reads=[ku, kd], writes=[("ps", bk)])
            cs = nst % 4
            nst += 1
            kc = ("c4_cst", cs)
            S.op("act", lambda cs=cs, bk=bk, ft=ft: nc.scalar.activation(
                out=cst[cs], in_=P.PS[:, bk, :], func=AF.Silu, bias=C["convp"][:, ft, 5:6]),
                reads=[("ps", bk), "c_convp"], writes=[kc])
            S.op("sp", lambda cs=cs, ft=ft, sb=sb: nc.sync.dma_start(out=XC[ft * 128:(ft + 1) * 128, sb * 512:(sb + 1) * 512], in_=cst[cs]),
                 reads=[kc], writes=[(jn, "XC", ft, sb)], dma=True, slot=kc)

    if getattr(P, "p4stop", 9) < 1:
        return
    P.phase_reset()
    vb = P.af32(D)
    S.op("sp", lambda: nc.sync.dma_start(out=vb, in_=P.vecs[0, 2 * D:3 * D].partition_broadcast(128)),
         writes=["s_ng"], dma=True, slot="s_ng")
    xsT = [P.abf(16 * 128).rearrange("p (f t) -> p f t", f=16) for _ in range(2)]
    BCT = [P.abf(8 * 128).rearrange("p (f t) -> p f t", f=8) for _ in range(2)]
    xtok = [P.abf(2048) for _ in range(2)]
    btok = [P.abf(512) for _ in range(2)]
    sml = [P.af32(64 * 10) for _ in range(2)]
    Wm = [P.abf(2 * 512).rearrange("p (d x) -> p d x", d=2) for _ in range(2)]
    xd = [P.abf(4 * 2048).rearrange("p (d x) -> p d x", d=4) for _ in range(2)]
    Ta = [P.af32(512) for _ in range(3)]
    Lb = [P.abf(512) for _ in range(3)]
    Mb2 = [[P.abf(512).rearrange("p (h l) -> p h l", h=4) for _ in range(16)] for _ in range(2)]
    yos = [P.abf(512) for _ in range(4)]
    hst = [P.af32(2048) for _ in range(2)]
    hbf = [P.abf(2048) for _ in range(2)]
    hbs = [P.abf(2048) for _ in range(2)]
    szb = [P.abf(2048) for _ in range(2)]
    yg = [P.af32(512) for _ in range(2)]
    sq = P.af32(512)
    nsm = [P.af32(4) for _ in range(2)]
    hob = [P.abf(512) for _ in range(2)]
    hts = [P.abf(512).rearrange("p (j q) -> p j q", j=4) for _ in range(2)]
    ctr = {"ta": 0, "yo": 0, "g": 0}

    def prep(c, passF):
        b = c % 2
        kx, kb, kt, kbt, ks = ("s_xsT", b), ("s_BCT", b), ("s_xtok", b), ("s_btok", b), ("s_sml", b)
        t0 = c * 128
        S.op("sp", lambda: nc.sync.dma_start(out=xsT[b], in_=XC[0:2048, t0:t0 + 128].rearrange("(f p) t -> p f t", p=128)),
             writes=[kx], dma=True, slot=kx)
        nb_ = 8 if passF else 4
        S.op("sp", lambda: nc.sync.dma_start(out=BCT[b][:, 0:nb_, :], in_=XC[2048:2048 + nb_ * 128, t0:t0 + 128].rearrange("(f p) t -> p f t", p=128)),
             writes=[kb], dma=True, slot=kb)
        v = sml[b]
        dtr, e_, dt, a, acum, ea, dst, dch, dtd = [v[:, i * 64:(i + 1) * 64] for i in range(9)]
        S.op("sp", lambda: nc.sync.dma_start(out=dtr, in_=sc["DTR"][t0:t0 + 128, :]), writes=[ks], dma=True, slot=ks)
        for g8 in range(2):
            bk = P.bank()
            psb = P.PS[:, bk, :].bitcast(BF16)
            for i in range(8):
                f = g8 * 8 + i
                S.op("pe", lambda f=f, i=i, psb=psb: nc.tensor.transpose(out=psb[:, i * 128:(i + 1) * 128], in_=xsT[b][:, f, :], identity=C["ident_bf"]),
                     reads=[kx, "c_ident_bf"], writes=[("ps", bk)])
            eng = "dve" if g8 == 0 else "act"
            if eng == "dve":
                S.op("dve", lambda g8=g8, psb=psb: nc.vector.tensor_copy(out=xtok[b][:, g8 * 1024:(g8 + 1) * 1024], in_=psb),
                     reads=[("ps", bk)], writes=[kt])
            else:
                S.op("act", lambda g8=g8, psb=psb: nc.scalar.copy(out=xtok[b][:, g8 * 1024:(g8 + 1) * 1024], in_=psb),
                     reads=[("ps", bk)], writes=[kt])
        bk = P.bank()
        psb = P.PS[:, bk, :].bitcast(BF16)
        for i in range(4):
            S.op("pe", lambda i=i, psb=psb: nc.tensor.transpose(out=psb[:, i * 128:(i + 1) * 128], in_=BCT[b][:, i, :], identity=C["ident_bf"]),
                 reads=[kb, "c_ident_bf"], writes=[("ps", bk)])
        S.op("dve", lambda psb=psb: nc.vector.tensor_copy(out=btok[b], in_=psb[:, 0:512]), reads=[("ps", bk)], writes=[kbt])
        S.op("dve", lambda: nc.vector.tensor_tensor(out=e_, in0=dtr, in1=C["dtb"], op=ALU.add), reads=[ks, "c_sm"], writes=[ks])
        S.op("act", lambda: nc.scalar.activation(out=e_, in_=e_, func=AF.Exp), reads=[ks], writes=[ks])
        S.op("act", lambda: nc.scalar.activation(out=dt, in_=e_, func=AF.Ln, bias=1.0), reads=[ks], writes=[ks])
        S.op("dve", lambda: nc.vector.tensor_tensor(out=a, in0=dt, in1=C["A"], op=ALU.mult), reads=[ks, "c_A"], writes=[ks])
        bk = P.bank()
        S.op("pe", lambda bk=bk: nc.tensor.matmul(P.PS[:, bk, 0:32], C["le"], a[:, 0:32], start=True, stop=True),
             reads=[ks, "c_le"], writes=[("ps", bk)])
        S.op("pe", lambda bk=bk: nc.tensor.matmul(P.PS[:, bk, 32:64], C["ge"], a[:, 32:64], start=True, stop=True),
             reads=[ks, "c_ge"], writes=[("ps", bk)])
        S.op("pe", lambda bk=bk: nc.tensor.matmul(P.PS[:, bk, 64:128], C["ones"], a, start=True, stop=True),
             reads=[ks, "c_ones"], writes=[("ps", bk)])
        S.op("dve", lambda bk=bk: nc.vector.tensor_copy(out=acum, in_=P.PS[:, bk, 0:64]), reads=[("ps", bk)], writes=[ks])
        S.op("act", lambda bk=bk: nc.scalar.activation(out=ea, in_=P.PS[:, bk, 0:64], func=AF.Exp), reads=[("ps", bk)], writes=[ks])
        S.op("act", lambda bk=bk: nc.scalar.activation(out=dch, in_=P.PS[:, bk, 64:128], func=AF.Exp), reads=[("ps", bk)], writes=[ks])
        S.op("dve", lambda bk=bk: nc.vector.tensor_tensor(out=dst, in0=P.PS[:, bk, 64:128], in1=acum, op=ALU.subtract),
             reads=[("ps", bk), ks], writes=[ks])
        S.op("act", lambda: nc.scalar.activation(out=dst, in_=dst, func=AF.Exp), reads=[ks], writes=[ks])
        S.op("dve", lambda: nc.vector.tensor_tensor(out=dtd, in0=dt, in1=dst, op=ALU.mult), reads=[ks], writes=[ks])
        return dict(b=b, kx=kx, kb=kb, kt=kt, kbt=kbt, ks=ks, dt=dt, a=a, ea=ea, dch=dch, dtd=dtd)

    def state_update(pp, d, xw_ap, kxw):
        b = pp["b"]
        for g in range(4):
            bk = P.bank()
            S.op("pe", lambda g=g, bk=bk: nc.tensor.matmul(
                P.PS[:, bk, :], btok[b][:, g * 128:(g + 1) * 128], xw_ap[:, g * 512:(g + 1) * 512], start=True, stop=True),
                reads=[pp["kbt"], kxw], writes=[("ps", bk)])
            hv = hst[d][:, g * 512:(g + 1) * 512]
            kh = ("s_h", d, g)
            dcol = pp["dch"][:, d * 32 + g * 8:d * 32 + (g + 1) * 8]
            S.op("dve", lambda hv=hv, dcol=dcol: nc.vector.tensor_tensor(
                out=hv.rearrange("p (e q) -> p e q", e=8), in0=hv.rearrange("p (e q) -> p e q", e=8), in1=_bc(dcol, 64), op=ALU.mult),
                reads=[kh, pp["ks"]], writes=[kh])
            S.op("dve", lambda hv=hv, bk=bk: nc.vector.tensor_tensor(out=hv, in0=hv, in1=P.PS[:, bk, :], op=ALU.add),
                 reads=[kh, ("ps", bk)], writes=[kh])
            S.op("act", lambda hv=hv, g=g, d=d: nc.scalar.copy(out=hbf[d][:, g * 512:(g + 1) * 512], in_=hv),
                 reads=[kh], writes=[("s_hbf", d, g)])

    for d in range(2):
        S.op("dve", lambda d=d: nc.vector.memset(hst[d], 0.0), writes=[("s_h", d, g) for g in range(4)])
        S.op("dve", lambda d=d: nc.vector.memset(hbf[d], 0.0), writes=[("s_hbf", d, g) for g in range(4)])
    def prepB(c):
        pp = prep(c, False)
        b = pp["b"]
        kxw = ("s_xd", b, 2)
        xw = xd[b][:, 2, :]
        S.op("pool", lambda: nc.gpsimd.tensor_tensor(
            out=xw.rearrange("p (h q) -> p h q", h=32), in0=xtok[b].rearrange("p (h q) -> p h q", h=32),
            in1=_bc(pp["dtd"][:, 32:64], 64), op=ALU.mult),
            reads=[pp["kt"], pp["ks"]], writes=[kxw])
        return (pp, xw, kxw)

    nxt = prepB(nck - 1) if nck > 1 else None
    for c in range(nck - 1, -1, -1):
        if c < nq:
            S.op("sp", lambda c=c: nc.sync.dma_start(out=HB[c], in_=hbf[1]),
                 reads=[("s_hbf", 1, g) for g in range(4)], writes=[(jn, "HB", c)], dma=True, slot="s_hbst")
        if c == 0:
            break
        cur = nxt
        nxt = prepB(c - 1) if c - 1 >= 1 else None
        state_update(cur[0], 1, cur[1], cur[2])

    if getattr(P, "p4stop", 9) < 2:
        return
    def chunkA(c):
        pp = prep(c, True)
        b = pp["b"]
        Mb = Mb2[b]
        t0 = c * 128
        ksz, khs = ("s_sz", b), ("s_hbs", b)
        S.op("sp", lambda b=b, t0=t0: nc.sync.dma_start(out=szb[b], in_=sc["SZ"][t0:t0 + 128, :]), writes=[ksz], dma=True, slot=ksz)
        S.op("sp", lambda b=b, c=c: nc.sync.dma_start(out=hbs[b], in_=HB[c]), reads=[(jn, "HB", c)], writes=[khs], dma=True, slot=khs)
        bk = P.bank()
        for g in range(4):
            S.op("pe", lambda g=g, bk=bk: nc.tensor.matmul(P.PS[:, bk, g * 128:(g + 1) * 128], BCT[b][:, g, :], BCT[b][:, 4 + g, :],
                                                          start=True, stop=True), reads=[pp["kb"]], writes=[("ps", bk)])
        kW = ("s_W", b)
        S.op("dve", lambda bk=bk: nc.vector.tensor_tensor(out=Wm[b][:, 0, :], in0=P.PS[:, bk, :], in1=C["le4fb"], op=ALU.mult),
             reads=[("ps", bk), "c_le4fb"], writes=[kW])
        S.op("dve", lambda bk=bk: nc.vector.tensor_tensor(out=Wm[b][:, 1, :], in0=P.PS[:, bk, :], in1=C["ge4fb"], op=ALU.mult),
             reads=[("ps", bk), "c_ge4fb"], writes=[kW])
        x3 = xtok[b].rearrange("p (h q) -> p h q", h=32)
        srcs = [pp["dt"][:, 0:32], pp["dt"][:, 32:64], pp["dtd"][:, 0:32], C["dskip"]]
        for i in range(4):
            eng = "pool" if i % 2 == 0 else "dve"
            eo = nc.gpsimd if eng == "pool" else nc.vector
            S.op(eng, lambda i=i, eo=eo: eo.tensor_tensor(out=xd[b][:, i, :].rearrange("p (h q) -> p h q", h=32), in0=x3,
                                                         in1=_bc(srcs[i], 64), op=ALU.mult),
                 reads=[pp["kt"], pp["ks"], "c_sm"], writes=[("s_xd", b, i)])
        if getattr(P, "p4stop", 9) < 3:
            return
        for d in range(2):
            for g in range(4):
                for hf in range(2):
                    u = d * 8 + g * 2 + hf
                    ti = ctr["ta"] % 3
                    ctr["ta"] += 1
                    h0 = d * 32 + g * 8 + hf * 4
                    S.op("pool", lambda ti=ti, d=d, h0=h0, pp=pp: nc.gpsimd.tensor_tensor(
                        out=Ta[ti].rearrange("p (h l) -> p h l", h=4), in0=C["le4f" if d == 0 else "ge4f"].rearrange("p (h l) -> p h l", h=4),
                        in1=_bc(pp["a"][:, h0:h0 + 4], 128), op=ALU.mult),
                        reads=[pp["ks"], "c_le4f", "c_ge4f"], writes=[("s_Ta", ti)])
                    bk2 = P.bank()
                    S.op("pe", lambda ti=ti, d=d, bk2=bk2: nc.tensor.matmul(P.PS[:, bk2, :], C["gt" if d == 0 else "lt"], Ta[ti], start=True, stop=True),
                         reads=[("s_Ta", ti), "c_gt", "c_lt"], writes=[("ps", bk2)])
                    S.op("act", lambda ti=ti, bk2=bk2: nc.scalar.activation(out=Lb[ti], in_=P.PS[:, bk2, :], func=AF.Exp),
                         reads=[("ps", bk2)], writes=[("s_L", ti)])
                    wv = Wm[b][:, d, g * 128:(g + 1) * 128]
                    S.op("dve", lambda u=u, ti=ti, wv=wv: nc.vector.tensor_tensor(
                        out=Mb[u], in0=Lb[ti].rearrange("p (h l) -> p h l", h=4),
                        in1=wv.rearrange("p (o l) -> p o l", o=1).to_broadcast([128, 4, 128]), op=ALU.mult),
                        reads=[("s_L", ti), kW], writes=[("s_M", b, u)])
        return dict(pp=pp, b=b, t0=t0, ksz=ksz, khs=khs, kW=kW, c=c)

    def chunkB(ctx):
        pp, b, t0, ksz, khs, kW, c = ctx["pp"], ctx["b"], ctx["t0"], ctx["ksz"], ctx["khs"], ctx["kW"], ctx["c"]
        Mb = Mb2[b]
        def groupF(g):
            yk = []
            for d in range(2):
                bk3 = P.bank()
                src = hbf[0] if d == 0 else hbs[b]
                ksrc = ("s_hbf", 0, g) if d == 0 else khs
                S.op("pe", lambda g=g, bk3=bk3, src=src: nc.tensor.matmul(P.PS[:, bk3, :], BCT[b][:, 4 + g, :], src[:, g * 512:(g + 1) * 512],
                                                                       start=True, stop=True), reads=[pp["kb"], ksrc], writes=[("ps", bk3)])
                yi = ctr["yo"] % 4
                ctr["yo"] += 1
                ecol = pp["ea"][:, d * 32 + g * 8:d * 32 + (g + 1) * 8]
                S.op("dve", lambda yi=yi, bk3=bk3, ecol=ecol: nc.vector.tensor_tensor(
                    out=yos[yi].rearrange("p (e q) -> p e q", e=8), in0=P.PS[:, bk3, :].rearrange("p (e q) -> p e q", e=8),
                    in1=_bc(ecol, 64), op=ALU.mult), reads=[("ps", bk3), pp["ks"]], writes=[("s_yos", yi)])
                yk.append(yi)
            if getattr(P, "p4stop", 9) < 5:
                return
            by = P.bank()
            rdY = [("s_yos", yk[0]), ("s_yos", yk[1]), ("s_xd", b, 3), "c_ident_bf"]
            S.op("pe", lambda by=by, yi=yk[0]: nc.tensor.matmul(P.PS[:, by, :], C["ident_bf"], yos[yi], start=True, stop=False),
                 reads=rdY, writes=[("ps", by)])
            S.op("pe", lambda by=by, yi=yk[1]: nc.tensor.matmul(P.PS[:, by, :], C["ident_bf"], yos[yi], start=False, stop=False),
                 reads=rdY, writes=[("ps", by)])
            S.op("pe", lambda by=by, g=g: nc.tensor.matmul(P.PS[:, by, :], C["ident_bf"], xd[b][:, 3, g * 512:(g + 1) * 512], start=False, stop=False),
                 reads=rdY, writes=[("ps", by)])
            for d in range(2):
                for hh in range(8):
                    u = d * 8 + g * 2 + hh // 4
                    h = g * 8 + hh
                    last = (d == 1 and hh == 7)
                    S.op("pe", lambda by=by, u=u, hh=hh, d=d, h=h, last=last: nc.tensor.matmul(
                        P.PS[:, by, hh * 64:(hh + 1) * 64], Mb[u][:, hh % 4, :], xd[b][:, d, h * 64:(h + 1) * 64], start=False, stop=last),
                        reads=[("s_M", b, u), ("s_xd", b, d)], writes=[("ps", by)])
            gi = ctr["g"] % 2
            ctr["g"] += 1
            kyg, kn = ("s_yg", gi), ("s_nsm", gi)
            S.op("dve", lambda gi=gi, by=by, g=g: nc.vector.tensor_tensor(out=yg[gi], in0=P.PS[:, by, :], in1=szb[b][:, g * 512:(g + 1) * 512], op=ALU.mult),
                 reads=[("ps", by), ksz], writes=[kyg])
            if c == 0 and g == 0:
                P.dbg("yg", yg[gi], kyg, 512)
                P.dbg("yos0", yos[yk[0]], ("s_yos", yk[0]), 512)
                P.dbg("yos1", yos[yk[1]], ("s_yos", yk[1]), 512)
                P.dbg("xD", xd[b][:, 3, 0:512], ("s_xd", b, 3), 512)
                P.dbg("xdf", xd[b][:, 0, 0:512], ("s_xd", b, 0), 512)
                P.dbg("M0", Mb[0].rearrange("p h l -> p (h l)"), ("s_M", b, 0), 512)
                P.dbg("M8", Mb[8].rearrange("p h l -> p (h l)"), ("s_M", b, 8), 512)
                P.dbg("Wm", Wm[b].rearrange("p d x -> p (d x)"), kW, 1024)
                P.dbg("sml", sml[b], pp["ks"], 640)
                P.dbg("sz", szb[b][:, 0:512], ksz, 512)
                P.dbg("xtok", xtok[b][:, 0:512], pp["kt"], 512)
                P.dbg("btok", btok[b][:, 0:512], pp["kbt"], 512)
                P.dbg("le4fb", C["le4fb"], "c_le4fb", 512)
                P.dbg("xsT", xsT[b][:, 0, :], pp["kx"], 128)
            S.op("dve", lambda gi=gi: nc.vector.tensor_tensor(out=sq, in0=yg[gi], in1=yg[gi], op=ALU.mult), reads=[kyg], writes=["s_sq"])
            S.op("dve", lambda gi=gi: nc.vector.reduce_sum(out=nsm[gi][:, 0:1], in_=sq, axis=AX.X), reads=["s_sq"], writes=[kn])
            S.op("dve", lambda gi=gi: nc.vector.tensor_scalar(out=nsm[gi][:, 1:2], in0=nsm[gi][:, 0:1], scalar1=1.0 / 512.0, scalar2=EPS,
                                                              op0=ALU.mult, op1=ALU.add), reads=[kn], writes=[kn])
            S.op("pool", lambda gi=gi: nc.gpsimd.tensor_tensor(out=nsm[gi][:, 1:2], in0=nsm[gi][:, 1:2], in1=C["mhalf"], op=ALU.pow),
                 reads=[kn, "c_mh"], writes=[kn])
            S.op("dve", lambda gi=gi, g=g: nc.vector.scalar_tensor_tensor(
                out=hob[gi], in0=yg[gi], scalar=nsm[gi][:, 1:2], in1=vb[:, g * 512:(g + 1) * 512], op0=ALU.mult, op1=ALU.mult),
                reads=[kyg, kn, "s_ng"], writes=[("s_hob", gi)])
            bt = P.bank()
            psb = P.PS[:, bt, :].bitcast(BF16)
            for j in range(4):
                S.op("pe", lambda gi=gi, j=j, psb=psb: nc.tensor.transpose(out=psb[:, j * 128:(j + 1) * 128], in_=hob[gi][:, j * 128:(j + 1) * 128],
                                                                       identity=C["ident_bf"]), reads=[("s_hob", gi), "c_ident_bf"], writes=[("ps", bt)])
            kh = ("s_hts", gi)
            S.op("act", lambda gi=gi, psb=psb: nc.scalar.copy(out=hts[gi], in_=psb[:, 0:512].rearrange("p (j q) -> p j q", j=4)),
                 reads=[("ps", bt)], writes=[kh])
            dsth = sc["HT"][2048 + g * 512:2048 + (g + 1) * 512, t0:t0 + 128].rearrange("(j p) q -> p j q", j=4)
            S.op("sp", lambda gi=gi, dsth=dsth: nc.sync.dma_start(out=dsth, in_=hts[gi]), reads=[kh], writes=[(jn, "HT", 2048 + g * 512, t0)],
                 dma=True, slot=kh)
        if getattr(P, "p4stop", 9) < 6:
            return
        for g in range(4):
            groupF(g)
        if c < nq - 1:
            state_update(pp, 0, xd[b][:, 2, :], ("s_xd", b, 2))

    ctx = chunkA(0)
    for c in range(nq):
        nctx = chunkA(c + 1) if c + 1 < nq else None
        chunkB(ctx)
        ctx = nctx


def _phase5(P, job):
    nc, S, C = P.nc, P.S, P.C
    jn, TK, TQ = job
    sc = P.sc[jn]
    y = P.jy[jn]
    P.phase_reset()
    vb = P.af32(2 * D)
    S.op("sp", lambda: nc.sync.dma_start(out=vb, in_=P.vecs[0, 3 * D:5 * D].partition_broadcast(128)),
         writes=["o_vb"], dma=True, slot="o_vb")
    lng, lnb = vb[:, 0:D], vb[:, D:2 * D]
    HTb2 = [P.abf(36 * 512).rearrange("p (k q) -> p k q", k=36) for _ in range(2)]
    wo = [P.abf(36 * 256) for _ in range(2)]
    rt = [P.af32(D) for _ in range(4)]
    st = [P.af32(24) for _ in range(2)]
    mv = [P.af32(2) for _ in range(2)]
    rs = [P.af32(2) for _ in range(2)]
    nw = [0]

    def load_ht(blk):
        q0 = blk * 512
        hb_ = blk % 2
        srch = sc["HT"][:, q0:q0 + 512].rearrange("(k p) q -> p k q", p=128)
        for k0 in range(0, 36, 9):
            S.op("sp", lambda k0=k0: nc.sync.dma_start(out=HTb2[hb_][:, k0:k0 + 9, :], in_=srch[:, k0:k0 + 9, :]),
                 writes=[("o_HTb", hb_)], dma=True, slot=("o_HTb", hb_, k0))

    def block(blk):
        q0 = blk * 512
        hb_ = blk % 2
        HTb = HTb2[hb_]
        kht = ("o_HTb", hb_)
        if blk + 1 < TQ // 512:
            load_ht(blk + 1)
        for tt in range(4):
            S.op("sp", lambda tt=tt: nc.sync.dma_start(out=rt[tt], in_=sc["XN"][q0 + tt * 128:q0 + (tt + 1) * 128, :]),
                 writes=[("o_r", tt)], dma=True, slot=("o_r", tt))
        for cc in range(8):
            ws = nw[0] % 2
            nw[0] += 1
            kw = ("o_w", ws)
            S.op("pool", lambda ws=ws, cc=cc: nc.gpsimd.dma_start(out=wo[ws], in_=P.WOB[cc]),
                 reads=[("WOB", cc)], writes=[kw], dma=True, slot=kw)
            for tt in range(4):
                bk = P.bank()
                for kc in range(36):
                    S.op("pe", lambda ws=ws, kc=kc, tt=tt, bk=bk: nc.tensor.matmul(
                        P.PS[:, bk, 0:256], HTb[:, kc, tt * 128:(tt + 1) * 128], wo[ws][:, kc * 256:(kc + 1) * 256],
                        start=(kc == 0), stop=(kc == 35)), reads=[kw, kht], writes=[("ps", bk)])
                rv = rt[tt][:, cc * 256:(cc + 1) * 256]
                S.op("dve", lambda rv=rv, bk=bk: nc.vector.scalar_tensor_tensor(
                    out=rv, in0=rv, scalar=ALPHA, in1=P.PS[:, bk, 0:256], op0=ALU.mult, op1=ALU.add),
                    reads=[("ps", bk), ("o_r", tt)], writes=[("o_r", tt)])
        for tt in range(4):
            b = tt % 2
            kr = ("o_r", tt)
            for q in range(4):
                S.op("dve", lambda b=b, q=q, tt=tt: nc.vector.bn_stats(out=st[b][:, q * 6:(q + 1) * 6], in_=rt[tt][:, q * 512:(q + 1) * 512]),
                     reads=[kr], writes=[("o_st", b)])
            S.op("dve", lambda b=b: nc.vector.bn_aggr(out=mv[b], in_=st[b]), reads=[("o_st", b)], writes=[("o_mv", b)])
            _ln_rstd(P, mv[b], rs[b][:, 0:1], rs[b][:, 1:2], [("o_mv", b)], "o_rs%d" % b)
            rk = ["o_rs%d_rstd" % b, "o_rs%d_nmr" % b]
            S.op("act", lambda b=b, tt=tt: nc.scalar.activation(out=rt[tt], in_=rt[tt], func=AF.Identity, scale=rs[b][:, 0:1], bias=rs[b][:, 1:2]),
                 reads=[kr] + rk, writes=[kr])
            S.op("pool", lambda tt=tt: nc.gpsimd.tensor_tensor(out=rt[tt], in0=rt[tt], in1=lng, op=ALU.mult), reads=[kr, "o_vb"], writes=[kr])
            S.op("dve", lambda tt=tt: nc.vector.tensor_tensor(out=rt[tt], in0=rt[tt], in1=lnb, op=ALU.add), reads=[kr, "o_vb"], writes=[kr])
            S.op("sp", lambda tt=tt: nc.sync.dma_start(out=y[q0 + tt * 128:q0 + (tt + 1) * 128, :], in_=rt[tt]),
                 reads=[kr], writes=[(jn, "y", q0, tt)], dma=True, slot=("o_yst", tt))

    load_ht(0)
    for blk in range(TQ // 512):
        block(blk)


def build_program(jobs, debug=()):
    P = Prog(jobs, debug=debug)
    P.sc = {}
    _p_setup(P)
    _p_consts(P)
    for j in jobs:
        _phase1(P, j)
        _phase2(P, j)
        _phase3(P, j)
        _phase4(P, j)
        _phase5(P, j)
    P.S.emit()
    return P


_CACHE = {}


def kernel(**inputs):
    xp = np.asarray(inputs["x_prompt"], np.float32)
    xs = np.asarray(inputs["x_sample"], np.float32)
    mp = np.asarray(inputs["mem_prompt"], np.float32)
    ms = np.asarray(inputs["mem_sample"], np.float32)
    B, T, _ = xp.shape
    SB, TS, _ = xs.shape
    ncores = 8
    assert 2 * B == ncores and SB == ncores
    TQ = T // 2
    jobs = [("p", T, TQ), ("s", TS, TS)]
    key = (T, TS)
    if key not in _CACHE:
        _CACHE[key] = build_program(jobs)
    P = _CACHE[key]
    shared = [prep_shared(inputs, False), prep_shared(inputs, True)]
    maps = []
    for c in range(ncores):
        flip = c % 2 == 1
        m = dict(shared[1 if flip else 0])
        a, b = xp[c // 2], xs[c]
        if flip:
            a, b = a[::-1], b[::-1]
        m["x_p"] = np.ascontiguousarray(a)
        m["x_s"] = np.ascontiguousarray(b)
        m["mem_p"] = np.ascontiguousarray(mp[c // 2])
        m["mem_s"] = np.ascontiguousarray(ms[c])
        maps.append(m)
    res = run_bass_kernel_spmd(P.nc, maps, core_ids=list(range(ncores)))
    yp = np.empty((B, T, D), np.float32)
    ys = np.empty((SB, TS, D), np.float32)
    for c in range(ncores):
        r = res.results[c]
        a = np.asarray(r["y_p"], np.float32)
        b = np.asarray(r["y_s"], np.float32)
        if c % 2 == 0:
            yp[c // 2, :TQ] = a
            ys[c] = b
        else:
            yp[c // 2, TQ:] = a[::-1]
            ys[c] = b[::-1]
    return (yp, ys)
```

```python
from contextlib import ExitStack
import math
import numpy as np
import concourse.bass as bass
import concourse.mybir as mybir
from concourse.bass_utils import run_bass_kernel_spmd

F32 = mybir.dt.float32
BF16 = mybir.dt.bfloat16
I32 = mybir.dt.int32
AF = mybir.ActivationFunctionType
ALU = mybir.AluOpType
AX = mybir.AxisListType

D = 2048
NKC = 16
D_XBC = 3072
D_MIX = 4608
NMEM = 256
EPS = 1e-5
ALPHA = 2.0 ** 0.25
LAM_INIT = 0.2
ATT_SCALE = 1.0 / math.sqrt(128.0)
SLOPES = [2.0 ** (-(h + 1)) for h in range(8)]
ALIBI_CUT = 64.0

EPOCH = 16000


class _Op:
    __slots__ = ("eng", "fn", "dma", "slot", "idx", "waits", "signal", "seq", "slot_cnt", "name")


class Sched:
    ENGS = ("pe", "act", "dve", "pool", "sp")

    def __init__(self, nc, same_engine_sync=True):
        self.nc = nc
        self.ops = {e: [] for e in self.ENGS}
        self.last_w = {}
        self.readers = {}
        self.seen = {e: {} for e in self.ENGS}
        self.slot_cnt = {}
        self.slot_last = {}
        self.same = same_engine_sync
        self.barrier_deps = []
        self.nops = 0

    def _need(self, o, d):
        if d.dma:
            key = ("s", d.slot)
            val = d.slot_cnt
        else:
            if d.eng == o.eng and (d.eng == "pe" or not self.same):
                return False
            key = ("c", d.eng)
            val = d.idx
        if self.seen[o.eng].get(key, 0) >= val:
            return False
        self.seen[o.eng][key] = val
        return True

    def op(self, eng, fn, reads=(), writes=(), dma=False, slot=None, name=None):
        o = _Op()
        o.eng = eng; o.fn = fn; o.dma = dma; o.slot = slot; o.signal = False; o.seq = None
        o.name = name; o.slot_cnt = 0
        lst = self.ops[eng]
        o.idx = len(lst) + 1
        deps = []
        for r in reads:
            deps.append(self.last_w.get(r))
        for w in writes:
            deps.append(self.last_w.get(w))
            rd = self.readers.get(w)
            if rd:
                deps.extend(rd.values())
        deps.extend(self.barrier_deps)
        if getattr(self, "serial", False) and getattr(self, "prev_op", None) is not None:
            deps.append(self.prev_op)
        self.prev_op = o
        if dma:
            assert slot is not None
            deps.append(self.slot_last.get(slot))
            c = self.slot_cnt.get(slot, 0) + 1
            self.slot_cnt[slot] = c
            o.slot_cnt = c
            self.slot_last[slot] = o
        deps = [d for d in deps if d is not None and d is not o]
        deps.sort(key=lambda d: -(d.slot_cnt if d.dma else d.idx))
        waits = []
        for d in deps:
            if self._need(o, d):
                d.signal = True
                waits.append(d)
        o.waits = waits
        lst.append(o)
        for r in reads:
            self.readers.setdefault(r, {})[("d", slot) if dma else eng] = o
        for w in writes:
            self.last_w[w] = o
            self.readers[w] = {}
        self.nops += 1
        return o

    def barrier(self):
        deps = []
        for e in self.ENGS:
            for o in reversed(self.ops[e]):
                if not o.dma:
                    deps.append(o)
                    break
        deps.extend(self.slot_last.values())
        self.barrier_deps = deps

    def emit(self):
        nc = self.nc
        self.barrier()
        self.op("sp", None, name="final")
        with ExitStack() as es:
            semtab = {}

            def getsem(key):
                if key not in semtab:
                    semtab[key] = es.enter_context(nc.semaphore("s%d" % len(semtab)))
                return semtab[key]

            for e in self.ENGS:
                n = 0
                for o in self.ops[e]:
                    if not o.dma and o.signal:
                        o.seq = n
                        n += 1
            engobj = {"pe": nc.tensor, "act": nc.scalar, "dve": nc.vector, "pool": nc.gpsimd, "sp": nc.sync}
            DE = EPOCH // 16
            for e in self.ENGS:
                for o in self.ops[e]:
                    if o.dma:
                        getsem(("s", o.slot, (o.slot_cnt - 1) // DE))
                    elif o.signal:
                        getsem(("c", e, o.seq // EPOCH))

            def run(e):
                eo = engobj[e]
                for o in self.ops[e]:
                    for d in o.waits:
                        if d.dma:
                            k = d.slot_cnt - 1
                            eo.wait_ge(getsem(("s", d.slot, k // DE)), 16 * (k % DE + 1))
                        else:
                            eo.wait_ge(getsem(("c", d.eng, d.seq // EPOCH)), d.seq % EPOCH + 1)
                    if o.fn is None:
                        continue
                    ins = o.fn()
                    if o.dma:
                        k = o.slot_cnt - 1
                        ins.then_inc(getsem(("s", o.slot, k // DE)), 16)
                    elif o.signal:
                        ins.then_inc(getsem(("c", e, o.seq // EPOCH)), 1)

            with nc.Block() as block:
                @block.tensor
                def _(x):
                    run("pe")

                @block.scalar
                def _(x):
                    run("act")

                @block.vector
                def _(x):
                    run("dve")

                @block.gpsimd
                def _(x):
                    run("pool")

                @block.sync
                def _(x):
                    run("sp")
            self.nsem = len(semtab)


class Prog:
    def __init__(self, jobs, debug=()):
        self.jobs = jobs
        self.debug = set(debug)
        self.nc = nc = bass.Bass("TRN2", target_bir_lowering=False)
        self.S = Sched(nc)
        self.es = ExitStack()
        self.dram = {}
        self.psum_rr = 0
        self.uid = 0

    def din(self, name, shape, dt=F32):
        t = self.nc.dram_tensor(name, list(shape), dt, kind="ExternalInput").ap()
        self.dram[name] = t
        return t

    def dout(self, name, shape, dt=F32):
        t = self.nc.dram_tensor(name, list(shape), dt, kind="ExternalOutput").ap()
        self.dram[name] = t
        return t

    def dscr(self, name, shape, dt=BF16):
        kind = "ExternalOutput" if name in self.debug else "Internal"
        t = self.nc.dram_tensor(name, list(shape), dt, kind=kind).ap()
        self.dram[name] = t
        return t

    def arena_init(self, words):
        self.A = self.es.enter_context(self.nc.sbuf_tensor("arena", [128, words], F32))
        self.AW = words
        self.aoff = 0
        self.perm_off = 0

    def af32(self, n, perm=False):
        off = self.aoff
        self.aoff += n
        assert self.aoff <= self.AW, ("arena overflow", self.aoff, self.AW)
        if perm:
            self.perm_off = self.aoff
        return self.A[:, off:off + n]

    def abf(self, n, perm=False):
        w = (n + 1) // 2
        v = self.af32(w, perm)
        return v.bitcast(BF16)[:, 0:n]

    def phase_reset(self):
        self.S.barrier()
        self.aoff = self.perm_off

    def dbg(self, name, ap, key, n):
        if name not in self.debug:
            return
        o = self.nc.dram_tensor("dbg_" + name, [128, n], ap.dtype, kind="ExternalOutput").ap()
        self.S.op("sp", lambda: self.nc.sync.dma_start(out=o[:, :], in_=ap), reads=[key], dma=True, slot="dbg_" + name)

    def bank(self):
        b = self.psum_rr
        self.psum_rr = (self.psum_rr + 1) % 8
        return b

    def u(self):
        self.uid += 1
        return self.uid


FM_GROUPS = [("QT", 0, 16), ("KT", 16, 16), ("XBC", 32, 24), ("QM", 56, 4)]
TM_GROUPS = [("V", 0, 4, False), ("SG", 4, 4, True), ("SZ", 8, 4, True), ("SGM", 12, 1, True)]


def _p_setup(P):
    nc = P.nc
    P.w_fm = P.din("w_fm", [60, 128, NKC * 128])
    P.w_tm = P.din("w_tm", [13, 128, NKC * 512])
    P.w_dt = P.din("w_dt", [128, NKC * 64])
    P.w_mk = P.din("w_mk", [4, 128, NKC * 128])
    P.w_mv = P.din("w_mv", [128, NKC * 512])
    P.w_o = P.din("w_o", [8, 128, 36 * 256])
    P.vecs = P.din("vecs", [1, 5 * D])
    P.smalls = P.din("smalls", [1, 1024])
    P.convp = P.din("convp", [128, 24 * 6])
    P.jx = {}
    P.jmem = {}
    P.jy = {}
    for (jn, TK, TQ) in P.jobs:
        P.jx[jn] = P.din("x_" + jn, [TK, D])
        P.jmem[jn] = P.din("mem_" + jn, [NMEM, D])
        P.jy[jn] = P.dout("y_" + jn, [TQ, D])
    P.arena_init(51200)
    P.PS = P.es.enter_context(nc.psum_tensor("ps", [128, 8, 512], F32))


def _p_consts(P):
    nc, S = P.nc, P.S
    C = P.C = {}
    jmp = P.af32(128, perm=True)
    S.op("pool", lambda: nc.gpsimd.iota(jmp, pattern=[[1, 128]], base=0, channel_multiplier=-1,
                                        allow_small_or_imprecise_dtypes=True), writes=["c_jmp"])

    def mk(key, op, dt=F32):
        t = P.af32(128, perm=True) if dt == F32 else P.abf(128, perm=True)
        S.op("dve", lambda: nc.vector.tensor_single_scalar(out=t, in_=jmp, scalar=0.0, op=op),
             reads=["c_jmp"], writes=["c_" + key])
        C[key] = t
        return t
    mk("ident", ALU.is_equal)
    mk("ident_bf", ALU.is_equal, BF16)
    mk("le", ALU.is_ge)
    mk("ge", ALU.is_le)
    mk("lt", ALU.is_gt)
    mk("gt", ALU.is_lt)
    for key, srck in (("le4f", "le"), ("ge4f", "ge")):
        t4 = P.af32(512, perm=True)
        t4b = P.abf(512, perm=True)
        for i in range(4):
            S.op("dve", lambda t4=t4, i=i, srck=srck: nc.vector.tensor_copy(out=t4[:, i * 128:(i + 1) * 128], in_=C[srck]),
                 reads=["c_" + srck], writes=["c_" + key])
            S.op("dve", lambda t4b=t4b, i=i, srck=srck: nc.vector.tensor_copy(out=t4b[:, i * 128:(i + 1) * 128], in_=C[srck]),
                 reads=["c_" + srck], writes=["c_" + key + "b"])
        C[key] = t4
        C[key + "b"] = t4b
    ones = P.af32(128, perm=True)
    S.op("dve", lambda: nc.vector.memset(ones, 1.0), writes=["c_ones"])
    C["ones"] = ones
    dt_ = P.af32(4 * 512, perm=True)
    dt3 = dt_.rearrange("p (a b) -> p a b", a=4)
    S.op("pool", lambda: nc.gpsimd.iota(dt_, pattern=[[-128, 4], [1, 512]], base=0, channel_multiplier=-1,
                                        allow_small_or_imprecise_dtypes=True), writes=["c_dtile"])
    S.op("dve", lambda: nc.vector.scalar_tensor_tensor(out=dt_, in0=dt_, scalar=-1.0, in1=dt_, op0=ALU.mult, op1=ALU.min),
         reads=["c_dtile"], writes=["c_dtile2", "c_dtile"])
    C["dtile"] = dt3
    IL = P.af32(64, perm=True)
    IR = P.af32(64, perm=True)
    IQ2 = P.af32(4, perm=True)
    S.op("pool", lambda: nc.gpsimd.iota(IL, pattern=[[-128, 64]], base=0, channel_multiplier=1,
                                        allow_small_or_imprecise_dtypes=True), writes=["c_IL"])
    S.op("pool", lambda: nc.gpsimd.iota(IR, pattern=[[128, 64]], base=0, channel_multiplier=1,
                                        allow_small_or_imprecise_dtypes=True), writes=["c_IR"])
    S.op("pool", lambda: nc.gpsimd.iota(IQ2, pattern=[[-128, 4]], base=512, channel_multiplier=-1,
                                        allow_small_or_imprecise_dtypes=True), writes=["c_IQ2"])
    bL = P.af32(8 * 64, perm=True).rearrange("p (h j) -> p h j", h=8)
    bR = P.af32(8 * 64, perm=True).rearrange("p (h j) -> p h j", h=8)
    fL = P.af32(8 * 4, perm=True).rearrange("p (h j) -> p h j", h=8)
    fR = P.af32(8 * 4, perm=True).rearrange("p (h j) -> p h j", h=8)
    for h in range(8):
        sl = SLOPES[h]
        S.op("dve", lambda h=h, sl=sl: nc.vector.tensor_single_scalar(out=bL[:, h, :], in_=IL, scalar=sl, op=ALU.mult),
             reads=["c_IL"], writes=["c_bL"])
        S.op("dve", lambda h=h, sl=sl: nc.vector.tensor_single_scalar(out=bR[:, h, :], in_=IR, scalar=-sl, op=ALU.mult),
             reads=["c_IR"], writes=["c_bR"])
        S.op("act", lambda h=h, sl=sl: nc.scalar.activation(out=fL[:, h, :], in_=IR[:, 0:4], func=AF.Exp, scale=-sl),
             reads=["c_IR"], writes=["c_fL"])
        S.op("act", lambda h=h, sl=sl: nc.scalar.activation(out=fR[:, h, :], in_=IQ2, func=AF.Exp, scale=-sl),
             reads=["c_IQ2"], writes=["c_fR"])
    C["bL"], C["bR"], C["fL"], C["fR"] = bL, bR, fL, fR
    sm = P.af32(1024, perm=True)
    S.op("sp", lambda: nc.sync.dma_start(out=sm, in_=P.smalls[0, :].partition_broadcast(128)),
         writes=["c_sm"], dma=True, slot="c_sm")
    C["dl"] = sm[:, 0:512]
    C["subln"] = sm[:, 512:768]
    C["dtb"] = sm[:, 768:832]
    C["alog"] = sm[:, 832:896]
    C["dskip"] = sm[:, 896:928]
    junk = P.af32(128)
    s12 = P.af32(4, perm=True)
    S.op("dve", lambda: nc.vector.tensor_tensor(out=junk, in0=sm[:, 0:128], in1=sm[:, 128:256], op=ALU.mult),
         reads=["c_sm"], writes=["junk0"])
    S.op("dve", lambda: nc.vector.reduce_sum(out=s12[:, 0:1], in_=junk, axis=AX.X), reads=["junk0"], writes=["c_s1"])
    S.op("dve", lambda: nc.vector.tensor_tensor(out=junk, in0=sm[:, 256:384], in1=sm[:, 384:512], op=ALU.mult),
         reads=["c_sm", "c_s1"], writes=["junk0"])
    S.op("dve", lambda: nc.vector.reduce_sum(out=s12[:, 1:2], in_=junk, axis=AX.X), reads=["junk0"], writes=["c_s2"])
    S.op("act", lambda: nc.scalar.activation(out=s12[:, 2:4], in_=s12[:, 0:2], func=AF.Exp),
         reads=["c_s1", "c_s2"], writes=["c_e12"])
    nlam = P.af32(1, perm=True)
    S.op("dve", lambda: nc.vector.tensor_tensor(out=nlam, in0=s12[:, 3:4], in1=s12[:, 2:3], op=ALU.subtract),
         reads=["c_e12"], writes=["c_nlam"])
    S.op("dve", lambda: nc.vector.tensor_single_scalar(out=nlam, in_=nlam, scalar=-LAM_INIT, op=ALU.add),
         reads=["c_nlam"], writes=["c_nlam"])
    C["nlam"] = nlam
    S.op("dve", lambda: nc.vector.tensor_single_scalar(out=C["subln"], in_=C["subln"], scalar=1.0 - LAM_INIT, op=ALU.mult),
         reads=["c_sm", "c_s1", "c_s2"], writes=["c_subln"])
    Abc = P.af32(64, perm=True)
    S.op("act", lambda: nc.scalar.activation(out=Abc, in_=C["alog"], func=AF.Exp), reads=["c_sm"], writes=["c_A"])
    S.op("dve", lambda: nc.vector.tensor_single_scalar(out=Abc, in_=Abc, scalar=-1.0, op=ALU.mult), reads=["c_A"], writes=["c_A"])
    C["A"] = Abc
    mh = P.af32(1, perm=True)
    S.op("dve", lambda: nc.vector.memset(mh, -0.5), writes=["c_mh"])
    C["mhalf"] = mh
    cp = P.af32(24 * 6, perm=True)
    S.op("sp", lambda: nc.sync.dma_start(out=cp, in_=P.convp[:, :]), writes=["c_convp"], dma=True, slot="c_convp")
    C["convp"] = cp.rearrange("p (f k) -> p f k", f=24)
    P.WOB = P.dscr("WOB", [8, 128, 36 * 256])
    P.aoff = P.perm_off
    wtmp = [P.abf(36 * 256) for _ in range(2)]
    for cc in range(8):
        ws = cc % 2
        kw = ("c_wtmp", ws)
        S.op("pool", lambda ws=ws, cc=cc: nc.gpsimd.dma_start(out=wtmp[ws], in_=P.w_o[cc], max_dma_last_dim=8192),
             writes=[kw], dma=True, slot=kw)
        S.op("sp", lambda ws=ws, cc=cc: nc.sync.dma_start(out=P.WOB[cc], in_=wtmp[ws]), reads=[kw], writes=[("WOB", cc)],
             dma=True, slot=("c_wst", ws))
    P.aoff = P.perm_off


def _ln_rstd(P, mv, rstd, nmr, rk, wk):
    nc, S = P.nc, P.S
    S.op("dve", lambda: nc.vector.tensor_single_scalar(out=rstd, in_=mv[:, 1:2], scalar=EPS, op=ALU.add),
         reads=rk, writes=[wk + "_rstd"])
    S.op("pool", lambda: nc.gpsimd.tensor_tensor(out=rstd, in0=rstd, in1=P.C["mhalf"], op=ALU.pow),
         reads=[wk + "_rstd", "c_mh"], writes=[wk + "_rstd"])
    S.op("dve", lambda: nc.vector.tensor_scalar(out=nmr, in0=mv[:, 0:1], scalar1=rstd, scalar2=-1.0, op0=ALU.mult, op1=ALU.mult),
         reads=rk + [wk + "_rstd"], writes=[wk + "_nmr"])


def _phase1(P, job):
    nc, S, C = P.nc, P.S, P.C
    jn, TK, TQ = job
    NB = min(1024, TQ)
    nblk = TK // NB
    ntt = NB // 128
    nsb = NB // 512
    x = P.jx[jn]
    sc = {}
    sc["QT"] = P.dscr(jn + "_QT", [D, TQ]); sc["KT"] = P.dscr(jn + "_KT", [D, TK])
    sc["V"] = P.dscr(jn + "_V", [TK, D]); sc["SG"] = P.dscr(jn + "_SG", [TQ, D]); sc["SZ"] = P.dscr(jn + "_SZ", [TQ, D])
    sc["XBC"] = P.dscr(jn + "_XBC", [D_XBC, TK]); sc["DTR"] = P.dscr(jn + "_DTR", [TK, 64], F32)
    sc["QM"] = P.dscr(jn + "_QM", [512, TQ]); sc["SGM"] = P.dscr(jn + "_SGM", [TQ, 512])
    sc["XN"] = P.dscr(jn + "_XN", [TQ, D], F32)
    sc["HT"] = P.dscr(jn + "_HT", [D_MIX, TQ])
    P.sc[jn] = sc

    P.phase_reset()
    vb = P.af32(2 * D)
    S.op("sp", lambda: nc.sync.dma_start(out=vb, in_=P.vecs[0, 0:2 * D].partition_broadcast(128)),
         writes=["p1_vb"], dma=True, slot="p1_vb")
    gin, bin_ = vb[:, 0:D], vb[:, D:2 * D]
    xnT2 = [P.abf(NKC * NB).rearrange("p (k t) -> p k t", k=NKC) for _ in range(2)]
    xin = [P.af32(D) for _ in range(2)]
    xc = [P.af32(D) for _ in range(2)]
    wfm = [P.abf(NKC * 128) for _ in range(3)]
    wtm = [P.abf(NKC * 512) for _ in range(2)]
    wdt = P.abf(NKC * 64)
    ost = [P.abf(512) for _ in range(4)]
    odt = [P.af32(64) for _ in range(2)]
    st = [P.af32(24) for _ in range(2)]
    mv = [P.af32(2) for _ in range(2)]
    rs = [P.af32(2) for _ in range(2)]
    S.op("pool", lambda: nc.gpsimd.dma_start(out=wdt, in_=P.w_dt[:, :]), writes=["p1_wdt"], dma=True, slot="p1_wdt")
    cnt = {"fm": 0, "tm": 0, "ost": 0, "ev": 0, "odt": 0, "ln": 0}

    def ln_tile(blk, tt):
        xb = blk % 2
        xnT = xnT2[xb]
        own = blk * NB < TQ
        t0 = blk * NB + tt * 128
        b = cnt["ln"] % 2
        cnt["ln"] += 1
        kx, kc_ = ("p1_xin", b), ("p1_xc", b)
        S.op("sp", lambda: nc.sync.dma_start(out=xin[b], in_=x[t0:t0 + 128, :]), writes=[kx], dma=True, slot=kx)
        for q in range(4):
            S.op("dve", lambda q=q: nc.vector.bn_stats(out=st[b][:, q * 6:(q + 1) * 6], in_=xin[b][:, q * 512:(q + 1) * 512]),
                 reads=[kx], writes=[("p1_st", b)])
        S.op("dve", lambda: nc.vector.bn_aggr(out=mv[b], in_=st[b]), reads=[("p1_st", b)], writes=[("p1_mv", b)])
        _ln_rstd(P, mv[b], rs[b][:, 0:1], rs[b][:, 1:2], [("p1_mv", b)], "p1_rs%d" % b)
        rk = ["p1_rs%d_rstd" % b, "p1_rs%d_nmr" % b]
        S.op("act", lambda: nc.scalar.activation(out=xc[b], in_=xin[b], func=AF.Identity, scale=rs[b][:, 0:1], bias=rs[b][:, 1:2]),
             reads=[kx] + rk, writes=[kc_])
        S.op("dve", lambda: nc.vector.tensor_tensor(out=xc[b], in0=xc[b], in1=gin, op=ALU.mult), reads=[kc_, "p1_vb"], writes=[kc_])
        S.op("dve", lambda: nc.vector.tensor_tensor(out=xc[b], in0=xc[b], in1=bin_, op=ALU.add), reads=[kc_, "p1_vb"], writes=[kc_])
        if own:
            S.op("sp", lambda: nc.sync.dma_start(out=sc["XN"][t0:t0 + 128, :], in_=xc[b]),
                 reads=[kc_], writes=[(jn, "XN", t0)], dma=True, slot=("p1_xnst", b))
        for g in range(4):
            bk = P.bank()
            for i in range(4):
                kc = g * 4 + i
                S.op("pe", lambda kc=kc, bk=bk, i=i: nc.tensor.transpose(out=P.PS[:, bk, i * 128:(i + 1) * 128],
                                                                      in_=xc[b][:, kc * 128:(kc + 1) * 128], identity=C["ident"]),
                     reads=[kc_, "c_ident"], writes=[("ps", bk)])
            src = P.PS[:, bk, :].rearrange("p (a b) -> p a b", a=4)
            dst = xnT[:, g * 4:(g + 1) * 4, tt * 128:(tt + 1) * 128]
            if g % 2 == 0:
                S.op("dve", lambda src=src, dst=dst: nc.vector.tensor_copy(out=dst, in_=src),
                     reads=[("ps", bk)], writes=[("p1_xnT", xb, tt)])
            else:
                S.op("act", lambda src=src, dst=dst: nc.scalar.copy(out=dst, in_=src),
                     reads=[("ps", bk)], writes=[("p1_xnT", xb, tt)])

    def proj_items(blk):
        xb = blk % 2
        xnT = xnT2[xb]
        own = blk * NB < TQ
        xkeys = [("p1_xnT", xb, tt) for tt in range(ntt)]
        items = []

        def fm_item(dst, wi, j):
            ws = cnt["fm"] % 3
            cnt["fm"] += 1
            kw = ("p1_wfm", ws)
            S.op("pool", lambda: nc.gpsimd.dma_start(out=wfm[ws], in_=P.w_fm[wi]), writes=[kw], dma=True, slot=kw)
            for sb in range(nsb):
                bk = P.bank()
                for kc in range(NKC):
                    S.op("pe", lambda kc=kc, sb=sb, bk=bk: nc.tensor.matmul(
                        P.PS[:, bk, :], wfm[ws][:, kc * 128:(kc + 1) * 128], xnT[:, kc, sb * 512:(sb + 1) * 512],
                        start=(kc == 0), stop=(kc == NKC - 1)),
                        reads=[kw] + xkeys[sb * 4:(sb + 1) * 4], writes=[("ps", bk)])
                os_ = cnt["ost"] % 4
                cnt["ost"] += 1
                ko = ("p1_ost", os_)
                cnt["ev"] += 1
                if cnt["ev"] % 2 == 0:
                    S.op("dve", lambda os_=os_, bk=bk: nc.vector.tensor_copy(out=ost[os_], in_=P.PS[:, bk, :]),
                         reads=[("ps", bk)], writes=[ko])
                else:
                    S.op("act", lambda os_=os_, bk=bk: nc.scalar.copy(out=ost[os_], in_=P.PS[:, bk, :]),
                         reads=[("ps", bk)], writes=[ko])
                c0 = blk * NB + sb * 512
                S.op("sp", lambda os_=os_, c0=c0: nc.sync.dma_start(
                    out=sc[dst][j * 128:(j + 1) * 128, c0:c0 + 512], in_=ost[os_]),
                    reads=[ko], writes=[(jn, dst, j, c0)], dma=True, slot=ko)

        def tm_item(dst, wi, j, silu):
            ws = cnt["tm"] % 2
            cnt["tm"] += 1
            kw = ("p1_wtm", ws)
            S.op("pool", lambda: nc.gpsimd.dma_start(out=wtm[ws], in_=P.w_tm[wi], max_dma_last_dim=8192),
                 writes=[kw], dma=True, slot=kw)
            for tt in range(ntt):
                bk = P.bank()
                for kc in range(NKC):
                    S.op("pe", lambda kc=kc, tt=tt, bk=bk: nc.tensor.matmul(
                        P.PS[:, bk, :], xnT[:, kc, tt * 128:(tt + 1) * 128], wtm[ws][:, kc * 512:(kc + 1) * 512],
                        start=(kc == 0), stop=(kc == NKC - 1)),
                        reads=[kw, xkeys[tt]], writes=[("ps", bk)])
                os_ = cnt["ost"] % 4
                cnt["ost"] += 1
                ko = ("p1_ost", os_)
                if silu:
                    S.op("act", lambda os_=os_, bk=bk: nc.scalar.activation(out=ost[os_], in_=P.PS[:, bk, :], func=AF.Silu),
                         reads=[("ps", bk)], writes=[ko])
                else:
                    S.op("dve", lambda os_=os_, bk=bk: nc.vector.tensor_copy(out=ost[os_], in_=P.PS[:, bk, :]),
                         reads=[("ps", bk)], writes=[ko])
                t0 = blk * NB + tt * 128
                S.op("sp", lambda os_=os_, t0=t0: nc.sync.dma_start(
                    out=sc[dst][t0:t0 + 128, j * 512:(j + 1) * 512], in_=ost[os_]),
                    reads=[ko], writes=[(jn, dst, t0, j)], dma=True, slot=ko)

        def dt_item():
            for tt in range(ntt):
                bk = P.bank()
                for kc in range(NKC):
                    S.op("pe", lambda kc=kc, tt=tt, bk=bk: nc.tensor.matmul(
                        P.PS[:, bk, 0:64], xnT[:, kc, tt * 128:(tt + 1) * 128], wdt[:, kc * 64:(kc + 1) * 64],
                        start=(kc == 0), stop=(kc == NKC - 1)),
                        reads=["p1_wdt", xkeys[tt]], writes=[("ps", bk)])
                os_ = cnt["odt"] % 2
                cnt["odt"] += 1
                ko = ("p1_odt", os_)
                S.op("dve", lambda os_=os_, bk=bk: nc.vector.tensor_copy(out=odt[os_], in_=P.PS[:, bk, 0:64]),
                     reads=[("ps", bk)], writes=[ko])
                t0 = blk * NB + tt * 128
                S.op("sp", lambda os_=os_, t0=t0: nc.sync.dma_start(out=sc["DTR"][t0:t0 + 128, :], in_=odt[os_]),
                     reads=[ko], writes=[(jn, "DTR", t0)], dma=True, slot=ko)

        for (dst, w0, nw) in FM_GROUPS:
            if not own and dst in ("QT", "QM"):
                continue
            for j in range(nw):
                items.append(lambda dst=dst, wi=w0 + j, j=j: fm_item(dst, wi, j))
        for (dst, c0w, ncw, silu) in TM_GROUPS:
            if not own and dst != "V":
                continue
            for j in range(ncw):
                items.append(lambda dst=dst, wi=c0w + j, j=j, silu=silu: tm_item(dst, wi, j, silu))
        items.append(dt_item)
        return items

    for tt in range(ntt):
        ln_tile(0, tt)
    for blk in range(nblk):
        items = proj_items(blk)
        pending = [(blk + 1, tt) for tt in range(ntt)] if blk + 1 < nblk else []
        stride = max(1, (len(items) - 2) // max(1, len(pending))) if pending else 0
        for i, it in enumerate(items):
            it()
            if pending and i % stride == stride - 1:
                ln_tile(*pending.pop(0))
        while pending:
            ln_tile(*pending.pop(0))


IN_SIZES = (2048, 2048, 2048, 2048, 2048, 3072, 64, 512, 512)
_OFF = np.concatenate([[0], np.cumsum(IN_SIZES)]).tolist()
O_Q, O_K, O_V, O_G, O_Z, O_XBC, O_DT, O_QM, O_GM = _OFF[:9]


def _tile_w(w, cols, width):
    K = w.shape[0]
    sub = w[:, cols]
    n = sub.shape[1]
    t = sub.reshape(K // 128, 128, n // width, width)
    t = np.transpose(t, (2, 1, 0, 3))
    return np.ascontiguousarray(t.reshape(n // width, 128, (K // 128) * width), dtype=np.float32)


def prep_shared(inp, flip):
    w_in = np.asarray(inp["w_in"][0], np.float32)
    ar = np.arange
    fm_cols = np.concatenate([ar(O_Q, O_Q + 2048), ar(O_K, O_K + 2048), ar(O_XBC, O_XBC + 3072), ar(O_QM, O_QM + 512)])
    tm_cols = np.concatenate([ar(O_V, O_V + 2048), ar(O_G, O_G + 2048), ar(O_Z, O_Z + 2048), ar(O_GM, O_GM + 512)])
    dt_cols = ar(O_DT, O_DT + 64)
    conv_w = np.asarray(inp["conv_w"][0], np.float32)
    dt_bias = np.asarray(inp["dt_bias"][0], np.float32)
    a_log = np.asarray(inp["a_log"][0], np.float32)
    if flip:
        dt_cols = np.concatenate([dt_cols[32:], dt_cols[:32]])
        conv_w = conv_w[::-1]
        dt_bias = dt_bias[::-1]
        a_log = a_log[::-1]
    out = {}
    out["w_fm"] = _tile_w(w_in, fm_cols, 128)
    out["w_tm"] = _tile_w(w_in, tm_cols, 512)
    out["w_dt"] = _tile_w(w_in, dt_cols, 64)[0]
    wkv = np.asarray(inp["w_mem_kv"][0], np.float32)
    out["w_mk"] = _tile_w(wkv, ar(0, 512), 128)
    out["w_mv"] = _tile_w(wkv, ar(512, 1024), 512)[0]
    out["w_o"] = _tile_w(np.asarray(inp["w_out"][0], np.float32), ar(0, 2048), 256)
    out["vecs"] = np.concatenate([np.asarray(inp[k], np.float32).reshape(-1) for k in
                                  ("ln_in_g", "ln_in_b", "ssm_norm_g", "ln_g", "ln_b")]).reshape(1, 5 * D)
    sm = np.zeros((1, 1024), np.float32)
    sm[0, 0:512] = np.asarray(inp["diff_lambda"], np.float32).reshape(-1)
    sm[0, 512:768] = np.asarray(inp["subln_g"], np.float32).reshape(-1)
    sm[0, 768:832] = dt_bias.reshape(-1)
    sm[0, 832:896] = a_log.reshape(-1)
    sm[0, 896:928] = np.asarray(inp["d_skip"], np.float32).reshape(-1)
    out["smalls"] = sm
    cb = np.asarray(inp["conv_b"][0], np.float32)
    cp = np.concatenate([conv_w.T, cb[:, None]], axis=1)
    cp = cp.reshape(24, 128, 6).transpose(1, 0, 2).reshape(128, 144)
    out["convp"] = np.ascontiguousarray(cp)
    return out


def _ht_store(P, jn, hts, kh, row0, q0, nj):
    nc, S = P.nc, P.S
    HT = P.sc[jn]["HT"]
    dst = HT[row0:row0 + nj * 128, q0:q0 + 512].rearrange("(j p) q -> p j q", j=nj)
    S.op("sp", lambda: nc.sync.dma_start(out=dst, in_=hts), reads=[kh], writes=[(jn, "HT", row0, q0)], dma=True, slot=kh)


def _phase3(P, job):
    nc, S, C = P.nc, P.S, P.C
    jn, TK, TQ = job
    sc = P.sc[jn]
    nkt = TK // 128
    nqc = TQ // 512
    P.phase_reset()
    KTb = [P.abf(2 * TK).rearrange("p (m t) -> p m t", m=2) for _ in range(2)]
    Vb = [P.abf(nkt * 258).rearrange("p (t e) -> p t e", e=258) for _ in range(2)]
    Qb = [P.abf(2 * 512).rearrange("p (m q) -> p m q", m=2) for _ in range(2)]
    Eb = [P.abf(512) for _ in range(5)]
    tD = [P.af32(512) for _ in range(2)]
    acc2 = [P.af32(4 * 2 * 257).rearrange("p (a m e) -> p a m e", a=4, m=2) for _ in range(2)]
    o_t = [P.af32(256) for _ in range(2)]
    sq_t = P.af32(256)
    t1_t = [P.af32(256) for _ in range(2)]
    sgb = [P.abf(256) for _ in range(2)]
    hb = [P.abf(256) for _ in range(2)]
    hts = [P.abf(2 * 512).rearrange("p (j q) -> p j q", j=2) for _ in range(2)]
    sm = [P.af32(8) for _ in range(2)]
    for b in range(2):
        S.op("dve", lambda b=b: nc.vector.memset(Vb[b][:, :, 256:258], 1.0), writes=[("a_v", b)])
    ctr = {"s": 0, "e": 0, "td": 0, "fin": 0, "q": 0, "hts": 0}

    def sbank():
        b = 4 + ctr["s"] % 4
        ctr["s"] += 1
        return b

    for h in range(8):
        hbuf = h % 2
        kk, kv = ("a_kt", hbuf), ("a_v", hbuf)
        src = sc["KT"][h * 256:(h + 1) * 256, :].rearrange("(m d) t -> d m t", m=2)
        S.op("sp", lambda hbuf=hbuf, src=src: nc.sync.dma_start(out=KTb[hbuf], in_=src),
             reads=[(jn, "KTall")], writes=[kk], dma=True, slot=kk)
        srcv = sc["V"][:, h * 256:(h + 1) * 256].rearrange("(t p) e -> p t e", p=128)
        for v0 in range(0, nkt, 8):
            v1 = min(nkt, v0 + 8)
            S.op("sp", lambda hbuf=hbuf, srcv=srcv, v0=v0, v1=v1: nc.sync.dma_start(out=Vb[hbuf][:, v0:v1, 0:256], in_=srcv[:, v0:v1, :]),
                 reads=[(jn, "Vall")], writes=[kv], dma=True, slot=(kv, v0))
        slope = SLOPES[h]
        for qc in range(nqc):
            q0 = qc * 512
            qb_ = ctr["q"] % 2
            ctr["q"] += 1
            kq = ("a_q", qb_)
            srcq = sc["QT"][h * 256:(h + 1) * 256, q0:q0 + 512].rearrange("(m d) q -> d m q", m=2)
            S.op("sp", lambda qb_=qb_, srcq=srcq: nc.sync.dma_start(out=Qb[qb_], in_=srcq),
                 reads=[(jn, "QTall")], writes=[kq], dma=True, slot=kq)
            kt0 = q0 // 128
            ab = ctr["q"] % 2
            acc = acc2[ab]
            regions = []
            ltiles = [kt for kt in range(0, kt0) if slope * (q0 - (kt * 128 + 127)) < ALIBI_CUT]
            rtiles = [kt for kt in range(kt0 + 4, nkt) if slope * (kt * 128 - (q0 + 511)) < ALIBI_CUT]
            if ltiles:
                regions.append(("L", ltiles))
            regions.append(("D", list(range(kt0, kt0 + 4))))
            if rtiles:
                regions.append(("R", rtiles))
            units = []
            for m in range(2):
                for ri, (rg, kts) in enumerate(regions):
                    for ki, kt in enumerate(kts):
                        units.append(dict(m=m, rg=rg, kt=kt, ki=ki, n=len(kts), first_region=(ri == 0)))

            def emit_qk(un, hbuf=hbuf, qb_=qb_, kk=kk, kq=kq, kt0=kt0, h=h, slope=slope):
                m, rg, kt = un["m"], un["rg"], un["kt"]
                bs = sbank()
                S.op("pe", lambda: nc.tensor.matmul(
                    P.PS[:, bs, :], KTb[hbuf][:, m, kt * 128:(kt + 1) * 128], Qb[qb_][:, m, :], start=True, stop=True),
                    reads=[kk, kq], writes=[("ps", bs)])
                e = ctr["e"] % 5
                ctr["e"] += 1
                ke = ("a_E", e)
                un["e"], un["ke"] = e, ke
                if rg == "L":
                    j = kt0 - kt
                    S.op("act", lambda: nc.scalar.activation(
                        out=Eb[e], in_=P.PS[:, bs, :], func=AF.Exp, scale=ATT_SCALE, bias=C["bL"][:, h, j:j + 1]),
                        reads=[("ps", bs), "c_bL"], writes=[ke])
                elif rg == "R":
                    j = kt - kt0 - 4
                    S.op("act", lambda: nc.scalar.activation(
                        out=Eb[e], in_=P.PS[:, bs, :], func=AF.Exp, scale=ATT_SCALE, bias=C["bR"][:, h, j:j + 1]),
                        reads=[("ps", bs), "c_bR"], writes=[ke])
                else:
                    jd = kt - kt0
                    td = ctr["td"] % 2
                    ctr["td"] += 1
                    S.op("dve", lambda: nc.vector.scalar_tensor_tensor(
                        out=tD[td], in0=C["dtile"][:, jd, :], scalar=slope / ATT_SCALE, in1=P.PS[:, bs, :],
                        op0=ALU.mult, op1=ALU.add),
                        reads=[("ps", bs), "c_dtile2"], writes=[("a_tD", td)])
                    S.op("act", lambda: nc.scalar.activation(
                        out=Eb[e], in_=tD[td], func=AF.Exp, scale=ATT_SCALE),
                        reads=[("a_tD", td)], writes=[ke])

            def emit_pv(un, hbuf=hbuf, kv=kv, h=h, acc=acc, ab=ab):
                m, rg, kt, ki, n = un["m"], un["rg"], un["kt"], un["ki"], un["n"]
                e, ke = un["e"], un["ke"]
                for qb in range(4):
                    S.op("pe", lambda qb=qb: nc.tensor.matmul(
                        P.PS[:, qb, 0:257], Eb[e][:, qb * 128:(qb + 1) * 128], Vb[hbuf][:, kt, 0:257],
                        start=(ki == 0), stop=(ki == n - 1)),
                        reads=[ke, kv], writes=[("ps", qb)])
                if ki != n - 1:
                    return
                for qb in range(4):
                    ka = ("a_acc", ab, qb, m)
                    dst = acc[:, qb, m, :]
                    srcp = P.PS[:, qb, 0:257]
                    if rg == "L":
                        f = C["fL"][:, h, qb:qb + 1]
                    elif rg == "R":
                        f = C["fR"][:, h, qb:qb + 1]
                    else:
                        f = None
                    if un["first_region"]:
                        if f is None:
                            S.op("dve", lambda dst=dst, srcp=srcp: nc.vector.tensor_copy(out=dst, in_=srcp),
                                 reads=[("ps", qb)], writes=[ka])
                        else:
                            S.op("dve", lambda dst=dst, srcp=srcp, f=f: nc.vector.tensor_scalar(
                                out=dst, in0=srcp, scalar1=f, scalar2=None, op0=ALU.mult),
                                reads=[("ps", qb), "c_fL", "c_fR"], writes=[ka])
                    else:
                        ff = 1.0 if f is None else f
                        S.op("dve", lambda dst=dst, srcp=srcp, ff=ff: nc.vector.scalar_tensor_tensor(
                            out=dst, in0=srcp, scalar=ff, in1=dst, op0=ALU.mult, op1=ALU.add),
                            reads=[("ps", qb), "c_fL", "c_fR", ka], writes=[ka])

            LA = 3
            for i in range(len(units) + LA):
                if i < len(units):
                    emit_qk(units[i])
                if i - LA >= 0:
                    emit_pv(units[i - LA])
            hs = ctr["hts"] % 2
            ctr["hts"] += 1
            kh = ("a_hts", hs)
            for qb in range(4):
                fb = ctr["fin"] % 2
                ctr["fin"] += 1
                smv = sm[fb]
                ks = ("a_sm", fb)
                ko = ("a_o", fb)
                ka0, ka1 = ("a_acc", ab, qb, 0), ("a_acc", ab, qb, 1)
                tq = q0 + qb * 128
                ksg = ("a_sg", fb)
                S.op("sp", lambda fb=fb, tq=tq, h=h: nc.sync.dma_start(out=sgb[fb], in_=sc["SG"][tq:tq + 128, h * 256:(h + 1) * 256]),
                     reads=[(jn, "SGall")], writes=[ksg], dma=True, slot=ksg)
                S.op("dve", lambda smv=smv, qb=qb, acc=acc: nc.vector.reciprocal(out=smv[:, 0:2], in_=acc[:, qb, :, 256]),
                     reads=[ka0, ka1], writes=[ks])
                S.op("dve", lambda smv=smv: nc.vector.tensor_tensor(out=smv[:, 2:3], in0=smv[:, 1:2], in1=C["nlam"], op=ALU.mult),
                     reads=[ks, "c_nlam"], writes=[ks])
                S.op("dve", lambda fb=fb, qb=qb, smv=smv, acc=acc: nc.vector.tensor_scalar(
                    out=o_t[fb], in0=acc[:, qb, 0, 0:256], scalar1=smv[:, 0:1], scalar2=None, op0=ALU.mult),
                    reads=[ka0, ks], writes=[ko])
                S.op("dve", lambda fb=fb, qb=qb, smv=smv, acc=acc: nc.vector.scalar_tensor_tensor(
                    out=o_t[fb], in0=acc[:, qb, 1, 0:256], scalar=smv[:, 2:3], in1=o_t[fb], op0=ALU.mult, op1=ALU.add),
                    reads=[ka1, ks, ko], writes=[ko])
                S.op("dve", lambda fb=fb: nc.vector.tensor_tensor(out=sq_t, in0=o_t[fb], in1=o_t[fb], op=ALU.mult),
                     reads=[ko], writes=["a_sq"])
                S.op("dve", lambda smv=smv: nc.vector.reduce_sum(out=smv[:, 3:4], in_=sq_t, axis=AX.X),
                     reads=["a_sq"], writes=[ks])
                S.op("dve", lambda smv=smv: nc.vector.tensor_scalar(out=smv[:, 4:5], in0=smv[:, 3:4], scalar1=1.0 / 256.0, scalar2=EPS,
                                                                    op0=ALU.mult, op1=ALU.add), reads=[ks], writes=[ks])
                S.op("pool", lambda smv=smv: nc.gpsimd.tensor_tensor(out=smv[:, 4:5], in0=smv[:, 4:5], in1=C["mhalf"], op=ALU.pow),
                     reads=[ks, "c_mh"], writes=[ks])
                S.op("dve", lambda fb=fb: nc.vector.tensor_tensor(out=t1_t[fb], in0=C["subln"], in1=sgb[fb], op=ALU.mult),
                     reads=["c_subln", ksg], writes=[("a_t1", fb)])
                S.op("dve", lambda fb=fb, smv=smv: nc.vector.scalar_tensor_tensor(
                    out=hb[fb], in0=o_t[fb], scalar=smv[:, 4:5], in1=t1_t[fb], op0=ALU.mult, op1=ALU.mult),
                    reads=[ko, ks, ("a_t1", fb)], writes=[("a_hb", fb)])
                bs = sbank()
                psb = P.PS[:, bs, :].bitcast(BF16)
                for j in range(2):
                    S.op("pe", lambda fb=fb, j=j, psb=psb: nc.tensor.transpose(out=psb[:, j * 128:(j + 1) * 128],
                                                                           in_=hb[fb][:, j * 128:(j + 1) * 128], identity=C["ident_bf"]),
                         reads=[("a_hb", fb), "c_ident_bf"], writes=[("ps", bs)])
                S.op("dve", lambda hs=hs, qb=qb, psb=psb: nc.vector.tensor_copy(
                    out=hts[hs][:, :, qb * 128:(qb + 1) * 128], in_=psb[:, 0:256].rearrange("p (j q) -> p j q", j=2)),
                    reads=[("ps", bs)], writes=[kh])
            _ht_store(P, jn, hts[hs], kh, h * 256, q0, 2)


def _phase2(P, job):
    nc, S, C = P.nc, P.S, P.C
    jn, TK, TQ = job
    sc = P.sc[jn]
    nqc = TQ // 512
    P.phase_reset()
    mem = P.jmem[jn]
    mt = [P.af32(D) for _ in range(2)]
    memT = P.abf(NKC * 256).rearrange("p (k t) -> p k t", k=NKC)
    wk = [P.abf(NKC * 128) for _ in range(2)]
    wv = P.abf(NKC * 512)
    KmT = P.abf(4 * 256).rearrange("p (h m) -> p h m", h=4)
    Vm = P.abf(2 * 4 * 130).rearrange("p (t h e) -> p t h e", t=2, h=4)
    Qb = [P.abf(4 * 512).rearrange("p (h q) -> p h q", h=4) for _ in range(2)]
    Eb = [P.abf(512) for _ in range(4)]
    sgb = [P.abf(512) for _ in range(2)]
    rz = [P.af32(4) for _ in range(2)]
    t1 = [P.af32(128) for _ in range(2)]
    hb = [P.abf(128) for _ in range(2)]
    hts = [P.abf(4 * 512).rearrange("p (j q) -> p j q", j=4) for _ in range(2)]
    S.op("dve", lambda: nc.vector.memset(Vm[:, :, :, 128:130], 1.0), writes=["m_Vm1"])
    S.op("pool", lambda: nc.gpsimd.dma_start(out=wv, in_=P.w_mv[:, :], max_dma_last_dim=8192), writes=["m_wv"], dma=True, slot="m_wv")
    for t in range(2):
        kx = ("m_mt", t)
        S.op("sp", lambda t=t: nc.sync.dma_start(out=mt[t], in_=mem[t * 128:(t + 1) * 128, :]), writes=[kx], dma=True, slot=kx)
        for g in range(4):
            bk = P.bank()
            for i in range(4):
                kc = g * 4 + i
                S.op("pe", lambda t=t, kc=kc, bk=bk, i=i: nc.tensor.transpose(out=P.PS[:, bk, i * 128:(i + 1) * 128],
                                                                          in_=mt[t][:, kc * 128:(kc + 1) * 128], identity=C["ident"]),
                     reads=[kx, "c_ident"], writes=[("ps", bk)])
            S.op("dve", lambda t=t, g=g, bk=bk: nc.vector.tensor_copy(
                out=memT[:, g * 4:(g + 1) * 4, t * 128:(t + 1) * 128], in_=P.PS[:, bk, :].rearrange("p (a b) -> p a b", a=4)),
                reads=[("ps", bk)], writes=[("m_memT", t)])
    mk = [("m_memT", 0), ("m_memT", 1)]
    for h in range(4):
        ws = h % 2
        kw = ("m_wk", ws)
        S.op("pool", lambda ws=ws, h=h: nc.gpsimd.dma_start(out=wk[ws], in_=P.w_mk[h]), writes=[kw], dma=True, slot=kw)
        bk = P.bank()
        for kc in range(NKC):
            S.op("pe", lambda ws=ws, kc=kc, bk=bk: nc.tensor.matmul(
                P.PS[:, bk, 0:256], wk[ws][:, kc * 128:(kc + 1) * 128], memT[:, kc, :], start=(kc == 0), stop=(kc == NKC - 1)),
                reads=[kw] + mk, writes=[("ps", bk)])
        S.op("dve", lambda h=h, bk=bk: nc.vector.tensor_copy(out=KmT[:, h, :], in_=P.PS[:, bk, 0:256]),
             reads=[("ps", bk)], writes=["m_KmT"])
    for t in range(2):
        bk = P.bank()
        for kc in range(NKC):
            S.op("pe", lambda t=t, kc=kc, bk=bk: nc.tensor.matmul(
                P.PS[:, bk, :], memT[:, kc, t * 128:(t + 1) * 128], wv[:, kc * 512:(kc + 1) * 512], start=(kc == 0), stop=(kc == NKC - 1)),
                reads=["m_wv", mk[t]], writes=[("ps", bk)])
        S.op("dve", lambda t=t, bk=bk: nc.vector.tensor_copy(
            out=Vm[:, t, :, 0:128], in_=P.PS[:, bk, :].rearrange("p (h e) -> p h e", h=4)),
            reads=[("ps", bk), "m_Vm1"], writes=["m_Vm"])
    ctr = {"s": 0, "e": 0, "f": 0}

    def sbank():
        b = 4 + ctr["s"] % 4
        ctr["s"] += 1
        return b
    scale = 1.0 / math.sqrt(128.0)
    for qc in range(nqc):
        q0 = qc * 512
        qb_ = qc % 2
        kq = ("m_q", qb_)
        srcq = sc["QM"][:, q0:q0 + 512].rearrange("(h d) q -> d h q", h=4)
        S.op("sp", lambda qb_=qb_, srcq=srcq: nc.sync.dma_start(out=Qb[qb_], in_=srcq), writes=[kq], dma=True, slot=kq)
        hs = qc % 2
        kh = ("m_hts", hs)
        for h in range(4):
            for t in range(2):
                bs = sbank()
                S.op("pe", lambda h=h, t=t, qb_=qb_, bs=bs: nc.tensor.matmul(
                    P.PS[:, bs, :], KmT[:, h, t * 128:(t + 1) * 128], Qb[qb_][:, h, :], start=True, stop=True),
                    reads=["m_KmT", kq], writes=[("ps", bs)])
                e = ctr["e"] % 4
                ctr["e"] += 1
                ke = ("m_E", e)
                S.op("act", lambda e=e, bs=bs: nc.scalar.activation(out=Eb[e], in_=P.PS[:, bs, :], func=AF.Exp, scale=scale),
                     reads=[("ps", bs)], writes=[ke])
                for qb in range(4):
                    S.op("pe", lambda e=e, qb=qb, t=t, h=h: nc.tensor.matmul(
                        P.PS[:, qb, 0:129], Eb[e][:, qb * 128:(qb + 1) * 128], Vm[:, t, h, 0:129], start=(t == 0), stop=(t == 1)),
                        reads=[ke, "m_Vm"], writes=[("ps", qb)])
            for qb in range(4):
                fb = ctr["f"] % 2
                ctr["f"] += 1
                tq = q0 + qb * 128
                ksg = ("m_sg", fb)
                if h == 0:
                    pass
                S.op("sp", lambda fb=fb, tq=tq, h=h: nc.sync.dma_start(out=sgb[fb][:, 0:128], in_=sc["SGM"][tq:tq + 128, h * 128:(h + 1) * 128]),
                     writes=[ksg], dma=True, slot=ksg)
                S.op("dve", lambda fb=fb, qb=qb: nc.vector.reciprocal(out=rz[fb][:, 0:1], in_=P.PS[:, qb, 128:129]),
                     reads=[("ps", qb)], writes=[("m_rz", fb)])
                S.op("dve", lambda fb=fb, qb=qb: nc.vector.tensor_scalar(
                    out=t1[fb], in0=P.PS[:, qb, 0:128], scalar1=rz[fb][:, 0:1], scalar2=None, op0=ALU.mult),
                    reads=[("ps", qb), ("m_rz", fb)], writes=[("m_t1", fb)])
                S.op("dve", lambda fb=fb: nc.vector.tensor_tensor(out=hb[fb], in0=t1[fb], in1=sgb[fb][:, 0:128], op=ALU.mult),
                     reads=[("m_t1", fb), ksg], writes=[("m_hb", fb)])
                bs = sbank()
                psb = P.PS[:, bs, :].bitcast(BF16)
                S.op("pe", lambda fb=fb, psb=psb: nc.tensor.transpose(out=psb[:, 0:128], in_=hb[fb], identity=C["ident_bf"]),
                     reads=[("m_hb", fb), "c_ident_bf"], writes=[("ps", bs)])
                S.op("dve", lambda hs=hs, h=h, qb=qb, psb=psb: nc.vector.tensor_copy(
                    out=hts[hs][:, h, qb * 128:(qb + 1) * 128], in_=psb[:, 0:128]),
                    reads=[("ps", bs)], writes=[kh])
        _ht_store(P, jn, hts[hs], kh, 4096, q0, 4)


def _bc(ap, n):
    return ap.to_broadcast([ap.shape[0], ap.shape[1], n])


def _phase4(P, job):
    nc, S, C = P.nc, P.S, P.C
    jn, TK, TQ = job
    sc = P.sc[jn]
    nck, nq = TK // 128, TQ // 128
    XC = sc["XC"] = P.dscr(jn + "_XC", [D_XBC, TK])
    HB = sc["HB"] = P.dscr(jn + "_HB", [nq, 128, 2048])
    P.phase_reset()
    ub = [P.abf(TK + 4) for _ in range(2)]
    dg = [P.abf(5 * 128).rearrange("p (k c) -> p k c", k=5) for _ in range(2)]
    cst = [P.abf(512) for _ in range(4)]
    for b in range(2):
        S.op("dve", lambda b=b: nc.vector.memset(ub[b][:, 0:2], 0.0), writes=[("c4_ub", b)])
        S.op("dve", lambda b=b: nc.vector.memset(ub[b][:, TK + 2:TK + 4], 0.0), writes=[("c4_ub", b)])
    nst = 0
    for ft in range(24):
        b = ft % 2
        ku, kd = ("c4_ub", b), ("c4_dg", b)
        S.op("sp", lambda b=b, ft=ft: nc.sync.dma_start(out=ub[b][:, 2:TK + 2], in_=sc["XBC"][ft * 128:(ft + 1) * 128, :]),
             writes=[ku], dma=True, slot=ku)
        for k in range(5):
            S.op("dve", lambda b=b, ft=ft, k=k: nc.vector.tensor_scalar(
                out=dg[b][:, k, :], in0=C["ident_bf"], scalar1=C["convp"][:, ft, k:k + 1], scalar2=None, op0=ALU.mult),
                reads=["c_ident_bf", "c_convp"], writes=[kd])
        for sb in range(TK // 512):
            bk = P.bank()
            for k in range(5):
                S.op("pe", lambda b=b, k=k, sb=sb, bk=bk: nc.tensor.matmul(
                    P.PS[:, bk, :], dg[b][:, k, :], ub[b][:, sb * 512 + k:sb * 512 + k + 512], start=(k == 0), stop=(k == 4)),
                    reads=[ku, kd], writes=[("ps", bk)])
            cs = nst % 4
            nst += 1
            kc = ("c4_cst", cs)
            S.op("act", lambda cs=cs, bk=bk, ft=ft: nc.scalar.activation(
                out=cst[cs], in_=P.PS[:, bk, :], func=AF.Silu, bias=C["convp"][:, ft, 5:6]),
                reads=[("ps", bk), "c_convp"], writes=[kc])
            S.op("sp", lambda cs=cs, ft=ft, sb=sb: nc.sync.dma_start(out=XC[ft * 128:(ft + 1) * 128, sb * 512:(sb + 1) * 512], in_=cst[cs]),
                 reads=[kc], writes=[(jn, "XC", ft, sb)], dma=True, slot=kc)

    if getattr(P, "p4stop", 9) < 1:
        return
    P.phase_reset()
    vb = P.af32(D)
    S.op("sp", lambda: nc.sync.dma_start(out=vb, in_=P.vecs[0, 2 * D:3 * D].partition_broadcast(128)),
         writes=["s_ng"], dma=True, slot="s_ng")
    xsT = [P.abf(16 * 128).rearrange("p (f t) -> p f t", f=16) for _ in range(2)]
    BCT = [P.abf(8 * 128).rearrange("p (f t) -> p f t", f=8) for _ in range(2)]
    xtok = [P.abf(2048) for _ in range(2)]
    btok = [P.abf(512) for _ in range(2)]
    sml = [P.af32(64 * 10) for _ in range(2)]
    Wm = [P.abf(2 * 512).rearrange("p (d x) -> p d x", d=2) for _ in range(2)]
    xd = [P.abf(4 * 2048).rearrange("p (d x) -> p d x", d=4) for _ in range(2)]
    Ta = [P.af32(512) for _ in range(3)]
    Lb = [P.abf(512) for _ in range(3)]
    Mb2 = [[P.abf(512).rearrange("p (h l) -> p h l", h=4) for _ in range(16)] for _ in range(2)]
    yos = [P.abf(512) for _ in range(4)]
    hst = [P.af32(2048) for _ in range(2)]
    hbf = [P.abf(2048) for _ in range(2)]
    hbs = [P.abf(2048) for _ in range(2)]
    szb = [P.abf(2048) for _ in range(2)]
    yg = [P.af32(512) for _ in range(2)]
    sq = P.af32(512)
    nsm = [P.af32(4) for _ in range(2)]
    hob = [P.abf(512) for _ in range(2)]
    hts = [P.abf(512).rearrange("p (j q) -> p j q", j=4) for _ in range(2)]
    ctr = {"ta": 0, "yo": 0, "g": 0}

    def prep(c, passF):
        b = c % 2
        kx, kb, kt, kbt, ks = ("s_xsT", b), ("s_BCT", b), ("s_xtok", b), ("s_btok", b), ("s_sml", b)
        t0 = c * 128
        S.op("sp", lambda: nc.sync.dma_start(out=xsT[b], in_=XC[0:2048, t0:t0 + 128].rearrange("(f p) t -> p f t", p=128)),
             writes=[kx], dma=True, slot=kx)
        nb_ = 8 if passF else 4
        S.op("sp", lambda: nc.sync.dma_start(out=BCT[b][:, 0:nb_, :], in_=XC[2048:2048 + nb_ * 128, t0:t0 + 128].rearrange("(f p) t -> p f t", p=128)),
             writes=[kb], dma=True, slot=kb)
        v = sml[b]
        dtr, e_, dt, a, acum, ea, dst, dch, dtd = [v[:, i * 64:(i + 1) * 64] for i in range(9)]
        S.op("sp", lambda: nc.sync.dma_start(out=dtr, in_=sc["DTR"][t0:t0 + 128, :]), writes=[ks], dma=True, slot=ks)
        for g8 in range(2):
            bk = P.bank()
            psb = P.PS[:, bk, :].bitcast(BF16)
            for i in range(8):
                f = g8 * 8 + i
                S.op("pe", lambda f=f, i=i, psb=psb: nc.tensor.transpose(out=psb[:, i * 128:(i + 1) * 128], in_=xsT[b][:, f, :], identity=C["ident_bf"]),
                     reads=[kx, "c_ident_bf"], writes=[("ps", bk)])
            eng = "dve" if g8 == 0 else "act"
            if eng == "dve":
                S.op("dve", lambda g8=g8, psb=psb: nc.vector.tensor_copy(out=xtok[b][:, g8 * 1024:(g8 + 1) * 1024], in_=psb),
                     reads=[("ps", bk)], writes=[kt])
            else:
                S.op("act", lambda g8=g8, psb=psb: nc.scalar.copy(out=xtok[b][:, g8 * 1024:(g8 + 1) * 1024], in_=psb),
                     reads=[("ps", bk)], writes=[kt])
        bk = P.bank()
        psb = P.PS[:, bk, :].bitcast(BF16)
        for i in range(4):
            S.op("pe", lambda i=i, psb=psb: nc.tensor.transpose(out=psb[:, i * 128:(i + 1) * 128], in_=BCT[b][:, i, :], identity=C["ident_bf"]),
                 reads=[kb, "c_ident_bf"], writes=[("ps", bk)])
        S.op("dve", lambda psb=psb: nc.vector.tensor_copy(out=btok[b], in_=psb[:, 0:512]), reads=[("ps", bk)], writes=[kbt])
        S.op("dve", lambda: nc.vector.tensor_tensor(out=e_, in0=dtr, in1=C["dtb"], op=ALU.add), reads=[ks, "c_sm"], writes=[ks])
        S.op("act", lambda: nc.scalar.activation(out=e_, in_=e_, func=AF.Exp), reads=[ks], writes=[ks])
        S.op("act", lambda: nc.scalar.activation(out=dt, in_=e_, func=AF.Ln, bias=1.0), reads=[ks], writes=[ks])
        S.op("dve", lambda: nc.vector.tensor_tensor(out=a, in0=dt, in1=C["A"], op=ALU.mult), reads=[ks, "c_A"], writes=[ks])
        bk = P.bank()
        S.op("pe", lambda bk=bk: nc.tensor.matmul(P.PS[:, bk, 0:32], C["le"], a[:, 0:32], start=True, stop=True),
             reads=[ks, "c_le"], writes=[("ps", bk)])
        S.op("pe", lambda bk=bk: nc.tensor.matmul(P.PS[:, bk, 32:64], C["ge"], a[:, 32:64], start=True, stop=True),
             reads=[ks, "c_ge"], writes=[("ps", bk)])
        S.op("pe", lambda bk=bk: nc.tensor.matmul(P.PS[:, bk, 64:128], C["ones"], a, start=True, stop=True),
             reads=[ks, "c_ones"], writes=[("ps", bk)])
        S.op("dve", lambda bk=bk: nc.vector.tensor_copy(out=acum, in_=P.PS[:, bk, 0:64]), reads=[("ps", bk)], writes=[ks])
        S.op("act", lambda bk=bk: nc.scalar.activation(out=ea, in_=P.PS[:, bk, 0:64], func=AF.Exp), reads=[("ps", bk)], writes=[ks])
        S.op("act", lambda bk=bk: nc.scalar.activation(out=dch, in_=P.PS[:, bk, 64:128], func=AF.Exp), reads=[("ps", bk)], writes=[ks])
        S.op("dve", lambda bk=bk: nc.vector.tensor_tensor(out=dst, in0=P.PS[:, bk, 64:128], in1=acum, op=ALU.subtract),
             reads=[("ps", bk), ks], writes=[ks])
        S.op("act", lambda: nc.scalar.activation(out=dst, in_=dst, func=AF.Exp), reads=[ks], writes=[ks])
        S.op("dve", lambda: nc.vector.tensor_tensor(out=dtd, in0=dt, in1=dst, op=ALU.mult), reads=[ks], writes=[ks])
        return dict(b=b, kx=kx, kb=kb, kt=kt, kbt=kbt, ks=ks, dt=dt, a=a, ea=ea, dch=dch, dtd=dtd)

    def state_update(pp, d, xw_ap, kxw):
        b = pp["b"]
        for g in range(4):
            bk = P.bank()
            S.op("pe", lambda g=g, bk=bk: nc.tensor.matmul(
                P.PS[:, bk, :], btok[b][:, g * 128:(g + 1) * 128], xw_ap[:, g * 512:(g + 1) * 512], start=True, stop=True),
                reads=[pp["kbt"], kxw], writes=[("ps", bk)])
            hv = hst[d][:, g * 512:(g + 1) * 512]
            kh = ("s_h", d, g)
            dcol = pp["dch"][:, d * 32 + g * 8:d * 32 + (g + 1) * 8]
            S.op("dve", lambda hv=hv, dcol=dcol: nc.vector.tensor_tensor(
                out=hv.rearrange("p (e q) -> p e q", e=8), in0=hv.rearrange("p (e q) -> p e q", e=8), in1=_bc(dcol, 64), op=ALU.mult),
                reads=[kh, pp["ks"]], writes=[kh])
            S.op("dve", lambda hv=hv, bk=bk: nc.vector.tensor_tensor(out=hv, in0=hv, in1=P.PS[:, bk, :], op=ALU.add),
                 reads=[kh, ("ps", bk)], writes=[kh])
            S.op("act", lambda hv=hv, g=g, d=d: nc.scalar.copy(out=hbf[d][:, g * 512:(g + 1) * 512], in_=hv),
                 reads=[kh], writes=[("s_hbf", d, g)])

    for d in range(2):
        S.op("dve", lambda d=d: nc.vector.memset(hst[d], 0.0), writes=[("s_h", d, g) for g in range(4)])
        S.op("dve", lambda d=d: nc.vector.memset(hbf[d], 0.0), writes=[("s_hbf", d, g) for g in range(4)])
    def prepB(c):
        pp = prep(c, False)
        b = pp["b"]
        kxw = ("s_xd", b, 2)
        xw = xd[b][:, 2, :]
        S.op("pool", lambda: nc.gpsimd.tensor_tensor(
            out=xw.rearrange("p (h q) -> p h q", h=32), in0=xtok[b].rearrange("p (h q) -> p h q", h=32),
            in1=_bc(pp["dtd"][:, 32:64], 64), op=ALU.mult),
            reads=[pp["kt"], pp["ks"]], writes=[kxw])
        return (pp, xw, kxw)

    nxt = prepB(nck - 1) if nck > 1 else None
    for c in range(nck - 1, -1, -1):
        if c < nq:
            S.op("sp", lambda c=c: nc.sync.dma_start(out=HB[c], in_=hbf[1]),
                 reads=[("s_hbf", 1, g) for g in range(4)], writes=[(jn, "HB", c)], dma=True, slot="s_hbst")
        if c == 0:
            break
        cur = nxt
        nxt = prepB(c - 1) if c - 1 >= 1 else None
        state_update(cur[0], 1, cur[1], cur[2])

    if getattr(P, "p4stop", 9) < 2:
        return
    def chunkA(c):
        pp = prep(c, True)
        b = pp["b"]
        Mb = Mb2[b]
        t0 = c * 128
        ksz, khs = ("s_sz", b), ("s_hbs", b)
        S.op("sp", lambda b=b, t0=t0: nc.sync.dma_start(out=szb[b], in_=sc["SZ"][t0:t0 + 128, :]), writes=[ksz], dma=True, slot=ksz)
        S.op("sp", lambda b=b, c=c: nc.sync.dma_start(out=hbs[b], in_=HB[c]), reads=[(jn, "HB", c)], writes=[khs], dma=True, slot=khs)
        bk = P.bank()
        for g in range(4):
            S.op("pe", lambda g=g, bk=bk: nc.tensor.matmul(P.PS[:, bk, g * 128:(g + 1) * 128], BCT[b][:, g, :], BCT[b][:, 4 + g, :],
                                                          start=True, stop=True), reads=[pp["kb"]], writes=[("ps", bk)])
        kW = ("s_W", b)
        S.op("dve", lambda bk=bk: nc.vector.tensor_tensor(out=Wm[b][:, 0, :], in0=P.PS[:, bk, :], in1=C["le4fb"], op=ALU.mult),
             reads=[("ps", bk), "c_le4fb"], writes=[kW])
        S.op("dve", lambda bk=bk: nc.vector.tensor_tensor(out=Wm[b][:, 1, :], in0=P.PS[:, bk, :], in1=C["ge4fb"], op=ALU.mult),
             reads=[("ps", bk), "c_ge4fb"], writes=[kW])
        x3 = xtok[b].rearrange("p (h q) -> p h q", h=32)
        srcs = [pp["dt"][:, 0:32], pp["dt"][:, 32:64], pp["dtd"][:, 0:32], C["dskip"]]
        for i in range(4):
            eng = "pool" if i % 2 == 0 else "dve"
            eo = nc.gpsimd if eng == "pool" else nc.vector
            S.op(eng, lambda i=i, eo=eo: eo.tensor_tensor(out=xd[b][:, i, :].rearrange("p (h q) -> p h q", h=32), in0=x3,
                                                         in1=_bc(srcs[i], 64), op=ALU.mult),
                 reads=[pp["kt"], pp["ks"], "c_sm"], writes=[("s_xd", b, i)])
        if getattr(P, "p4stop", 9) < 3:
            return
        for d in range(2):
            for g in range(4):
                for hf in range(2):
                    u = d * 8 + g * 2 + hf
                    ti = ctr["ta"] % 3
                    ctr["ta"] += 1
                    h0 = d * 32 + g * 8 + hf * 4
                    S.op("pool", lambda ti=ti, d=d, h0=h0, pp=pp: nc.gpsimd.tensor_tensor(
                        out=Ta[ti].rearrange("p (h l) -> p h l", h=4), in0=C["le4f" if d == 0 else "ge4f"].rearrange("p (h l) -> p h l", h=4),
                        in1=_bc(pp["a"][:, h0:h0 + 4], 128), op=ALU.mult),
                        reads=[pp["ks"], "c_le4f", "c_ge4f"], writes=[("s_Ta", ti)])
                    bk2 = P.bank()
                    S.op("pe", lambda ti=ti, d=d, bk2=bk2: nc.tensor.matmul(P.PS[:, bk2, :], C["gt" if d == 0 else "lt"], Ta[ti], start=True, stop=True),
                         reads=[("s_Ta", ti), "c_gt", "c_lt"], writes=[("ps", bk2)])
                    S.op("act", lambda ti=ti, bk2=bk2: nc.scalar.activation(out=Lb[ti], in_=P.PS[:, bk2, :], func=AF.Exp),
                         reads=[("ps", bk2)], writes=[("s_L", ti)])
                    wv = Wm[b][:, d, g * 128:(g + 1) * 128]
                    S.op("dve", lambda u=u, ti=ti, wv=wv: nc.vector.tensor_tensor(
                        out=Mb[u], in0=Lb[ti].rearrange("p (h l) -> p h l", h=4),
                        in1=wv.rearrange("p (o l) -> p o l", o=1).to_broadcast([128, 4, 128]), op=ALU.mult),
                        reads=[("s_L", ti), kW], writes=[("s_M", b, u)])
        return dict(pp=pp, b=b, t0=t0, ksz=ksz, khs=khs, kW=kW, c=c)

    def chunkB(ctx):
        pp, b, t0, ksz, khs, kW, c = ctx["pp"], ctx["b"], ctx["t0"], ctx["ksz"], ctx["khs"], ctx["kW"], ctx["c"]
        Mb = Mb2[b]
        def groupF(g):
            yk = []
            for d in range(2):
                bk3 = P.bank()
                src = hbf[0] if d == 0 else hbs[b]
                ksrc = ("s_hbf", 0, g) if d == 0 else khs
                S.op("pe", lambda g=g, bk3=bk3, src=src: nc.tensor.matmul(P.PS[:, bk3, :], BCT[b][:, 4 + g, :], src[:, g * 512:(g + 1) * 512],
                                                                       start=True, stop=True), reads=[pp["kb"], ksrc], writes=[("ps", bk3)])
                yi = ctr["yo"] % 4
                ctr["yo"] += 1
                ecol = pp["ea"][:, d * 32 + g * 8:d * 32 + (g + 1) * 8]
                S.op("dve", lambda yi=yi, bk3=bk3, ecol=ecol: nc.vector.tensor_tensor(
                    out=yos[yi].rearrange("p (e q) -> p e q", e=8), in0=P.PS[:, bk3, :].rearrange("p (e q) -> p e q", e=8),
                    in1=_bc(ecol, 64), op=ALU.mult), reads=[("ps", bk3), pp["ks"]], writes=[("s_yos", yi)])
                yk.append(yi)
            if getattr(P, "p4stop", 9) < 5:
                return
            by = P.bank()
            rdY = [("s_yos", yk[0]), ("s_yos", yk[1]), ("s_xd", b, 3), "c_ident_bf"]
            S.op("pe", lambda by=by, yi=yk[0]: nc.tensor.matmul(P.PS[:, by, :], C["ident_bf"], yos[yi], start=True, stop=False),
                 reads=rdY, writes=[("ps", by)])
            S.op("pe", lambda by=by, yi=yk[1]: nc.tensor.matmul(P.PS[:, by, :], C["ident_bf"], yos[yi], start=False, stop=False),
                 reads=rdY, writes=[("ps", by)])
            S.op("pe", lambda by=by, g=g: nc.tensor.matmul(P.PS[:, by, :], C["ident_bf"], xd[b][:, 3, g * 512:(g + 1) * 512], start=False, stop=False),
                 reads=rdY, writes=[("ps", by)])
            for d in range(2):
                for hh in range(8):
                    u = d * 8 + g * 2 + hh // 4
                    h = g * 8 + hh
                    last = (d == 1 and hh == 7)
                    S.op("pe", lambda by=by, u=u, hh=hh, d=d, h=h, last=last: nc.tensor.matmul(
                        P.PS[:, by, hh * 64:(hh + 1) * 64], Mb[u][:, hh % 4, :], xd[b][:, d, h * 64:(h + 1) * 64], start=False, stop=last),
                        reads=[("s_M", b, u), ("s_xd", b, d)], writes=[("ps", by)])
            gi = ctr["g"] % 2
            ctr["g"] += 1
            kyg, kn = ("s_yg", gi), ("s_nsm", gi)
            S.op("dve", lambda gi=gi, by=by, g=g: nc.vector.tensor_tensor(out=yg[gi], in0=P.PS[:, by, :], in1=szb[b][:, g * 512:(g + 1) * 512], op=ALU.mult),
                 reads=[("ps", by), ksz], writes=[kyg])
            if c == 0 and g == 0:
                P.dbg("yg", yg[gi], kyg, 512)
                P.dbg("yos0", yos[yk[0]], ("s_yos", yk[0]), 512)
                P.dbg("yos1", yos[yk[1]], ("s_yos", yk[1]), 512)
                P.dbg("xD", xd[b][:, 3, 0:512], ("s_xd", b, 3), 512)
                P.dbg("xdf", xd[b][:, 0, 0:512], ("s_xd", b, 0), 512)
                P.dbg("M0", Mb[0].rearrange("p h l -> p (h l)"), ("s_M", b, 0), 512)
                P.dbg("M8", Mb[8].rearrange("p h l -> p (h l)"), ("s_M", b, 8), 512)
                P.dbg("Wm", Wm[b].rearrange("p d x -> p (d x)"), kW, 1024)
                P.dbg("sml", sml[b], pp["ks"], 640)
                P.dbg("sz", szb[b][:, 0:512], ksz, 512)
                P.dbg("xtok", xtok[b][:, 0:512], pp["kt"], 512)
                P.dbg("btok", btok[b][:, 0:512], pp["kbt"], 512)
                P.dbg("le4fb", C["le4fb"], "c_le4fb", 512)
                P.dbg("xsT", xsT[b][:, 0, :], pp["kx"], 128)
            S.op("dve", lambda gi=gi: nc.vector.tensor_tensor(out=sq, in0=yg[gi], in1=yg[gi], op=ALU.mult), reads=[kyg], writes=["s_sq"])
            S.op("dve", lambda gi=gi: nc.vector.reduce_sum(out=nsm[gi][:, 0:1], in_=sq, axis=AX.X), reads=["s_sq"], writes=[kn])
            S.op("dve", lambda gi=gi: nc.vector.tensor_scalar(out=nsm[gi][:, 1:2], in0=nsm[gi][:, 0:1], scalar1=1.0 / 512.0, scalar2=EPS,
                                                              op0=ALU.mult, op1=ALU.add), reads=[kn], writes=[kn])
            S.op("pool", lambda gi=gi: nc.gpsimd.tensor_tensor(out=nsm[gi][:, 1:2], in0=nsm[gi][:, 1:2], in1=C["mhalf"], op=ALU.pow),
                 reads=[kn, "c_mh"], writes=[kn])
            S.op("dve", lambda gi=gi, g=g: nc.vector.scalar_tensor_tensor(
                out=hob[gi], in0=yg[gi], scalar=nsm[gi][:, 1:2], in1=vb[:, g * 512:(g + 1) * 512], op0=ALU.mult, op1=ALU.mult),
                reads=[kyg, kn, "s_ng"], writes=[("s_hob", gi)])
            bt = P.bank()
            psb = P.PS[:, bt, :].bitcast(BF16)
            for j in range(4):
                S.op("pe", lambda gi=gi, j=j, psb=psb: nc.tensor.transpose(out=psb[:, j * 128:(j + 1) * 128], in_=hob[gi][:, j * 128:(j + 1) * 128],
                                                                       identity=C["ident_bf"]), reads=[("s_hob", gi), "c_ident_bf"], writes=[("ps", bt)])
            kh = ("s_hts", gi)
            S.op("act", lambda gi=gi, psb=psb: nc.scalar.copy(out=hts[gi], in_=psb[:, 0:512].rearrange("p (j q) -> p j q", j=4)),
                 reads=[("ps", bt)], writes=[kh])
            dsth = sc["HT"][2048 + g * 512:2048 + (g + 1) * 512, t0:t0 + 128].rearrange("(j p) q -> p j q", j=4)
            S.op("sp", lambda gi=gi, dsth=dsth: nc.sync.dma_start(out=dsth, in_=hts[gi]), reads=[kh], writes=[(jn, "HT", 2048 + g * 512, t0)],
                 dma=True, slot=kh)
        if getattr(P, "p4stop", 9) < 6:
            return
        for g in range(4):
            groupF(g)
        if c < nq - 1:
            state_update(pp, 0, xd[b][:, 2, :], ("s_xd", b, 2))

    ctx = chunkA(0)
    for c in range(nq):
        nctx = chunkA(c + 1) if c + 1 < nq else None
        chunkB(ctx)
        ctx = nctx


def _phase5(P, job):
    nc, S, C = P.nc, P.S, P.C
    jn, TK, TQ = job
    sc = P.sc[jn]
    y = P.jy[jn]
    P.phase_reset()
    vb = P.af32(2 * D)
    S.op("sp", lambda: nc.sync.dma_start(out=vb, in_=P.vecs[0, 3 * D:5 * D].partition_broadcast(128)),
         writes=["o_vb"], dma=True, slot="o_vb")
    lng, lnb = vb[:, 0:D], vb[:, D:2 * D]
    HTb2 = [P.abf(36 * 512).rearrange("p (k q) -> p k q", k=36) for _ in range(2)]
    wo = [P.abf(36 * 256) for _ in range(2)]
    rt = [P.af32(D) for _ in range(4)]
    st = [P.af32(24) for _ in range(2)]
    mv = [P.af32(2) for _ in range(2)]
    rs = [P.af32(2) for _ in range(2)]
    nw = [0]

    def load_ht(blk):
        q0 = blk * 512
        hb_ = blk % 2
        srch = sc["HT"][:, q0:q0 + 512].rearrange("(k p) q -> p k q", p=128)
        for k0 in range(0, 36, 9):
            S.op("sp", lambda k0=k0: nc.sync.dma_start(out=HTb2[hb_][:, k0:k0 + 9, :], in_=srch[:, k0:k0 + 9, :]),
                 writes=[("o_HTb", hb_)], dma=True, slot=("o_HTb", hb_, k0))

    def block(blk):
        q0 = blk * 512
        hb_ = blk % 2
        HTb = HTb2[hb_]
        kht = ("o_HTb", hb_)
        if blk + 1 < TQ // 512:
            load_ht(blk + 1)
        for tt in range(4):
            S.op("sp", lambda tt=tt: nc.sync.dma_start(out=rt[tt], in_=sc["XN"][q0 + tt * 128:q0 + (tt + 1) * 128, :]),
                 writes=[("o_r", tt)], dma=True, slot=("o_r", tt))
        for cc in range(8):
            ws = nw[0] % 2
            nw[0] += 1
            kw = ("o_w", ws)
            S.op("pool", lambda ws=ws, cc=cc: nc.gpsimd.dma_start(out=wo[ws], in_=P.WOB[cc]),
                 reads=[("WOB", cc)], writes=[kw], dma=True, slot=kw)
            for tt in range(4):
                bk = P.bank()
                for kc in range(36):
                    S.op("pe", lambda ws=ws, kc=kc, tt=tt, bk=bk: nc.tensor.matmul(
                        P.PS[:, bk, 0:256], HTb[:, kc, tt * 128:(tt + 1) * 128], wo[ws][:, kc * 256:(kc + 1) * 256],
                        start=(kc == 0), stop=(kc == 35)), reads=[kw, kht], writes=[("ps", bk)])
                rv = rt[tt][:, cc * 256:(cc + 1) * 256]
                S.op("dve", lambda rv=rv, bk=bk: nc.vector.scalar_tensor_tensor(
                    out=rv, in0=rv, scalar=ALPHA, in1=P.PS[:, bk, 0:256], op0=ALU.mult, op1=ALU.add),
                    reads=[("ps", bk), ("o_r", tt)], writes=[("o_r", tt)])
        for tt in range(4):
            b = tt % 2
            kr = ("o_r", tt)
            for q in range(4):
                S.op("dve", lambda b=b, q=q, tt=tt: nc.vector.bn_stats(out=st[b][:, q * 6:(q + 1) * 6], in_=rt[tt][:, q * 512:(q + 1) * 512]),
                     reads=[kr], writes=[("o_st", b)])
            S.op("dve", lambda b=b: nc.vector.bn_aggr(out=mv[b], in_=st[b]), reads=[("o_st", b)], writes=[("o_mv", b)])
            _ln_rstd(P, mv[b], rs[b][:, 0:1], rs[b][:, 1:2], [("o_mv", b)], "o_rs%d" % b)
            rk = ["o_rs%d_rstd" % b, "o_rs%d_nmr" % b]
            S.op("act", lambda b=b, tt=tt: nc.scalar.activation(out=rt[tt], in_=rt[tt], func=AF.Identity, scale=rs[b][:, 0:1], bias=rs[b][:, 1:2]),
                 reads=[kr] + rk, writes=[kr])
            S.op("pool", lambda tt=tt: nc.gpsimd.tensor_tensor(out=rt[tt], in0=rt[tt], in1=lng, op=ALU.mult), reads=[kr, "o_vb"], writes=[kr])
            S.op("dve", lambda tt=tt: nc.vector.tensor_tensor(out=rt[tt], in0=rt[tt], in1=lnb, op=ALU.add), reads=[kr, "o_vb"], writes=[kr])
            S.op("sp", lambda tt=tt: nc.sync.dma_start(out=y[q0 + tt * 128:q0 + (tt + 1) * 128, :], in_=rt[tt]),
                 reads=[kr], writes=[(jn, "y", q0, tt)], dma=True, slot=("o_yst", tt))

    load_ht(0)
    for blk in range(TQ // 512):
        block(blk)


def build_program(jobs, debug=()):
    P = Prog(jobs, debug=debug)
    P.sc = {}
    _p_setup(P)
    _p_consts(P)
    for j in jobs:
        _phase1(P, j)
        _phase2(P, j)
        _phase3(P, j)
        _phase4(P, j)
        _phase5(P, j)
    P.S.emit()
    return P


_CACHE = {}


def kernel(**inputs):
    xp = np.asarray(inputs["x_prompt"], np.float32)
    xs = np.asarray(inputs["x_sample"], np.float32)
    mp = np.asarray(inputs["mem_prompt"], np.float32)
    ms = np.asarray(inputs["mem_sample"], np.float32)
    B, T, _ = xp.shape
    SB, TS, _ = xs.shape
    ncores = 8
    assert 2 * B == ncores and SB == ncores
    TQ = T // 2
    jobs = [("p", T, TQ), ("s", TS, TS)]
    key = (T, TS)
    if key not in _CACHE:
        _CACHE[key] = build_program(jobs)
    P = _CACHE[key]
    shared = [prep_shared(inputs, False), prep_shared(inputs, True)]
    maps = []
    for c in range(ncores):
        flip = c % 2 == 1
        m = dict(shared[1 if flip else 0])
        a, b = xp[c // 2], xs[c]
        if flip:
            a, b = a[::-1], b[::-1]
        m["x_p"] = np.ascontiguousarray(a)
        m["x_s"] = np.ascontiguousarray(b)
        m["mem_p"] = np.ascontiguousarray(mp[c // 2])
        m["mem_s"] = np.ascontiguousarray(ms[c])
        maps.append(m)
    res = run_bass_kernel_spmd(P.nc, maps, core_ids=list(range(ncores)))
    yp = np.empty((B, T, D), np.float32)
    ys = np.empty((SB, TS, D), np.float32)
    for c in range(ncores):
        r = res.results[c]
        a = np.asarray(r["y_p"], np.float32)
        b = np.asarray(r["y_s"], np.float32)
        if c % 2 == 0:
            yp[c // 2, :TQ] = a
            ys[c] = b
        else:
            yp[c // 2, TQ:] = a[::-1]
            ys[c] = b[::-1]
    return (yp, ys)
```

```python
from contextlib import ExitStack
import math
import numpy as np
import concourse.bass as bass
import concourse.mybir as mybir
from concourse.bass_utils import run_bass_kernel_spmd

F32 = mybir.dt.float32
BF16 = mybir.dt.bfloat16
I32 = mybir.dt.int32
AF = mybir.ActivationFunctionType
ALU = mybir.AluOpType
AX = mybir.AxisListType

D = 2048
NKC = 16
D_XBC = 3072
D_MIX = 4608
NMEM = 256
EPS = 1e-5
ALPHA = 2.0 ** 0.25
LAM_INIT = 0.2
ATT_SCALE = 1.0 / math.sqrt(128.0)
SLOPES = [2.0 ** (-(h + 1)) for h in range(8)]
ALIBI_CUT = 64.0

EPOCH = 16000


class _Op:
    __slots__ = ("eng", "fn", "dma", "slot", "idx", "waits", "signal", "seq", "slot_cnt", "name")


class Sched:
    ENGS = ("pe", "act", "dve", "pool", "sp")

    def __init__(self, nc, same_engine_sync=True):
        self.nc = nc
        self.ops = {e: [] for e in self.ENGS}
        self.last_w = {}
        self.readers = {}
        self.seen = {e: {} for e in self.ENGS}
        self.slot_cnt = {}
        self.slot_last = {}
        self.same = same_engine_sync
        self.barrier_deps = []
        self.nops = 0

    def _need(self, o, d):
        if d.dma:
            key = ("s", d.slot)
            val = d.slot_cnt
        else:
            if d.eng == o.eng and (d.eng == "pe" or not self.same):
                return False
            key = ("c", d.eng)
            val = d.idx
        if self.seen[o.eng].get(key, 0) >= val:
            return False
        self.seen[o.eng][key] = val
        return True

    def op(self, eng, fn, reads=(), writes=(), dma=False, slot=None, name=None):
        o = _Op()
        o.eng = eng; o.fn = fn; o.dma = dma; o.slot = slot; o.signal = False; o.seq = None
        o.name = name; o.slot_cnt = 0
        lst = self.ops[eng]
        o.idx = len(lst) + 1
        deps = []
        for r in reads:
            deps.append(self.last_w.get(r))
        for w in writes:
            deps.append(self.last_w.get(w))
            rd = self.readers.get(w)
            if rd:
                deps.extend(rd.values())
        deps.extend(self.barrier_deps)
        if getattr(self, "serial", False) and getattr(self, "prev_op", None) is not None:
            deps.append(self.prev_op)
        self.prev_op = o
        if dma:
            assert slot is not None
            deps.append(self.slot_last.get(slot))
            c = self.slot_cnt.get(slot, 0) + 1
            self.slot_cnt[slot] = c
            o.slot_cnt = c
            self.slot_last[slot] = o
        deps = [d for d in deps if d is not None and d is not o]
        deps.sort(key=lambda d: -(d.slot_cnt if d.dma else d.idx))
        waits = []
        for d in deps:
            if self._need(o, d):
                d.signal = True
                waits.append(d)
        o.waits = waits
        lst.append(o)
        for r in reads:
            self.readers.setdefault(r, {})[("d", slot) if dma else eng] = o
        for w in writes:
            self.last_w[w] = o
            self.readers[w] = {}
        self.nops += 1
        return o

    def barrier(self):
        deps = []
        for e in self.ENGS:
            for o in reversed(self.ops[e]):
                if not o.dma:
                    deps.append(o)
                    break
        deps.extend(self.slot_last.values())
        self.barrier_deps = deps

    def emit(self):
        nc = self.nc
        self.barrier()
        self.op("sp", None, name="final")
        with ExitStack() as es:
            semtab = {}

            def getsem(key):
                if key not in semtab:
                    semtab[key] = es.enter_context(nc.semaphore("s%d" % len(semtab)))
                return semtab[key]

            for e in self.ENGS:
                n = 0
                for o in self.ops[e]:
                    if not o.dma and o.signal:
                        o.seq = n
                        n += 1
            engobj = {"pe": nc.tensor, "act": nc.scalar, "dve": nc.vector, "pool": nc.gpsimd, "sp": nc.sync}
            DE = EPOCH // 16
            for e in self.ENGS:
                for o in self.ops[e]:
                    if o.dma:
                        getsem(("s", o.slot, (o.slot_cnt - 1) // DE))
                    elif o.signal:
                        getsem(("c", e, o.seq // EPOCH))

            def run(e):
                eo = engobj[e]
                for o in self.ops[e]:
                    for d in o.waits:
                        if d.dma:
                            k = d.slot_cnt - 1
                            eo.wait_ge(getsem(("s", d.slot, k // DE)), 16 * (k % DE + 1))
                        else:
                            eo.wait_ge(getsem(("c", d.eng, d.seq // EPOCH)), d.seq % EPOCH + 1)
                    if o.fn is None:
                        continue
                    ins = o.fn()
                    if o.dma:
                        k = o.slot_cnt - 1
                        ins.then_inc(getsem(("s", o.slot, k // DE)), 16)
                    elif o.signal:
                        ins.then_inc(getsem(("c", e, o.seq // EPOCH)), 1)

            with nc.Block() as block:
                @block.tensor
                def _(x):
                    run("pe")

                @block.scalar
                def _(x):
                    run("act")

                @block.vector
                def _(x):
                    run("dve")

                @block.gpsimd
                def _(x):
                    run("pool")

                @block.sync
                def _(x):
                    run("sp")
            self.nsem = len(semtab)


class Prog:
    def __init__(self, jobs, debug=()):
        self.jobs = jobs
        self.debug = set(debug)
        self.nc = nc = bass.Bass("TRN2", target_bir_lowering=False)
        self.S = Sched(nc)
        self.es = ExitStack()
        self.dram = {}
        self.psum_rr = 0
        self.uid = 0

    def din(self, name, shape, dt=F32):
        t = self.nc.dram_tensor(name, list(shape), dt, kind="ExternalInput").ap()
        self.dram[name] = t
        return t

    def dout(self, name, shape, dt=F32):
        t = self.nc.dram_tensor(name, list(shape), dt, kind="ExternalOutput").ap()
        self.dram[name] = t
        return t

    def dscr(self, name, shape, dt=BF16):
        kind = "ExternalOutput" if name in self.debug else "Internal"
        t = self.nc.dram_tensor(name, list(shape), dt, kind=kind).ap()
        self.dram[name] = t
        return t

    def arena_init(self, words):
        self.A = self.es.enter_context(self.nc.sbuf_tensor("arena", [128, words], F32))
        self.AW = words
        self.aoff = 0
        self.perm_off = 0

    def af32(self, n, perm=False):
        off = self.aoff
        self.aoff += n
        assert self.aoff <= self.AW, ("arena overflow", self.aoff, self.AW)
        if perm:
            self.perm_off = self.aoff
        return self.A[:, off:off + n]

    def abf(self, n, perm=False):
        w = (n + 1) // 2
        v = self.af32(w, perm)
        return v.bitcast(BF16)[:, 0:n]

    def phase_reset(self):
        self.S.barrier()
        self.aoff = self.perm_off

    def dbg(self, name, ap, key, n):
        if name not in self.debug:
            return
        o = self.nc.dram_tensor("dbg_" + name, [128, n], ap.dtype, kind="ExternalOutput").ap()
        self.S.op("sp", lambda: self.nc.sync.dma_start(out=o[:, :], in_=ap), reads=[key], dma=True, slot="dbg_" + name)

    def bank(self):
        b = self.psum_rr
        self.psum_rr = (self.psum_rr + 1) % 8
        return b

    def u(self):
        self.uid += 1
        return self.uid


FM_GROUPS = [("QT", 0, 16), ("KT", 16, 16), ("XBC", 32, 24), ("QM", 56, 4)]
TM_GROUPS = [("V", 0, 4, False), ("SG", 4, 4, True), ("SZ", 8, 4, True), ("SGM", 12, 1, True)]


def _p_setup(P):
    nc = P.nc
    P.w_fm = P.din("w_fm", [60, 128, NKC * 128])
    P.w_tm = P.din("w_tm", [13, 128, NKC * 512])
    P.w_dt = P.din("w_dt", [128, NKC * 64])
    P.w_mk = P.din("w_mk", [4, 128, NKC * 128])
    P.w_mv = P.din("w_mv", [128, NKC * 512])
    P.w_o = P.din("w_o", [8, 128, 36 * 256])
    P.vecs = P.din("vecs", [1, 5 * D])
    P.smalls = P.din("smalls", [1, 1024])
    P.convp = P.din("convp", [128, 24 * 6])
    P.jx = {}
    P.jmem = {}
    P.jy = {}
    for (jn, TK, TQ) in P.jobs:
        P.jx[jn] = P.din("x_" + jn, [TK, D])
        P.jmem[jn] = P.din("mem_" + jn, [NMEM, D])
        P.jy[jn] = P.dout("y_" + jn, [TQ, D])
    P.arena_init(51200)
    P.PS = P.es.enter_context(nc.psum_tensor("ps", [128, 8, 512], F32))


def _p_consts(P):
    nc, S = P.nc, P.S
    C = P.C = {}
    jmp = P.af32(128, perm=True)
    S.op("pool", lambda: nc.gpsimd.iota(jmp, pattern=[[1, 128]], base=0, channel_multiplier=-1,
                                        allow_small_or_imprecise_dtypes=True), writes=["c_jmp"])

    def mk(key, op, dt=F32):
        t = P.af32(128, perm=True) if dt == F32 else P.abf(128, perm=True)
        S.op("dve", lambda: nc.vector.tensor_single_scalar(out=t, in_=jmp, scalar=0.0, op=op),
             reads=["c_jmp"], writes=["c_" + key])
        C[key] = t
        return t
    mk("ident", ALU.is_equal)
    mk("ident_bf", ALU.is_equal, BF16)
    mk("le", ALU.is_ge)
    mk("ge", ALU.is_le)
    mk("lt", ALU.is_gt)
    mk("gt", ALU.is_lt)
    for key, srck in (("le4f", "le"), ("ge4f", "ge")):
        t4 = P.af32(512, perm=True)
        t4b = P.abf(512, perm=True)
        for i in range(4):
            S.op("dve", lambda t4=t4, i=i, srck=srck: nc.vector.tensor_copy(out=t4[:, i * 128:(i + 1) * 128], in_=C[srck]),
                 reads=["c_" + srck], writes=["c_" + key])
            S.op("dve", lambda t4b=t4b, i=i, srck=srck: nc.vector.tensor_copy(out=t4b[:, i * 128:(i + 1) * 128], in_=C[srck]),
                 reads=["c_" + srck], writes=["c_" + key + "b"])
        C[key] = t4
        C[key + "b"] = t4b
    ones = P.af32(128, perm=True)
    S.op("dve", lambda: nc.vector.memset(ones, 1.0), writes=["c_ones"])
    C["ones"] = ones
    dt_ = P.af32(4 * 512, perm=True)
    dt3 = dt_.rearrange("p (a b) -> p a b", a=4)
    S.op("pool", lambda: nc.gpsimd.iota(dt_, pattern=[[-128, 4], [1, 512]], base=0, channel_multiplier=-1,
                                        allow_small_or_imprecise_dtypes=True), writes=["c_dtile"])
    S.op("dve", lambda: nc.vector.scalar_tensor_tensor(out=dt_, in0=dt_, scalar=-1.0, in1=dt_, op0=ALU.mult, op1=ALU.min),
         reads=["c_dtile"], writes=["c_dtile2", "c_dtile"])
    C["dtile"] = dt3
    IL = P.af32(64, perm=True)
    IR = P.af32(64, perm=True)
    IQ2 = P.af32(4, perm=True)
    S.op("pool", lambda: nc.gpsimd.iota(IL, pattern=[[-128, 64]], base=0, channel_multiplier=1,
                                        allow_small_or_imprecise_dtypes=True), writes=["c_IL"])
    S.op("pool", lambda: nc.gpsimd.iota(IR, pattern=[[128, 64]], base=0, channel_multiplier=1,
                                        allow_small_or_imprecise_dtypes=True), writes=["c_IR"])
    S.op("pool", lambda: nc.gpsimd.iota(IQ2, pattern=[[-128, 4]], base=512, channel_multiplier=-1,
                                        allow_small_or_imprecise_dtypes=True), writes=["c_IQ2"])
    bL = P.af32(8 * 64, perm=True).rearrange("p (h j) -> p h j", h=8)
    bR = P.af32(8 * 64, perm=True).rearrange("p (h j) -> p h j", h=8)
    fL = P.af32(8 * 4, perm=True).rearrange("p (h j) -> p h j", h=8)
    fR = P.af32(8 * 4, perm=True).rearrange("p (h j) -> p h j", h=8)
    for h in range(8):
        sl = SLOPES[h]
        S.op("dve", lambda h=h, sl=sl: nc.vector.tensor_single_scalar(out=bL[:, h, :], in_=IL, scalar=sl, op=ALU.mult),
             reads=["c_IL"], writes=["c_bL"])
        S.op("dve", lambda h=h, sl=sl: nc.vector.tensor_single_scalar(out=bR[:, h, :], in_=IR, scalar=-sl, op=ALU.mult),
             reads=["c_IR"], writes=["c_bR"])
        S.op("act", lambda h=h, sl=sl: nc.scalar.activation(out=fL[:, h, :], in_=IR[:, 0:4], func=AF.Exp, scale=-sl),
             reads=["c_IR"], writes=["c_fL"])
        S.op("act", lambda h=h, sl=sl: nc.scalar.activation(out=fR[:, h, :], in_=IQ2, func=AF.Exp, scale=-sl),
             reads=["c_IQ2"], writes=["c_fR"])
    C["bL"], C["bR"], C["fL"], C["fR"] = bL, bR, fL, fR
    sm = P.af32(1024, perm=True)
    S.op("sp", lambda: nc.sync.dma_start(out=sm, in_=P.smalls[0, :].partition_broadcast(128)),
         writes=["c_sm"], dma=True, slot="c_sm")
    C["dl"] = sm[:, 0:512]
    C["subln"] = sm[:, 512:768]
    C["dtb"] = sm[:, 768:832]
    C["alog"] = sm[:, 832:896]
    C["dskip"] = sm[:, 896:928]
    junk = P.af32(128)
    s12 = P.af32(4, perm=True)
    S.op("dve", lambda: nc.vector.tensor_tensor(out=junk, in0=sm[:, 0:128], in1=sm[:, 128:256], op=ALU.mult),
         reads=["c_sm"], writes=["junk0"])
    S.op("dve", lambda: nc.vector.reduce_sum(out=s12[:, 0:1], in_=junk, axis=AX.X), reads=["junk0"], writes=["c_s1"])
    S.op("dve", lambda: nc.vector.tensor_tensor(out=junk, in0=sm[:, 256:384], in1=sm[:, 384:512], op=ALU.mult),
         reads=["c_sm", "c_s1"], writes=["junk0"])
    S.op("dve", lambda: nc.vector.reduce_sum(out=s12[:, 1:2], in_=junk, axis=AX.X), reads=["junk0"], writes=["c_s2"])
    S.op("act", lambda: nc.scalar.activation(out=s12[:, 2:4], in_=s12[:, 0:2], func=AF.Exp),
         reads=["c_s1", "c_s2"], writes=["c_e12"])
    nlam = P.af32(1, perm=True)
    S.op("dve", lambda: nc.vector.tensor_tensor(out=nlam, in0=s12[:, 3:4], in1=s12[:, 2:3], op=ALU.subtract),
         reads=["c_e12"], writes=["c_nlam"])
    S.op("dve", lambda: nc.vector.tensor_single_scalar(out=nlam, in_=nlam, scalar=-LAM_INIT, op=ALU.add),
         reads=["c_nlam"], writes=["c_nlam"])
    C["nlam"] = nlam
    S.op("dve", lambda: nc.vector.tensor_single_scalar(out=C["subln"], in_=C["subln"], scalar=1.0 - LAM_INIT, op=ALU.mult),
         reads=["c_sm", "c_s1", "c_s2"], writes=["c_subln"])
    Abc = P.af32(64, perm=True)
    S.op("act", lambda: nc.scalar.activation(out=Abc, in_=C["alog"], func=AF.Exp), reads=["c_sm"], writes=["c_A"])
    S.op("dve", lambda: nc.vector.tensor_single_scalar(out=Abc, in_=Abc, scalar=-1.0, op=ALU.mult), reads=["c_A"], writes=["c_A"])
    C["A"] = Abc
    mh = P.af32(1, perm=True)
    S.op("dve", lambda: nc.vector.memset(mh, -0.5), writes=["c_mh"])
    C["mhalf"] = mh
    cp = P.af32(24 * 6, perm=True)
    S.op("sp", lambda: nc.sync.dma_start(out=cp, in_=P.convp[:, :]), writes=["c_convp"], dma=True, slot="c_convp")
    C["convp"] = cp.rearrange("p (f k) -> p f k", f=24)
    P.WOB = P.dscr("WOB", [8, 128, 36 * 256])
    P.aoff = P.perm_off
    wtmp = [P.abf(36 * 256) for _ in range(2)]
    for cc in range(8):
        ws = cc % 2
        kw = ("c_wtmp", ws)
        S.op("pool", lambda ws=ws, cc=cc: nc.gpsimd.dma_start(out=wtmp[ws], in_=P.w_o[cc], max_dma_last_dim=8192),
             writes=[kw], dma=True, slot=kw)
        S.op("sp", lambda ws=ws, cc=cc: nc.sync.dma_start(out=P.WOB[cc], in_=wtmp[ws]), reads=[kw], writes=[("WOB", cc)],
             dma=True, slot=("c_wst", ws))
    P.aoff = P.perm_off


def _ln_rstd(P, mv, rstd, nmr, rk, wk):
    nc, S = P.nc, P.S
    S.op("dve", lambda: nc.vector.tensor_single_scalar(out=rstd, in_=mv[:, 1:2], scalar=EPS, op=ALU.add),
         reads=rk, writes=[wk + "_rstd"])
    S.op("pool", lambda: nc.gpsimd.tensor_tensor(out=rstd, in0=rstd, in1=P.C["mhalf"], op=ALU.pow),
         reads=[wk + "_rstd", "c_mh"], writes=[wk + "_rstd"])
    S.op("dve", lambda: nc.vector.tensor_scalar(out=nmr, in0=mv[:, 0:1], scalar1=rstd, scalar2=-1.0, op0=ALU.mult, op1=ALU.mult),
         reads=rk + [wk + "_rstd"], writes=[wk + "_nmr"])


def _phase1(P, job):
    nc, S, C = P.nc, P.S, P.C
    jn, TK, TQ = job
    NB = min(1024, TQ)
    nblk = TK // NB
    ntt = NB // 128
    nsb = NB // 512
    x = P.jx[jn]
    sc = {}
    sc["QT"] = P.dscr(jn + "_QT", [D, TQ]); sc["KT"] = P.dscr(jn + "_KT", [D, TK])
    sc["V"] = P.dscr(jn + "_V", [TK, D]); sc["SG"] = P.dscr(jn + "_SG", [TQ, D]); sc["SZ"] = P.dscr(jn + "_SZ", [TQ, D])
    sc["XBC"] = P.dscr(jn + "_XBC", [D_XBC, TK]); sc["DTR"] = P.dscr(jn + "_DTR", [TK, 64], F32)
    sc["QM"] = P.dscr(jn + "_QM", [512, TQ]); sc["SGM"] = P.dscr(jn + "_SGM", [TQ, 512])
    sc["XN"] = P.dscr(jn + "_XN", [TQ, D], F32)
    sc["HT"] = P.dscr(jn + "_HT", [D_MIX, TQ])
    P.sc[jn] = sc

    P.phase_reset()
    vb = P.af32(2 * D)
    S.op("sp", lambda: nc.sync.dma_start(out=vb, in_=P.vecs[0, 0:2 * D].partition_broadcast(128)),
         writes=["p1_vb"], dma=True, slot="p1_vb")
    gin, bin_ = vb[:, 0:D], vb[:, D:2 * D]
    xnT2 = [P.abf(NKC * NB).rearrange("p (k t) -> p k t", k=NKC) for _ in range(2)]
    xin = [P.af32(D) for _ in range(2)]
    xc = [P.af32(D) for _ in range(2)]
    wfm = [P.abf(NKC * 128) for _ in range(3)]
    wtm = [P.abf(NKC * 512) for _ in range(2)]
    wdt = P.abf(NKC * 64)
    ost = [P.abf(512) for _ in range(4)]
    odt = [P.af32(64) for _ in range(2)]
    st = [P.af32(24) for _ in range(2)]
    mv = [P.af32(2) for _ in range(2)]
    rs = [P.af32(2) for _ in range(2)]
    S.op("pool", lambda: nc.gpsimd.dma_start(out=wdt, in_=P.w_dt[:, :]), writes=["p1_wdt"], dma=True, slot="p1_wdt")
    cnt = {"fm": 0, "tm": 0, "ost": 0, "ev": 0, "odt": 0, "ln": 0}

    def ln_tile(blk, tt):
        xb = blk % 2
        xnT = xnT2[xb]
        own = blk * NB < TQ
        t0 = blk * NB + tt * 128
        b = cnt["ln"] % 2
        cnt["ln"] += 1
        kx, kc_ = ("p1_xin", b), ("p1_xc", b)
        S.op("sp", lambda: nc.sync.dma_start(out=xin[b], in_=x[t0:t0 + 128, :]), writes=[kx], dma=True, slot=kx)
        for q in range(4):
            S.op("dve", lambda q=q: nc.vector.bn_stats(out=st[b][:, q * 6:(q + 1) * 6], in_=xin[b][:, q * 512:(q + 1) * 512]),
                 reads=[kx], writes=[("p1_st", b)])
        S.op("dve", lambda: nc.vector.bn_aggr(out=mv[b], in_=st[b]), reads=[("p1_st", b)], writes=[("p1_mv", b)])
        _ln_rstd(P, mv[b], rs[b][:, 0:1], rs[b][:, 1:2], [("p1_mv", b)], "p1_rs%d" % b)
        rk = ["p1_rs%d_rstd" % b, "p1_rs%d_nmr" % b]
        S.op("act", lambda: nc.scalar.activation(out=xc[b], in_=xin[b], func=AF.Identity, scale=rs[b][:, 0:1], bias=rs[b][:, 1:2]),
             reads=[kx] + rk, writes=[kc_])
        S.op("dve", lambda: nc.vector.tensor_tensor(out=xc[b], in0=xc[b], in1=gin, op=ALU.mult), reads=[kc_, "p1_vb"], writes=[kc_])
        S.op("dve", lambda: nc.vector.tensor_tensor(out=xc[b], in0=xc[b], in1=bin_, op=ALU.add), reads=[kc_, "p1_vb"], writes=[kc_])
        if own:
            S.op("sp", lambda: nc.sync.dma_start(out=sc["XN"][t0:t0 + 128, :], in_=xc[b]),
                 reads=[kc_], writes=[(jn, "XN", t0)], dma=True, slot=("p1_xnst", b))
        return lambda: ln_back(xb, tt, b, kc_)

    def ln_back(xb, tt, b, kc_):
        xnT = xnT2[xb]
        for g in range(4):
            bk = P.bank()
            for i in range(4):
                kc = g * 4 + i
                S.op("pe", lambda kc=kc, bk=bk, i=i: nc.tensor.transpose(out=P.PS[:, bk, i * 128:(i + 1) * 128],
                                                                      in_=xc[b][:, kc * 128:(kc + 1) * 128], identity=C["ident"]),
                     reads=[kc_, "c_ident"], writes=[("ps", bk)])
            src = P.PS[:, bk, :].rearrange("p (a b) -> p a b", a=4)
            dst = xnT[:, g * 4:(g + 1) * 4, tt * 128:(tt + 1) * 128]
            if g % 2 == 0:
                S.op("dve", lambda src=src, dst=dst: nc.vector.tensor_copy(out=dst, in_=src),
                     reads=[("ps", bk)], writes=[("p1_xnT", xb, tt)])
            else:
                S.op("act", lambda src=src, dst=dst: nc.scalar.copy(out=dst, in_=src),
                     reads=[("ps", bk)], writes=[("p1_xnT", xb, tt)])

    def proj_items(blk):
        xb = blk % 2
        xnT = xnT2[xb]
        own = blk * NB < TQ
        xkeys = [("p1_xnT", xb, tt) for tt in range(ntt)]
        items = []

        def fm_item(dst, wi, j):
            ws = cnt["fm"] % 3
            cnt["fm"] += 1
            kw = ("p1_wfm", ws)
            S.op("pool", lambda: nc.gpsimd.dma_start(out=wfm[ws], in_=P.w_fm[wi]), writes=[kw], dma=True, slot=kw)
            for sb in range(nsb):
                bk = P.bank()
                for kc in range(NKC):
                    S.op("pe", lambda kc=kc, sb=sb, bk=bk: nc.tensor.matmul(
                        P.PS[:, bk, :], wfm[ws][:, kc * 128:(kc + 1) * 128], xnT[:, kc, sb * 512:(sb + 1) * 512],
                        start=(kc == 0), stop=(kc == NKC - 1)),
                        reads=[kw] + xkeys[sb * 4:(sb + 1) * 4], writes=[("ps", bk)])
                os_ = cnt["ost"] % 4
                cnt["ost"] += 1
                ko = ("p1_ost", os_)
                cnt["ev"] += 1
                if cnt["ev"] % 2 == 0:
                    S.op("dve", lambda os_=os_, bk=bk: nc.vector.tensor_copy(out=ost[os_], in_=P.PS[:, bk, :]),
                         reads=[("ps", bk)], writes=[ko])
                else:
                    S.op("act", lambda os_=os_, bk=bk: nc.scalar.copy(out=ost[os_], in_=P.PS[:, bk, :]),
                         reads=[("ps", bk)], writes=[ko])
                c0 = blk * NB + sb * 512
                S.op("sp", lambda os_=os_, c0=c0: nc.sync.dma_start(
                    out=sc[dst][j * 128:(j + 1) * 128, c0:c0 + 512], in_=ost[os_]),
                    reads=[ko], writes=[(jn, dst, j, c0)], dma=True, slot=ko)

        def tm_item(dst, wi, j, silu):
            ws = cnt["tm"] % 2
            cnt["tm"] += 1
            kw = ("p1_wtm", ws)
            S.op("pool", lambda: nc.gpsimd.dma_start(out=wtm[ws], in_=P.w_tm[wi], max_dma_last_dim=8192),
                 writes=[kw], dma=True, slot=kw)
            for tt in range(ntt):
                bk = P.bank()
                for kc in range(NKC):
                    S.op("pe", lambda kc=kc, tt=tt, bk=bk: nc.tensor.matmul(
                        P.PS[:, bk, :], xnT[:, kc, tt * 128:(tt + 1) * 128], wtm[ws][:, kc * 512:(kc + 1) * 512],
                        start=(kc == 0), stop=(kc == NKC - 1)),
                        reads=[kw, xkeys[tt]], writes=[("ps", bk)])
                os_ = cnt["ost"] % 4
                cnt["ost"] += 1
                ko = ("p1_ost", os_)
                if silu:
                    S.op("act", lambda os_=os_, bk=bk: nc.scalar.activation(out=ost[os_], in_=P.PS[:, bk, :], func=AF.Silu),
                         reads=[("ps", bk)], writes=[ko])
                else:
                    S.op("dve", lambda os_=os_, bk=bk: nc.vector.tensor_copy(out=ost[os_], in_=P.PS[:, bk, :]),
                         reads=[("ps", bk)], writes=[ko])
                t0 = blk * NB + tt * 128
                S.op("sp", lambda os_=os_, t0=t0: nc.sync.dma_start(
                    out=sc[dst][t0:t0 + 128, j * 512:(j + 1) * 512], in_=ost[os_]),
                    reads=[ko], writes=[(jn, dst, t0, j)], dma=True, slot=ko)

        def dt_item():
            for tt in range(ntt):
                bk = P.bank()
                for kc in range(NKC):
                    S.op("pe", lambda kc=kc, tt=tt, bk=bk: nc.tensor.matmul(
                        P.PS[:, bk, 0:64], xnT[:, kc, tt * 128:(tt + 1) * 128], wdt[:, kc * 64:(kc + 1) * 64],
                        start=(kc == 0), stop=(kc == NKC - 1)),
                        reads=["p1_wdt", xkeys[tt]], writes=[("ps", bk)])
                os_ = cnt["odt"] % 2
                cnt["odt"] += 1
                ko = ("p1_odt", os_)
                S.op("dve", lambda os_=os_, bk=bk: nc.vector.tensor_copy(out=odt[os_], in_=P.PS[:, bk, 0:64]),
                     reads=[("ps", bk)], writes=[ko])
                t0 = blk * NB + tt * 128
                S.op("sp", lambda os_=os_, t0=t0: nc.sync.dma_start(out=sc["DTR"][t0:t0 + 128, :], in_=odt[os_]),
                     reads=[ko], writes=[(jn, "DTR", t0)], dma=True, slot=ko)

        for (dst, w0, nw) in FM_GROUPS:
            if not own and dst in ("QT", "QM"):
                continue
            for j in range(nw):
                items.append(lambda dst=dst, wi=w0 + j, j=j: fm_item(dst, wi, j))
        for (dst, c0w, ncw, silu) in TM_GROUPS:
            if not own and dst != "V":
                continue
            for j in range(ncw):
                items.append(lambda dst=dst, wi=c0w + j, j=j, silu=silu: tm_item(dst, wi, j, silu))
        items.append(dt_item)
        return items

    tiles0 = [(0, tt) for tt in range(ntt)]
    back = ln_tile(*tiles0[0])
    for k in range(len(tiles0)):
        nb_ = ln_tile(*tiles0[k + 1]) if k + 1 < len(tiles0) else None
        back()
        back = nb_
    for blk in range(nblk):
        items = proj_items(blk)
        pending = [(blk + 1, tt) for tt in range(ntt)] if blk + 1 < nblk else []
        stride = max(1, (len(items) - 2) // max(1, len(pending))) if pending else 0
        back = ln_tile(*pending.pop(0)) if pending else None
        for i, it in enumerate(items):
            it()
            if back is not None and i % stride == stride - 1:
                nb_ = ln_tile(*pending.pop(0)) if pending else None
                back()
                back = nb_
        while back is not None:
            nb_ = ln_tile(*pending.pop(0)) if pending else None
            back()
            back = nb_


IN_SIZES = (2048, 2048, 2048, 2048, 2048, 3072, 64, 512, 512)
_OFF = np.concatenate([[0], np.cumsum(IN_SIZES)]).tolist()
O_Q, O_K, O_V, O_G, O_Z, O_XBC, O_DT, O_QM, O_GM = _OFF[:9]


def _tile_w(w, cols, width):
    K = w.shape[0]
    sub = w[:, cols]
    n = sub.shape[1]
    t = sub.reshape(K // 128, 128, n // width, width)
    t = np.transpose(t, (2, 1, 0, 3))
    return np.ascontiguousarray(t.reshape(n // width, 128, (K // 128) * width), dtype=np.float32)


def prep_shared(inp, flip):
    w_in = np.asarray(inp["w_in"][0], np.float32)
    ar = np.arange
    fm_cols = np.concatenate([ar(O_Q, O_Q + 2048), ar(O_K, O_K + 2048), ar(O_XBC, O_XBC + 3072), ar(O_QM, O_QM + 512)])
    tm_cols = np.concatenate([ar(O_V, O_V + 2048), ar(O_G, O_G + 2048), ar(O_Z, O_Z + 2048), ar(O_GM, O_GM + 512)])
    dt_cols = ar(O_DT, O_DT + 64)
    conv_w = np.asarray(inp["conv_w"][0], np.float32)
    dt_bias = np.asarray(inp["dt_bias"][0], np.float32)
    a_log = np.asarray(inp["a_log"][0], np.float32)
    if flip:
        dt_cols = np.concatenate([dt_cols[32:], dt_cols[:32]])
        conv_w = conv_w[::-1]
        dt_bias = dt_bias[::-1]
        a_log = a_log[::-1]
    out = {}
    out["w_fm"] = _tile_w(w_in, fm_cols, 128)
    out["w_tm"] = _tile_w(w_in, tm_cols, 512)
    out["w_dt"] = _tile_w(w_in, dt_cols, 64)[0]
    wkv = np.asarray(inp["w_mem_kv"][0], np.float32)
    out["w_mk"] = _tile_w(wkv, ar(0, 512), 128)
    out["w_mv"] = _tile_w(wkv, ar(512, 1024), 512)[0]
    out["w_o"] = _tile_w(np.asarray(inp["w_out"][0], np.float32), ar(0, 2048), 256)
    out["vecs"] = np.concatenate([np.asarray(inp[k], np.float32).reshape(-1) for k in
                                  ("ln_in_g", "ln_in_b", "ssm_norm_g", "ln_g", "ln_b")]).reshape(1, 5 * D)
    sm = np.zeros((1, 1024), np.float32)
    sm[0, 0:512] = np.asarray(inp["diff_lambda"], np.float32).reshape(-1)
    sm[0, 512:768] = np.asarray(inp["subln_g"], np.float32).reshape(-1)
    sm[0, 768:832] = dt_bias.reshape(-1)
    sm[0, 832:896] = a_log.reshape(-1)
    sm[0, 896:928] = np.asarray(inp["d_skip"], np.float32).reshape(-1)
    out["smalls"] = sm
    cb = np.asarray(inp["conv_b"][0], np.float32)
    cp = np.concatenate([conv_w.T, cb[:, None]], axis=1)
    cp = cp.reshape(24, 128, 6).transpose(1, 0, 2).reshape(128, 144)
    out["convp"] = np.ascontiguousarray(cp)
    return out


def _ht_store(P, jn, hts, kh, row0, q0, nj):
    nc, S = P.nc, P.S
    HT = P.sc[jn]["HT"]
    dst = HT[row0:row0 + nj * 128, q0:q0 + 512].rearrange("(j p) q -> p j q", j=nj)
    S.op("sp", lambda: nc.sync.dma_start(out=dst, in_=hts), reads=[kh], writes=[(jn, "HT", row0, q0)], dma=True, slot=kh)


def _phase3(P, job):
    nc, S, C = P.nc, P.S, P.C
    jn, TK, TQ = job
    sc = P.sc[jn]
    nkt = TK // 128
    nqc = TQ // 512
    P.phase_reset()
    KTb = [P.abf(2 * TK).rearrange("p (m t) -> p m t", m=2) for _ in range(2)]
    Vb = [P.abf(nkt * 258).rearrange("p (t e) -> p t e", e=258) for _ in range(2)]
    Qb = [P.abf(2 * 512).rearrange("p (m q) -> p m q", m=2) for _ in range(2)]
    Eb = [P.abf(512) for _ in range(5)]
    tD = [P.af32(512) for _ in range(2)]
    acc2 = [P.af32(4 * 2 * 257).rearrange("p (a m e) -> p a m e", a=4, m=2) for _ in range(2)]
    o_t = [P.af32(256) for _ in range(2)]
    sq_t = P.af32(256)
    t1_t = [P.af32(256) for _ in range(2)]
    sgb = [P.abf(256) for _ in range(2)]
    hb = [P.abf(256) for _ in range(2)]
    hts = [P.abf(2 * 512).rearrange("p (j q) -> p j q", j=2) for _ in range(2)]
    sm = [P.af32(8) for _ in range(2)]
    for b in range(2):
        S.op("dve", lambda b=b: nc.vector.memset(Vb[b][:, :, 256:258], 1.0), writes=[("a_v", b)])
    ctr = {"s": 0, "e": 0, "td": 0, "fin": 0, "q": 0, "hts": 0}

    def sbank():
        b = 4 + ctr["s"] % 4
        ctr["s"] += 1
        return b

    for h in range(8):
        hbuf = h % 2
        kk, kv = ("a_kt", hbuf), ("a_v", hbuf)
        src = sc["KT"][h * 256:(h + 1) * 256, :].rearrange("(m d) t -> d m t", m=2)
        S.op("sp", lambda hbuf=hbuf, src=src: nc.sync.dma_start(out=KTb[hbuf], in_=src),
             reads=[(jn, "KTall")], writes=[kk], dma=True, slot=kk)
        srcv = sc["V"][:, h * 256:(h + 1) * 256].rearrange("(t p) e -> p t e", p=128)
        for v0 in range(0, nkt, 8):
            v1 = min(nkt, v0 + 8)
            S.op("sp", lambda hbuf=hbuf, srcv=srcv, v0=v0, v1=v1: nc.sync.dma_start(out=Vb[hbuf][:, v0:v1, 0:256], in_=srcv[:, v0:v1, :]),
                 reads=[(jn, "Vall")], writes=[kv], dma=True, slot=(kv, v0))
        slope = SLOPES[h]
        for qc in range(nqc):
            q0 = qc * 512
            qb_ = ctr["q"] % 2
            ctr["q"] += 1
            kq = ("a_q", qb_)
            srcq = sc["QT"][h * 256:(h + 1) * 256, q0:q0 + 512].rearrange("(m d) q -> d m q", m=2)
            S.op("sp", lambda qb_=qb_, srcq=srcq: nc.sync.dma_start(out=Qb[qb_], in_=srcq),
                 reads=[(jn, "QTall")], writes=[kq], dma=True, slot=kq)
            kt0 = q0 // 128
            ab = ctr["q"] % 2
            acc = acc2[ab]
            regions = []
            ltiles = [kt for kt in range(0, kt0) if slope * (q0 - (kt * 128 + 127)) < ALIBI_CUT]
            rtiles = [kt for kt in range(kt0 + 4, nkt) if slope * (kt * 128 - (q0 + 511)) < ALIBI_CUT]
            if ltiles:
                regions.append(("L", ltiles))
            regions.append(("D", list(range(kt0, kt0 + 4))))
            if rtiles:
                regions.append(("R", rtiles))
            units = []
            for m in range(2):
                for ri, (rg, kts) in enumerate(regions):
                    for ki, kt in enumerate(kts):
                        units.append(dict(m=m, rg=rg, kt=kt, ki=ki, n=len(kts), first_region=(ri == 0)))

            def emit_qk(un, hbuf=hbuf, qb_=qb_, kk=kk, kq=kq, kt0=kt0, h=h, slope=slope):
                m, rg, kt = un["m"], un["rg"], un["kt"]
                bs = sbank()
                S.op("pe", lambda: nc.tensor.matmul(
                    P.PS[:, bs, :], KTb[hbuf][:, m, kt * 128:(kt + 1) * 128], Qb[qb_][:, m, :], start=True, stop=True),
                    reads=[kk, kq], writes=[("ps", bs)])
                e = ctr["e"] % 5
                ctr["e"] += 1
                ke = ("a_E", e)
                un["e"], un["ke"] = e, ke
                if rg == "L":
                    j = kt0 - kt
                    S.op("act", lambda: nc.scalar.activation(
                        out=Eb[e], in_=P.PS[:, bs, :], func=AF.Exp, scale=ATT_SCALE, bias=C["bL"][:, h, j:j + 1]),
                        reads=[("ps", bs), "c_bL"], writes=[ke])
                elif rg == "R":
                    j = kt - kt0 - 4
                    S.op("act", lambda: nc.scalar.activation(
                        out=Eb[e], in_=P.PS[:, bs, :], func=AF.Exp, scale=ATT_SCALE, bias=C["bR"][:, h, j:j + 1]),
                        reads=[("ps", bs), "c_bR"], writes=[ke])
                else:
                    jd = kt - kt0
                    td = ctr["td"] % 2
                    ctr["td"] += 1
                    S.op("dve", lambda: nc.vector.scalar_tensor_tensor(
                        out=tD[td], in0=C["dtile"][:, jd, :], scalar=slope / ATT_SCALE, in1=P.PS[:, bs, :],
                        op0=ALU.mult, op1=ALU.add),
                        reads=[("ps", bs), "c_dtile2"], writes=[("a_tD", td)])
                    S.op("act", lambda: nc.scalar.activation(
                        out=Eb[e], in_=tD[td], func=AF.Exp, scale=ATT_SCALE),
                        reads=[("a_tD", td)], writes=[ke])

            def emit_pv(un, hbuf=hbuf, kv=kv, h=h, acc=acc, ab=ab):
                m, rg, kt, ki, n = un["m"], un["rg"], un["kt"], un["ki"], un["n"]
                e, ke = un["e"], un["ke"]
                for qb in range(4):
                    S.op("pe", lambda qb=qb: nc.tensor.matmul(
                        P.PS[:, qb, 0:257], Eb[e][:, qb * 128:(qb + 1) * 128], Vb[hbuf][:, kt, 0:257],
                        start=(ki == 0), stop=(ki == n - 1)),
                        reads=[ke, kv], writes=[("ps", qb)])
                if ki != n - 1:
                    return
                for qb in range(4):
                    ka = ("a_acc", ab, qb, m)
                    dst = acc[:, qb, m, :]
                    srcp = P.PS[:, qb, 0:257]
                    if rg == "L":
                        f = C["fL"][:, h, qb:qb + 1]
                    elif rg == "R":
                        f = C["fR"][:, h, qb:qb + 1]
                    else:
                        f = None
                    if un["first_region"]:
                        if f is None:
                            S.op("dve", lambda dst=dst, srcp=srcp: nc.vector.tensor_copy(out=dst, in_=srcp),
                                 reads=[("ps", qb)], writes=[ka])
                        else:
                            S.op("dve", lambda dst=dst, srcp=srcp, f=f: nc.vector.tensor_scalar(
                                out=dst, in0=srcp, scalar1=f, scalar2=None, op0=ALU.mult),
                                reads=[("ps", qb), "c_fL", "c_fR"], writes=[ka])
                    else:
                        ff = 1.0 if f is None else f
                        S.op("dve", lambda dst=dst, srcp=srcp, ff=ff: nc.vector.scalar_tensor_tensor(
                            out=dst, in0=srcp, scalar=ff, in1=dst, op0=ALU.mult, op1=ALU.add),
                            reads=[("ps", qb), "c_fL", "c_fR", ka], writes=[ka])

            LA = 3
            for i in range(len(units) + LA):
                if i < len(units):
                    emit_qk(units[i])
                if i - LA >= 0:
                    emit_pv(units[i - LA])
            hs = ctr["hts"] % 2
            ctr["hts"] += 1
            kh = ("a_hts", hs)
            for qb in range(4):
                fb = ctr["fin"] % 2
                ctr["fin"] += 1
                smv = sm[fb]
                ks = ("a_sm", fb)
                ko = ("a_o", fb)
                ka0, ka1 = ("a_acc", ab, qb, 0), ("a_acc", ab, qb, 1)
                tq = q0 + qb * 128
                ksg = ("a_sg", fb)
                S.op("sp", lambda fb=fb, tq=tq, h=h: nc.sync.dma_start(out=sgb[fb], in_=sc["SG"][tq:tq + 128, h * 256:(h + 1) * 256]),
                     reads=[(jn, "SGall")], writes=[ksg], dma=True, slot=ksg)
                S.op("dve", lambda smv=smv, qb=qb, acc=acc: nc.vector.reciprocal(out=smv[:, 0:2], in_=acc[:, qb, :, 256]),
                     reads=[ka0, ka1], writes=[ks])
                S.op("dve", lambda smv=smv: nc.vector.tensor_tensor(out=smv[:, 2:3], in0=smv[:, 1:2], in1=C["nlam"], op=ALU.mult),
                     reads=[ks, "c_nlam"], writes=[ks])
                S.op("dve", lambda fb=fb, qb=qb, smv=smv, acc=acc: nc.vector.tensor_scalar(
                    out=o_t[fb], in0=acc[:, qb, 0, 0:256], scalar1=smv[:, 0:1], scalar2=None, op0=ALU.mult),
                    reads=[ka0, ks], writes=[ko])
                S.op("dve", lambda fb=fb, qb=qb, smv=smv, acc=acc: nc.vector.scalar_tensor_tensor(
                    out=o_t[fb], in0=acc[:, qb, 1, 0:256], scalar=smv[:, 2:3], in1=o_t[fb], op0=ALU.mult, op1=ALU.add),
                    reads=[ka1, ks, ko], writes=[ko])
                S.op("dve", lambda fb=fb: nc.vector.tensor_tensor(out=sq_t, in0=o_t[fb], in1=o_t[fb], op=ALU.mult),
                     reads=[ko], writes=["a_sq"])
                S.op("dve", lambda smv=smv: nc.vector.reduce_sum(out=smv[:, 3:4], in_=sq_t, axis=AX.X),
                     reads=["a_sq"], writes=[ks])
                S.op("dve", lambda smv=smv: nc.vector.tensor_scalar(out=smv[:, 4:5], in0=smv[:, 3:4], scalar1=1.0 / 256.0, scalar2=EPS,
                                                                    op0=ALU.mult, op1=ALU.add), reads=[ks], writes=[ks])
                S.op("pool", lambda smv=smv: nc.gpsimd.tensor_tensor(out=smv[:, 4:5], in0=smv[:, 4:5], in1=C["mhalf"], op=ALU.pow),
                     reads=[ks, "c_mh"], writes=[ks])
                S.op("dve", lambda fb=fb: nc.vector.tensor_tensor(out=t1_t[fb], in0=C["subln"], in1=sgb[fb], op=ALU.mult),
                     reads=["c_subln", ksg], writes=[("a_t1", fb)])
                S.op("dve", lambda fb=fb, smv=smv: nc.vector.scalar_tensor_tensor(
                    out=hb[fb], in0=o_t[fb], scalar=smv[:, 4:5], in1=t1_t[fb], op0=ALU.mult, op1=ALU.mult),
                    reads=[ko, ks, ("a_t1", fb)], writes=[("a_hb", fb)])
                bs = sbank()
                psb = P.PS[:, bs, :].bitcast(BF16)
                for j in range(2):
                    S.op("pe", lambda fb=fb, j=j, psb=psb: nc.tensor.transpose(out=psb[:, j * 128:(j + 1) * 128],
                                                                           in_=hb[fb][:, j * 128:(j + 1) * 128], identity=C["ident_bf"]),
                         reads=[("a_hb", fb), "c_ident_bf"], writes=[("ps", bs)])
                S.op("dve", lambda hs=hs, qb=qb, psb=psb: nc.vector.tensor_copy(
                    out=hts[hs][:, :, qb * 128:(qb + 1) * 128], in_=psb[:, 0:256].rearrange("p (j q) -> p j q", j=2)),
                    reads=[("ps", bs)], writes=[kh])
            _ht_store(P, jn, hts[hs], kh, h * 256, q0, 2)


def _phase2(P, job):
    nc, S, C = P.nc, P.S, P.C
    jn, TK, TQ = job
    sc = P.sc[jn]
    nqc = TQ // 512
    P.phase_reset()
    mem = P.jmem[jn]
    mt = [P.af32(D) for _ in range(2)]
    memT = P.abf(NKC * 256).rearrange("p (k t) -> p k t", k=NKC)
    wk = [P.abf(NKC * 128) for _ in range(2)]
    wv = P.abf(NKC * 512)
    KmT = P.abf(4 * 256).rearrange("p (h m) -> p h m", h=4)
    Vm = P.abf(2 * 4 * 130).rearrange("p (t h e) -> p t h e", t=2, h=4)
    Qb = [P.abf(4 * 512).rearrange("p (h q) -> p h q", h=4) for _ in range(2)]
    Eb = [P.abf(512) for _ in range(4)]
    sgb = [P.abf(512) for _ in range(2)]
    rz = [P.af32(4) for _ in range(2)]
    t1 = [P.af32(128) for _ in range(2)]
    hb = [P.abf(128) for _ in range(2)]
    hts = [P.abf(4 * 512).rearrange("p (j q) -> p j q", j=4) for _ in range(2)]
    S.op("dve", lambda: nc.vector.memset(Vm[:, :, :, 128:130], 1.0), writes=["m_Vm1"])
    S.op("pool", lambda: nc.gpsimd.dma_start(out=wv, in_=P.w_mv[:, :], max_dma_last_dim=8192), writes=["m_wv"], dma=True, slot="m_wv")
    for t in range(2):
        kx = ("m_mt", t)
        S.op("sp", lambda t=t: nc.sync.dma_start(out=mt[t], in_=mem[t * 128:(t + 1) * 128, :]), writes=[kx], dma=True, slot=kx)
        for g in range(4):
            bk = P.bank()
            for i in range(4):
                kc = g * 4 + i
                S.op("pe", lambda t=t, kc=kc, bk=bk, i=i: nc.tensor.transpose(out=P.PS[:, bk, i * 128:(i + 1) * 128],
                                                                          in_=mt[t][:, kc * 128:(kc + 1) * 128], identity=C["ident"]),
                     reads=[kx, "c_ident"], writes=[("ps", bk)])
            S.op("dve", lambda t=t, g=g, bk=bk: nc.vector.tensor_copy(
                out=memT[:, g * 4:(g + 1) * 4, t * 128:(t + 1) * 128], in_=P.PS[:, bk, :].rearrange("p (a b) -> p a b", a=4)),
                reads=[("ps", bk)], writes=[("m_memT", t)])
    mk = [("m_memT", 0), ("m_memT", 1)]
    for h in range(4):
        ws = h % 2
        kw = ("m_wk", ws)
        S.op("pool", lambda ws=ws, h=h: nc.gpsimd.dma_start(out=wk[ws], in_=P.w_mk[h]), writes=[kw], dma=True, slot=kw)
        bk = P.bank()
        for kc in range(NKC):
            S.op("pe", lambda ws=ws, kc=kc, bk=bk: nc.tensor.matmul(
                P.PS[:, bk, 0:256], wk[ws][:, kc * 128:(kc + 1) * 128], memT[:, kc, :], start=(kc == 0), stop=(kc == NKC - 1)),
                reads=[kw] + mk, writes=[("ps", bk)])
        S.op("dve", lambda h=h, bk=bk: nc.vector.tensor_copy(out=KmT[:, h, :], in_=P.PS[:, bk, 0:256]),
             reads=[("ps", bk)], writes=["m_KmT"])
    for t in range(2):
        bk = P.bank()
        for kc in range(NKC):
            S.op("pe", lambda t=t, kc=kc, bk=bk: nc.tensor.matmul(
                P.PS[:, bk, :], memT[:, kc, t * 128:(t + 1) * 128], wv[:, kc * 512:(kc + 1) * 512], start=(kc == 0), stop=(kc == NKC - 1)),
                reads=["m_wv", mk[t]], writes=[("ps", bk)])
        S.op("dve", lambda t=t, bk=bk: nc.vector.tensor_copy(
            out=Vm[:, t, :, 0:128], in_=P.PS[:, bk, :].rearrange("p (h e) -> p h e", h=4)),
            reads=[("ps", bk), "m_Vm1"], writes=["m_Vm"])
    ctr = {"s": 0, "e": 0, "f": 0}

    def sbank():
        b = 4 + ctr["s"] % 4
        ctr["s"] += 1
        return b
    scale = 1.0 / math.sqrt(128.0)
    for qc in range(nqc):
        q0 = qc * 512
        qb_ = qc % 2
        kq = ("m_q", qb_)
        srcq = sc["QM"][:, q0:q0 + 512].rearrange("(h d) q -> d h q", h=4)
        S.op("sp", lambda qb_=qb_, srcq=srcq: nc.sync.dma_start(out=Qb[qb_], in_=srcq), writes=[kq], dma=True, slot=kq)
        hs = qc % 2
        kh = ("m_hts", hs)
        for h in range(4):
            for t in range(2):
                bs = sbank()
                S.op("pe", lambda h=h, t=t, qb_=qb_, bs=bs: nc.tensor.matmul(
                    P.PS[:, bs, :], KmT[:, h, t * 128:(t + 1) * 128], Qb[qb_][:, h, :], start=True, stop=True),
                    reads=["m_KmT", kq], writes=[("ps", bs)])
                e = ctr["e"] % 4
                ctr["e"] += 1
                ke = ("m_E", e)
                S.op("act", lambda e=e, bs=bs: nc.scalar.activation(out=Eb[e], in_=P.PS[:, bs, :], func=AF.Exp, scale=scale),
                     reads=[("ps", bs)], writes=[ke])
                for qb in range(4):
                    S.op("pe", lambda e=e, qb=qb, t=t, h=h: nc.tensor.matmul(
                        P.PS[:, qb, 0:129], Eb[e][:, qb * 128:(qb + 1) * 128], Vm[:, t, h, 0:129], start=(t == 0), stop=(t == 1)),
                        reads=[ke, "m_Vm"], writes=[("ps", qb)])
            for qb in range(4):
                fb = ctr["f"] % 2
                ctr["f"] += 1
                tq = q0 + qb * 128
                ksg = ("m_sg", fb)
                if h == 0:
                    pass
                S.op("sp", lambda fb=fb, tq=tq, h=h: nc.sync.dma_start(out=sgb[fb][:, 0:128], in_=sc["SGM"][tq:tq + 128, h * 128:(h + 1) * 128]),
                     writes=[ksg], dma=True, slot=ksg)
                S.op("dve", lambda fb=fb, qb=qb: nc.vector.reciprocal(out=rz[fb][:, 0:1], in_=P.PS[:, qb, 128:129]),
                     reads=[("ps", qb)], writes=[("m_rz", fb)])
                S.op("dve", lambda fb=fb, qb=qb: nc.vector.tensor_scalar(
                    out=t1[fb], in0=P.PS[:, qb, 0:128], scalar1=rz[fb][:, 0:1], scalar2=None, op0=ALU.mult),
                    reads=[("ps", qb), ("m_rz", fb)], writes=[("m_t1", fb)])
                S.op("dve", lambda fb=fb: nc.vector.tensor_tensor(out=hb[fb], in0=t1[fb], in1=sgb[fb][:, 0:128], op=ALU.mult),
                     reads=[("m_t1", fb), ksg], writes=[("m_hb", fb)])
                bs = sbank()
                psb = P.PS[:, bs, :].bitcast(BF16)
                S.op("pe", lambda fb=fb, psb=psb: nc.tensor.transpose(out=psb[:, 0:128], in_=hb[fb], identity=C["ident_bf"]),
                     reads=[("m_hb", fb), "c_ident_bf"], writes=[("ps", bs)])
                S.op("dve", lambda hs=hs, h=h, qb=qb, psb=psb: nc.vector.tensor_copy(
                    out=hts[hs][:, h, qb * 128:(qb + 1) * 128], in_=psb[:, 0:128]),
                    reads=[("ps", bs)], writes=[kh])
        _ht_store(P, jn, hts[hs], kh, 4096, q0, 4)


def _bc(ap, n):
    return ap.to_broadcast([ap.shape[0], ap.shape[1], n])


def _phase4(P, job):
    nc, S, C = P.nc, P.S, P.C
    jn, TK, TQ = job
    sc = P.sc[jn]
    nck, nq = TK // 128, TQ // 128
    XC = sc["XC"] = P.dscr(jn + "_XC", [D_XBC, TK])
    HB = sc["HB"] = P.dscr(jn + "_HB", [nq, 128, 2048])
    P.phase_reset()
    ub = [P.abf(TK + 4) for _ in range(2)]
    dg = [P.abf(5 * 128).rearrange("p (k c) -> p k c", k=5) for _ in range(2)]
    cst = [P.abf(512) for _ in range(4)]
    for b in range(2):
        S.op("dve", lambda b=b: nc.vector.memset(ub[b][:, 0:2], 0.0), writes=[("c4_ub", b)])
        S.op("dve", lambda b=b: nc.vector.memset(ub[b][:, TK + 2:TK + 4], 0.0), writes=[("c4_ub", b)])
    nst = 0
    for ft in range(24):
        b = ft % 2
        ku, kd = ("c4_ub", b), ("c4_dg", b)
        S.op("sp", lambda b=b, ft=ft: nc.sync.dma_start(out=ub[b][:, 2:TK + 2], in_=sc["XBC"][ft * 128:(ft + 1) * 128, :]),
             writes=[ku], dma=True, slot=ku)
        for k in range(5):
            S.op("dve", lambda b=b, ft=ft, k=k: nc.vector.tensor_scalar(
                out=dg[b][:, k, :], in0=C["ident_bf"], scalar1=C["convp"][:, ft, k:k + 1], scalar2=None, op0=ALU.mult),
                reads=["c_ident_bf", "c_convp"], writes=[kd])
        for sb in range(TK // 512):
            bk = P.bank()
            for k in range(5):
                S.op("pe", lambda b=b, k=k, sb=sb, bk=bk: nc.tensor.matmul(
                    P.PS[:, bk, :], dg[b][:, k, :], ub[b][:, sb * 512 + k:sb * 512 + k + 512], start=(k == 0), stop=(k == 4)),
                    reads=[ku, kd], writes=[("ps", bk)])
            cs = nst % 4
            nst += 1
            kc = ("c4_cst", cs)
            S.op("act", lambda cs=cs, bk=bk, ft=ft: nc.scalar.activation(
                out=cst[cs], in_=P.PS[:, bk, :], func=AF.Silu, bias=C["convp"][:, ft, 5:6]),
                reads=[("ps", bk), "c_convp"], writes=[kc])
            S.op("sp", lambda cs=cs, ft=ft, sb=sb: nc.sync.dma_start(out=XC[ft * 128:(ft + 1) * 128, sb * 512:(sb + 1) * 512], in_=cst[cs]),
                 reads=[kc], writes=[(jn, "XC", ft, sb)], dma=True, slot=kc)

    if getattr(P, "p4stop", 9) < 1:
        return
    P.phase_reset()
    vb = P.af32(D)
    S.op("sp", lambda: nc.sync.dma_start(out=vb, in_=P.vecs[0, 2 * D:3 * D].partition_broadcast(128)),
         writes=["s_ng"], dma=True, slot="s_ng")
    xsT = [P.abf(16 * 128).rearrange("p (f t) -> p f t", f=16) for _ in range(2)]
    BCT = [P.abf(8 * 128).rearrange("p (f t) -> p f t", f=8) for _ in range(2)]
    xtok = [P.abf(2048) for _ in range(2)]
    btok = [P.abf(512) for _ in range(2)]
    sml = [P.af32(64 * 10) for _ in range(2)]
    Wm = [P.abf(2 * 512).rearrange("p (d x) -> p d x", d=2) for _ in range(2)]
    xd = [P.abf(4 * 2048).rearrange("p (d x) -> p d x", d=4) for _ in range(2)]
    Ta = [P.af32(512) for _ in range(3)]
    Lb = [P.abf(512) for _ in range(3)]
    Mb2 = [[P.abf(512).rearrange("p (h l) -> p h l", h=4) for _ in range(16)] for _ in range(2)]
    yos = [P.abf(512) for _ in range(4)]
    hst = [P.af32(2048) for _ in range(2)]
    hbf = [P.abf(2048) for _ in range(2)]
    hbs = [P.abf(2048) for _ in range(2)]
    szb = [P.abf(2048) for _ in range(2)]
    yg = [P.af32(512) for _ in range(2)]
    sq = P.af32(512)
    nsm = [P.af32(4) for _ in range(2)]
    hob = [P.abf(512) for _ in range(2)]
    hts = [P.abf(512).rearrange("p (j q) -> p j q", j=4) for _ in range(2)]
    ctr = {"ta": 0, "yo": 0, "g": 0}

    def prep(c, passF):
        b = c % 2
        kx, kb, kt, kbt, ks = ("s_xsT", b), ("s_BCT", b), ("s_xtok", b), ("s_btok", b), ("s_sml", b)
        t0 = c * 128
        S.op("sp", lambda: nc.sync.dma_start(out=xsT[b], in_=XC[0:2048, t0:t0 + 128].rearrange("(f p) t -> p f t", p=128)),
             writes=[kx], dma=True, slot=kx)
        nb_ = 8 if passF else 4
        S.op("sp", lambda: nc.sync.dma_start(out=BCT[b][:, 0:nb_, :], in_=XC[2048:2048 + nb_ * 128, t0:t0 + 128].rearrange("(f p) t -> p f t", p=128)),
             writes=[kb], dma=True, slot=kb)
        v = sml[b]
        dtr, e_, dt, a, acum, ea, dst, dch, dtd = [v[:, i * 64:(i + 1) * 64] for i in range(9)]
        S.op("sp", lambda: nc.sync.dma_start(out=dtr, in_=sc["DTR"][t0:t0 + 128, :]), writes=[ks], dma=True, slot=ks)
        for g8 in range(2):
            bk = P.bank()
            psb = P.PS[:, bk, :].bitcast(BF16)
            for i in range(8):
                f = g8 * 8 + i
                S.op("pe", lambda f=f, i=i, psb=psb: nc.tensor.transpose(out=psb[:, i * 128:(i + 1) * 128], in_=xsT[b][:, f, :], identity=C["ident_bf"]),
                     reads=[kx, "c_ident_bf"], writes=[("ps", bk)])
            eng = "dve" if g8 == 0 else "act"
            if eng == "dve":
                S.op("dve", lambda g8=g8, psb=psb: nc.vector.tensor_copy(out=xtok[b][:, g8 * 1024:(g8 + 1) * 1024], in_=psb),
                     reads=[("ps", bk)], writes=[kt])
            else:
                S.op("act", lambda g8=g8, psb=psb: nc.scalar.copy(out=xtok[b][:, g8 * 1024:(g8 + 1) * 1024], in_=psb),
                     reads=[("ps", bk)], writes=[kt])
        bk = P.bank()
        psb = P.PS[:, bk, :].bitcast(BF16)
        for i in range(4):
            S.op("pe", lambda i=i, psb=psb: nc.tensor.transpose(out=psb[:, i * 128:(i + 1) * 128], in_=BCT[b][:, i, :], identity=C["ident_bf"]),
                 reads=[kb, "c_ident_bf"], writes=[("ps", bk)])
        S.op("dve", lambda psb=psb: nc.vector.tensor_copy(out=btok[b], in_=psb[:, 0:512]), reads=[("ps", bk)], writes=[kbt])
        S.op("dve", lambda: nc.vector.tensor_tensor(out=e_, in0=dtr, in1=C["dtb"], op=ALU.add), reads=[ks, "c_sm"], writes=[ks])
        S.op("act", lambda: nc.scalar.activation(out=e_, in_=e_, func=AF.Exp), reads=[ks], writes=[ks])
        S.op("act", lambda: nc.scalar.activation(out=dt, in_=e_, func=AF.Ln, bias=1.0), reads=[ks], writes=[ks])
        S.op("dve", lambda: nc.vector.tensor_tensor(out=a, in0=dt, in1=C["A"], op=ALU.mult), reads=[ks, "c_A"], writes=[ks])
        bk = P.bank()
        S.op("pe", lambda bk=bk: nc.tensor.matmul(P.PS[:, bk, 0:32], C["le"], a[:, 0:32], start=True, stop=True),
             reads=[ks, "c_le"], writes=[("ps", bk)])
        S.op("pe", lambda bk=bk: nc.tensor.matmul(P.PS[:, bk, 32:64], C["ge"], a[:, 32:64], start=True, stop=True),
             reads=[ks, "c_ge"], writes=[("ps", bk)])
        S.op("pe", lambda bk=bk: nc.tensor.matmul(P.PS[:, bk, 64:128], C["ones"], a, start=True, stop=True),
             reads=[ks, "c_ones"], writes=[("ps", bk)])
        S.op("dve", lambda bk=bk: nc.vector.tensor_copy(out=acum, in_=P.PS[:, bk, 0:64]), reads=[("ps", bk)], writes=[ks])
        S.op("act", lambda bk=bk: nc.scalar.activation(out=ea, in_=P.PS[:, bk, 0:64], func=AF.Exp), reads=[("ps", bk)], writes=[ks])
        S.op("act", lambda bk=bk: nc.scalar.activation(out=dch, in_=P.PS[:, bk, 64:128], func=AF.Exp), reads=[("ps", bk)], writes=[ks])
        S.op("dve", lambda bk=bk: nc.vector.tensor_tensor(out=dst, in0=P.PS[:, bk, 64:128], in1=acum, op=ALU.subtract),
             reads=[("ps", bk), ks], writes=[ks])
        S.op("act", lambda: nc.scalar.activation(out=dst, in_=dst, func=AF.Exp), reads=[ks], writes=[ks])
        S.op("dve", lambda: nc.vector.tensor_tensor(out=dtd, in0=dt, in1=dst, op=ALU.mult), reads=[ks], writes=[ks])
        return dict(b=b, kx=kx, kb=kb, kt=kt, kbt=kbt, ks=ks, dt=dt, a=a, ea=ea, dch=dch, dtd=dtd)

    def state_update(pp, d, xw_ap, kxw):
        b = pp["b"]
        for g in range(4):
            bk = P.bank()
            S.op("pe", lambda g=g, bk=bk: nc.tensor.matmul(
                P.PS[:, bk, :], btok[b][:, g * 128:(g + 1) * 128], xw_ap[:, g * 512:(g + 1) * 512], start=True, stop=True),
                reads=[pp["kbt"], kxw], writes=[("ps", bk)])
            hv = hst[d][:, g * 512:(g + 1) * 512]
            kh = ("s_h", d, g)
            dcol = pp["dch"][:, d * 32 + g * 8:d * 32 + (g + 1) * 8]
            S.op("dve", lambda hv=hv, dcol=dcol: nc.vector.tensor_tensor(
                out=hv.rearrange("p (e q) -> p e q", e=8), in0=hv.rearrange("p (e q) -> p e q", e=8), in1=_bc(dcol, 64), op=ALU.mult),
                reads=[kh, pp["ks"]], writes=[kh])
            S.op("dve", lambda hv=hv, bk=bk: nc.vector.tensor_tensor(out=hv, in0=hv, in1=P.PS[:, bk, :], op=ALU.add),
                 reads=[kh, ("ps", bk)], writes=[kh])
            S.op("act", lambda hv=hv, g=g, d=d: nc.scalar.copy(out=hbf[d][:, g * 512:(g + 1) * 512], in_=hv),
                 reads=[kh], writes=[("s_hbf", d, g)])

    for d in range(2):
        S.op("dve", lambda d=d: nc.vector.memset(hst[d], 0.0), writes=[("s_h", d, g) for g in range(4)])
        S.op("dve", lambda d=d: nc.vector.memset(hbf[d], 0.0), writes=[("s_hbf", d, g) for g in range(4)])
    def prepB(c):
        pp = prep(c, False)
        b = pp["b"]
        kxw = ("s_xd", b, 2)
        xw = xd[b][:, 2, :]
        S.op("pool", lambda: nc.gpsimd.tensor_tensor(
            out=xw.rearrange("p (h q) -> p h q", h=32), in0=xtok[b].rearrange("p (h q) -> p h q", h=32),
            in1=_bc(pp["dtd"][:, 32:64], 64), op=ALU.mult),
            reads=[pp["kt"], pp["ks"]], writes=[kxw])
        return (pp, xw, kxw)

    nxt = prepB(nck - 1) if nck > 1 else None
    for c in range(nck - 1, -1, -1):
        if c < nq:
            S.op("sp", lambda c=c: nc.sync.dma_start(out=HB[c], in_=hbf[1]),
                 reads=[("s_hbf", 1, g) for g in range(4)], writes=[(jn, "HB", c)], dma=True, slot="s_hbst")
        if c == 0:
            break
        cur = nxt
        nxt = prepB(c - 1) if c - 1 >= 1 else None
        state_update(cur[0], 1, cur[1], cur[2])

    if getattr(P, "p4stop", 9) < 2:
        return
    def chunkA(c):
        pp = prep(c, True)
        b = pp["b"]
        Mb = Mb2[b]
        t0 = c * 128
        ksz, khs = ("s_sz", b), ("s_hbs", b)
        S.op("sp", lambda b=b, t0=t0: nc.sync.dma_start(out=szb[b], in_=sc["SZ"][t0:t0 + 128, :]), writes=[ksz], dma=True, slot=ksz)
        S.op("sp", lambda b=b, c=c: nc.sync.dma_start(out=hbs[b], in_=HB[c]), reads=[(jn, "HB", c)], writes=[khs], dma=True, slot=khs)
        bk = P.bank()
        for g in range(4):
            S.op("pe", lambda g=g, bk=bk: nc.tensor.matmul(P.PS[:, bk, g * 128:(g + 1) * 128], BCT[b][:, g, :], BCT[b][:, 4 + g, :],
                                                          start=True, stop=True), reads=[pp["kb"]], writes=[("ps", bk)])
        kW = ("s_W", b)
        S.op("dve", lambda bk=bk: nc.vector.tensor_tensor(out=Wm[b][:, 0, :], in0=P.PS[:, bk, :], in1=C["le4fb"], op=ALU.mult),
             reads=[("ps", bk), "c_le4fb"], writes=[kW])
        S.op("dve", lambda bk=bk: nc.vector.tensor_tensor(out=Wm[b][:, 1, :], in0=P.PS[:, bk, :], in1=C["ge4fb"], op=ALU.mult),
             reads=[("ps", bk), "c_ge4fb"], writes=[kW])
        x3 = xtok[b].rearrange("p (h q) -> p h q", h=32)
        srcs = [pp["dt"][:, 0:32], pp["dt"][:, 32:64], pp["dtd"][:, 0:32], C["dskip"]]
        for i in range(4):
            eng = "pool" if i % 2 == 0 else "dve"
            eo = nc.gpsimd if eng == "pool" else nc.vector
            S.op(eng, lambda i=i, eo=eo: eo.tensor_tensor(out=xd[b][:, i, :].rearrange("p (h q) -> p h q", h=32), in0=x3,
                                                         in1=_bc(srcs[i], 64), op=ALU.mult),
                 reads=[pp["kt"], pp["ks"], "c_sm"], writes=[("s_xd", b, i)])
        if getattr(P, "p4stop", 9) < 3:
            return
        for d in range(2):
            for g in range(4):
                for hf in range(2):
                    u = d * 8 + g * 2 + hf
                    ti = ctr["ta"] % 3
                    ctr["ta"] += 1
                    h0 = d * 32 + g * 8 + hf * 4
                    S.op("pool", lambda ti=ti, d=d, h0=h0, pp=pp: nc.gpsimd.tensor_tensor(
                        out=Ta[ti].rearrange("p (h l) -> p h l", h=4), in0=C["le4f" if d == 0 else "ge4f"].rearrange("p (h l) -> p h l", h=4),
                        in1=_bc(pp["a"][:, h0:h0 + 4], 128), op=ALU.mult),
                        reads=[pp["ks"], "c_le4f", "c_ge4f"], writes=[("s_Ta", ti)])
                    bk2 = P.bank()
                    S.op("pe", lambda ti=ti, d=d, bk2=bk2: nc.tensor.matmul(P.PS[:, bk2, :], C["gt" if d == 0 else "lt"], Ta[ti], start=True, stop=True),
                         reads=[("s_Ta", ti), "c_gt", "c_lt"], writes=[("ps", bk2)])
                    S.op("act", lambda ti=ti, bk2=bk2: nc.scalar.activation(out=Lb[ti], in_=P.PS[:, bk2, :], func=AF.Exp),
                         reads=[("ps", bk2)], writes=[("s_L", ti)])
                    wv = Wm[b][:, d, g * 128:(g + 1) * 128]
                    S.op("dve", lambda u=u, ti=ti, wv=wv: nc.vector.tensor_tensor(
                        out=Mb[u], in0=Lb[ti].rearrange("p (h l) -> p h l", h=4),
                        in1=wv.rearrange("p (o l) -> p o l", o=1).to_broadcast([128, 4, 128]), op=ALU.mult),
                        reads=[("s_L", ti), kW], writes=[("s_M", b, u)])
        return dict(pp=pp, b=b, t0=t0, ksz=ksz, khs=khs, kW=kW, c=c)

    def chunkB(ctx):
        pp, b, t0, ksz, khs, kW, c = ctx["pp"], ctx["b"], ctx["t0"], ctx["ksz"], ctx["khs"], ctx["kW"], ctx["c"]
        Mb = Mb2[b]
        def groupF(g):
            yk = []
            for d in range(2):
                bk3 = P.bank()
                src = hbf[0] if d == 0 else hbs[b]
                ksrc = ("s_hbf", 0, g) if d == 0 else khs
                S.op("pe", lambda g=g, bk3=bk3, src=src: nc.tensor.matmul(P.PS[:, bk3, :], BCT[b][:, 4 + g, :], src[:, g * 512:(g + 1) * 512],
                                                                       start=True, stop=True), reads=[pp["kb"], ksrc], writes=[("ps", bk3)])
                yi = ctr["yo"] % 4
                ctr["yo"] += 1
                ecol = pp["ea"][:, d * 32 + g * 8:d * 32 + (g + 1) * 8]
                S.op("dve", lambda yi=yi, bk3=bk3, ecol=ecol: nc.vector.tensor_tensor(
                    out=yos[yi].rearrange("p (e q) -> p e q", e=8), in0=P.PS[:, bk3, :].rearrange("p (e q) -> p e q", e=8),
                    in1=_bc(ecol, 64), op=ALU.mult), reads=[("ps", bk3), pp["ks"]], writes=[("s_yos", yi)])
                yk.append(yi)
            if getattr(P, "p4stop", 9) < 5:
                return
            by = P.bank()
            rdY = [("s_yos", yk[0]), ("s_yos", yk[1]), ("s_xd", b, 3), "c_ident_bf"]
            S.op("pe", lambda by=by, yi=yk[0]: nc.tensor.matmul(P.PS[:, by, :], C["ident_bf"], yos[yi], start=True, stop=False),
                 reads=rdY, writes=[("ps", by)])
            S.op("pe", lambda by=by, yi=yk[1]: nc.tensor.matmul(P.PS[:, by, :], C["ident_bf"], yos[yi], start=False, stop=False),
                 reads=rdY, writes=[("ps", by)])
            S.op("pe", lambda by=by, g=g: nc.tensor.matmul(P.PS[:, by, :], C["ident_bf"], xd[b][:, 3, g * 512:(g + 1) * 512], start=False, stop=False),
                 reads=rdY, writes=[("ps", by)])
            for d in range(2):
                for hh in range(8):
                    u = d * 8 + g * 2 + hh // 4
                    h = g * 8 + hh
                    last = (d == 1 and hh == 7)
                    S.op("pe", lambda by=by, u=u, hh=hh, d=d, h=h, last=last: nc.tensor.matmul(
                        P.PS[:, by, hh * 64:(hh + 1) * 64], Mb[u][:, hh % 4, :], xd[b][:, d, h * 64:(h + 1) * 64], start=False, stop=last),
                        reads=[("s_M", b, u), ("s_xd", b, d)], writes=[("ps", by)])
            gi = ctr["g"] % 2
            ctr["g"] += 1
            kyg, kn = ("s_yg", gi), ("s_nsm", gi)
            S.op("dve", lambda gi=gi, by=by, g=g: nc.vector.tensor_tensor(out=yg[gi], in0=P.PS[:, by, :], in1=szb[b][:, g * 512:(g + 1) * 512], op=ALU.mult),
                 reads=[("ps", by), ksz], writes=[kyg])
            if c == 0 and g == 0:
                P.dbg("yg", yg[gi], kyg, 512)
                P.dbg("yos0", yos[yk[0]], ("s_yos", yk[0]), 512)
                P.dbg("yos1", yos[yk[1]], ("s_yos", yk[1]), 512)
                P.dbg("xD", xd[b][:, 3, 0:512], ("s_xd", b, 3), 512)
                P.dbg("xdf", xd[b][:, 0, 0:512], ("s_xd", b, 0), 512)
                P.dbg("M0", Mb[0].rearrange("p h l -> p (h l)"), ("s_M", b, 0), 512)
                P.dbg("M8", Mb[8].rearrange("p h l -> p (h l)"), ("s_M", b, 8), 512)
                P.dbg("Wm", Wm[b].rearrange("p d x -> p (d x)"), kW, 1024)
                P.dbg("sml", sml[b], pp["ks"], 640)
                P.dbg("sz", szb[b][:, 0:512], ksz, 512)
                P.dbg("xtok", xtok[b][:, 0:512], pp["kt"], 512)
                P.dbg("btok", btok[b][:, 0:512], pp["kbt"], 512)
                P.dbg("le4fb", C["le4fb"], "c_le4fb", 512)
                P.dbg("xsT", xsT[b][:, 0, :], pp["kx"], 128)
            S.op("dve", lambda gi=gi: nc.vector.tensor_tensor(out=sq, in0=yg[gi], in1=yg[gi], op=ALU.mult), reads=[kyg], writes=["s_sq"])
            S.op("dve", lambda gi=gi: nc.vector.reduce_sum(out=nsm[gi][:, 0:1], in_=sq, axis=AX.X), reads=["s_sq"], writes=[kn])
            S.op("dve", lambda gi=gi: nc.vector.tensor_scalar(out=nsm[gi][:, 1:2], in0=nsm[gi][:, 0:1], scalar1=1.0 / 512.0, scalar2=EPS,
                                                              op0=ALU.mult, op1=ALU.add), reads=[kn], writes=[kn])
            S.op("pool", lambda gi=gi: nc.gpsimd.tensor_tensor(out=nsm[gi][:, 1:2], in0=nsm[gi][:, 1:2], in1=C["mhalf"], op=ALU.pow),
                 reads=[kn, "c_mh"], writes=[kn])
            S.op("dve", lambda gi=gi, g=g: nc.vector.scalar_tensor_tensor(
                out=hob[gi], in0=yg[gi], scalar=nsm[gi][:, 1:2], in1=vb[:, g * 512:(g + 1) * 512], op0=ALU.mult, op1=ALU.mult),
                reads=[kyg, kn, "s_ng"], writes=[("s_hob", gi)])
            bt = P.bank()
            psb = P.PS[:, bt, :].bitcast(BF16)
            for j in range(4):
                S.op("pe", lambda gi=gi, j=j, psb=psb: nc.tensor.transpose(out=psb[:, j * 128:(j + 1) * 128], in_=hob[gi][:, j * 128:(j + 1) * 128],
                                                                       identity=C["ident_bf"]), reads=[("s_hob", gi), "c_ident_bf"], writes=[("ps", bt)])
            kh = ("s_hts", gi)
            S.op("act", lambda gi=gi, psb=psb: nc.scalar.copy(out=hts[gi], in_=psb[:, 0:512].rearrange("p (j q) -> p j q", j=4)),
                 reads=[("ps", bt)], writes=[kh])
            dsth = sc["HT"][2048 + g * 512:2048 + (g + 1) * 512, t0:t0 + 128].rearrange("(j p) q -> p j q", j=4)
            S.op("sp", lambda gi=gi, dsth=dsth: nc.sync.dma_start(out=dsth, in_=hts[gi]), reads=[kh], writes=[(jn, "HT", 2048 + g * 512, t0)],
                 dma=True, slot=kh)
        if getattr(P, "p4stop", 9) < 6:
            return
        for g in range(4):
            groupF(g)
        if c < nq - 1:
            state_update(pp, 0, xd[b][:, 2, :], ("s_xd", b, 2))

    ctx = chunkA(0)
    for c in range(nq):
        nctx = chunkA(c + 1) if c + 1 < nq else None
        chunkB(ctx)
        ctx = nctx


def _phase5(P, job):
    nc, S, C = P.nc, P.S, P.C
    jn, TK, TQ = job
    sc = P.sc[jn]
    y = P.jy[jn]
    P.phase_reset()
    vb = P.af32(2 * D)
    S.op("sp", lambda: nc.sync.dma_start(out=vb, in_=P.vecs[0, 3 * D:5 * D].partition_broadcast(128)),
         writes=["o_vb"], dma=True, slot="o_vb")
    lng, lnb = vb[:, 0:D], vb[:, D:2 * D]
    HTb2 = [P.abf(36 * 512).rearrange("p (k q) -> p k q", k=36) for _ in range(2)]
    wo = [P.abf(36 * 256) for _ in range(2)]
    rt = [P.af32(D) for _ in range(4)]
    st = [P.af32(24) for _ in range(2)]
    mv = [P.af32(2) for _ in range(2)]
    rs = [P.af32(2) for _ in range(2)]
    nw = [0]

    def load_ht(blk):
        q0 = blk * 512
        hb_ = blk % 2
        srch = sc["HT"][:, q0:q0 + 512].rearrange("(k p) q -> p k q", p=128)
        for k0 in range(0, 36, 9):
            S.op("sp", lambda k0=k0: nc.sync.dma_start(out=HTb2[hb_][:, k0:k0 + 9, :], in_=srch[:, k0:k0 + 9, :]),
                 writes=[("o_HTb", hb_)], dma=True, slot=("o_HTb", hb_, k0))

    def block(blk):
        q0 = blk * 512
        hb_ = blk % 2
        HTb = HTb2[hb_]
        kht = ("o_HTb", hb_)
        if blk + 1 < TQ // 512:
            load_ht(blk + 1)
        for tt in range(4):
            S.op("sp", lambda tt=tt: nc.sync.dma_start(out=rt[tt], in_=sc["XN"][q0 + tt * 128:q0 + (tt + 1) * 128, :]),
                 writes=[("o_r", tt)], dma=True, slot=("o_r", tt))
        for cc in range(8):
            ws = nw[0] % 2
            nw[0] += 1
            kw = ("o_w", ws)
            S.op("pool", lambda ws=ws, cc=cc: nc.gpsimd.dma_start(out=wo[ws], in_=P.WOB[cc]),
                 reads=[("WOB", cc)], writes=[kw], dma=True, slot=kw)
            for tt in range(4):
                bk = P.bank()
                for kc in range(36):
                    S.op("pe", lambda ws=ws, kc=kc, tt=tt, bk=bk: nc.tensor.matmul(
                        P.PS[:, bk, 0:256], HTb[:, kc, tt * 128:(tt + 1) * 128], wo[ws][:, kc * 256:(kc + 1) * 256],
                        start=(kc == 0), stop=(kc == 35)), reads=[kw, kht], writes=[("ps", bk)])
                rv = rt[tt][:, cc * 256:(cc + 1) * 256]
                S.op("dve", lambda rv=rv, bk=bk: nc.vector.scalar_tensor_tensor(
                    out=rv, in0=rv, scalar=ALPHA, in1=P.PS[:, bk, 0:256], op0=ALU.mult, op1=ALU.add),
                    reads=[("ps", bk), ("o_r", tt)], writes=[("o_r", tt)])
        for tt in range(4):
            b = tt % 2
            kr = ("o_r", tt)
            for q in range(4):
                S.op("dve", lambda b=b, q=q, tt=tt: nc.vector.bn_stats(out=st[b][:, q * 6:(q + 1) * 6], in_=rt[tt][:, q * 512:(q + 1) * 512]),
                     reads=[kr], writes=[("o_st", b)])
            S.op("dve", lambda b=b: nc.vector.bn_aggr(out=mv[b], in_=st[b]), reads=[("o_st", b)], writes=[("o_mv", b)])
            _ln_rstd(P, mv[b], rs[b][:, 0:1], rs[b][:, 1:2], [("o_mv", b)], "o_rs%d" % b)
            rk = ["o_rs%d_rstd" % b, "o_rs%d_nmr" % b]
            S.op("act", lambda b=b, tt=tt: nc.scalar.activation(out=rt[tt], in_=rt[tt], func=AF.Identity, scale=rs[b][:, 0:1], bias=rs[b][:, 1:2]),
                 reads=[kr] + rk, writes=[kr])
            S.op("pool", lambda tt=tt: nc.gpsimd.tensor_tensor(out=rt[tt], in0=rt[tt], in1=lng, op=ALU.mult), reads=[kr, "o_vb"], writes=[kr])
            S.op("dve", lambda tt=tt: nc.vector.tensor_tensor(out=rt[tt], in0=rt[tt], in1=lnb, op=ALU.add), reads=[kr, "o_vb"], writes=[kr])
            S.op("sp", lambda tt=tt: nc.sync.dma_start(out=y[q0 + tt * 128:q0 + (tt + 1) * 128, :], in_=rt[tt]),
                 reads=[kr], writes=[(jn, "y", q0, tt)], dma=True, slot=("o_yst", tt))

    load_ht(0)
    for blk in range(TQ // 512):
        block(blk)


def build_program(jobs, debug=()):
    P = Prog(jobs, debug=debug)
    P.sc = {}
    _p_setup(P)
    _p_consts(P)
    for j in jobs:
        _phase1(P, j)
        _phase2(P, j)
        _phase3(P, j)
        _phase4(P, j)
        _phase5(P, j)
    P.S.emit()
    return P


_CACHE = {}


def kernel(**inputs):
    xp = np.asarray(inputs["x_prompt"], np.float32)
    xs = np.asarray(inputs["x_sample"], np.float32)
    mp = np.asarray(inputs["mem_prompt"], np.float32)
    ms = np.asarray(inputs["mem_sample"], np.float32)
    B, T, _ = xp.shape
    SB, TS, _ = xs.shape
    ncores = 8
    assert 2 * B == ncores and SB == ncores
    TQ = T // 2
    jobs = [("p", T, TQ), ("s", TS, TS)]
    key = (T, TS)
    if key not in _CACHE:
        _CACHE[key] = build_program(jobs)
    P = _CACHE[key]
    shared = [prep_shared(inputs, False), prep_shared(inputs, True)]
    maps = []
    for c in range(ncores):
        flip = c % 2 == 1
        m = dict(shared[1 if flip else 0])
        a, b = xp[c // 2], xs[c]
        if flip:
            a, b = a[::-1], b[::-1]
        m["x_p"] = np.ascontiguousarray(a)
        m["x_s"] = np.ascontiguousarray(b)
        m["mem_p"] = np.ascontiguousarray(mp[c // 2])
        m["mem_s"] = np.ascontiguousarray(ms[c])
        maps.append(m)
    res = run_bass_kernel_spmd(P.nc, maps, core_ids=list(range(ncores)))
    yp = np.empty((B, T, D), np.float32)
    ys = np.empty((SB, TS, D), np.float32)
    for c in range(ncores):
        r = res.results[c]
        a = np.asarray(r["y_p"], np.float32)
        b = np.asarray(r["y_s"], np.float32)
        if c % 2 == 0:
            yp[c // 2, :TQ] = a
            ys[c] = b
        else:
            yp[c // 2, TQ:] = a[::-1]
            ys[c] = b[::-1]
    return (yp, ys)
```
